# Optimizing a Trainium2 kernel written in Bass

```python
import numpy as np
import jax, jax.numpy as jnp
from jax import lax


D_MODEL = 2048
BATCH = 2
SEQ = 16384
DEPTH = 1

MIX_WIDTH = D_MODEL
CONV_WIDTH = MIX_WIDTH // 2
CONV_GROUP = 128
CONV_KSIZE = 3
HEAD_DIM = 128
N_HEADS = (MIX_WIDTH - CONV_WIDTH) // HEAD_DIM
N_KV = 2
GROUP_R = N_HEADS // N_KV
ATTN_WIDTH = N_HEADS * HEAD_DIM
KV_WIDTH = N_KV * HEAD_DIM
CMP_BLOCK = 32
CMP_STRIDE = 16
CMP_HIDDEN = 2 * HEAD_DIM
SEL_BLOCK = 64
SEL_TOP_N = 16
WINDOW = 512
Q_BLOCK = 128
ROPE_THETA = 10000.0
D_FF = -(-8 * D_MODEL // (3 * 256)) * 256
IN_WIDTH = 3 * CONV_WIDTH + ATTN_WIDTH + 6 * KV_WIDTH + 3 * N_HEADS
EPS = 1e-6
NEG_INF = -1e30
FORCE_BONUS = 1e4

kernel_name = 'hymba_conv_nsa_adaln_block'


def rms_norm(x, g):
    xf = x.astype(jnp.float32)
    y = xf * lax.rsqrt(jnp.mean(xf * xf, axis=-1, keepdims=True) + EPS)
    return (y * g.astype(jnp.float32)).astype(x.dtype)


def modulate(h, shift, scale):
    return h * (1.0 + scale[:, None, :]) + shift[:, None, :]


def rope_tables(pos):
    inv = 1.0 / (ROPE_THETA ** (jnp.arange(0, HEAD_DIM, 2, dtype=jnp.float32) / HEAD_DIM))
    ang = pos.astype(jnp.float32)[..., None] * inv
    return jnp.cos(ang), jnp.sin(ang)


def apply_rope(x, cos, sin):
    cos = cos[:, :, None, :].astype(x.dtype)
    sin = sin[:, :, None, :].astype(x.dtype)
    x1, x2 = jnp.split(x, 2, axis=-1)
    return jnp.concatenate([x1 * cos - x2 * sin, x2 * cos + x1 * sin], axis=-1)


def masked_softmax(s, mask):
    return jax.nn.softmax(jnp.where(mask, s.astype(jnp.float32), NEG_INF), axis=-1)


def short_conv_mixer(b_gate, c_gate, h, conv_w):
    u = c_gate * h
    y = lax.conv_general_dilated(u, conv_w.astype(u.dtype)[:, None, :], window_strides=(1,),
                                 padding=[(CONV_KSIZE - 1, 0)],
                                 dimension_numbers=('NWC', 'WIO', 'NWC'),
                                 feature_group_count=CONV_WIDTH)
    return b_gate * y


def compress(kv, pe, w1, b1, w2):
    S = kv.shape[1]
    n_cmp = (S - CMP_BLOCK) // CMP_STRIDE + 1
    idx = np.arange(n_cmp)[:, None] * CMP_STRIDE + np.arange(CMP_BLOCK)[None, :]
    blocks = kv[:, idx] + pe[None, None, :, None, :]
    hid = jax.nn.gelu(jnp.einsum('bnlgd,ldh->bngh', blocks, w1) + b1)
    return jnp.einsum('bngh,he->bnge', hid, w2)


def nsa_mixer(q, k_cmp_raw, v_cmp_raw, k_slc, v_slc, k_win, v_win, gate_logits, positions,
              q_norm, k_norm, pe_k, k_w1, k_b1, k_w2, pe_v, v_w1, v_b1, v_w2):
    B, S = q.shape[0], q.shape[1]
    cos, sin = rope_tables(positions)
    q = apply_rope(rms_norm(q, q_norm), cos, sin)
    k_slc = apply_rope(rms_norm(k_slc, k_norm[1]), cos, sin)
    k_win = apply_rope(rms_norm(k_win, k_norm[2]), cos, sin)

    k_cmp = compress(k_cmp_raw, pe_k, k_w1, k_b1, k_w2)
    v_cmp = compress(v_cmp_raw, pe_v, v_w1, v_b1, v_w2)
    n_cmp = k_cmp.shape[1]
    cmp_start = np.arange(n_cmp) * CMP_STRIDE
    cmp_end = cmp_start + CMP_BLOCK - 1
    k_cmp = apply_rope(rms_norm(k_cmp, k_norm[0]), cos[:, cmp_end], sin[:, cmp_end])

    n_sel = S // SEL_BLOCK
    n_top = min(SEL_TOP_N, n_sel)
    sel_start = np.arange(n_sel) * SEL_BLOCK
    overlap = jnp.asarray(((cmp_start[:, None] < sel_start[None, :] + SEL_BLOCK)
                           & (cmp_end[:, None] >= sel_start[None, :])).astype(np.float32))
    cmp_end_j = jnp.asarray(cmp_end)

    ks_blocks = k_slc.reshape(B, n_sel, SEL_BLOCK, N_KV, HEAD_DIM).transpose(0, 3, 1, 2, 4)
    vs_blocks = v_slc.reshape(B, n_sel, SEL_BLOCK, N_KV, HEAD_DIM).transpose(0, 3, 1, 2, 4)
    pad = ((0, 0), (WINDOW, 0), (0, 0), (0, 0))
    kw_pad = jnp.pad(k_win, pad)
    vw_pad = jnp.pad(v_win, pad)
    qg = q.reshape(B, S, N_KV, GROUP_R, HEAD_DIM)
    gates = jax.nn.sigmoid(gate_logits.astype(jnp.float32)).astype(q.dtype)
    scale = HEAD_DIM ** -0.5
    gather_blocks = jax.vmap(jax.vmap(lambda blk, ix: blk[ix]))

    def block_fn(qi):
        q0 = qi * Q_BLOCK
        t = q0 + jnp.arange(Q_BLOCK)
        qb = lax.dynamic_slice_in_dim(qg, q0, Q_BLOCK, axis=1) * scale

        m_c = cmp_end_j[None, :] <= t[:, None]
        p_c = masked_softmax(jnp.einsum('bqgrd,bngd->bgrqn', qb, k_cmp), m_c)
        p_c = jnp.where(m_c, p_c, 0.0)
        o_cmp = jnp.einsum('bgrqn,bngd->bqgrd', p_c.astype(v_cmp.dtype), v_cmp)

        imp = jnp.einsum('bgrqn,nj->bgqj', p_c, overlap)
        j = jnp.arange(n_sel)[None, :]
        cur = (t // SEL_BLOCK)[:, None]
        valid = j * SEL_BLOCK <= t[:, None]
        forced = (j == 0) | (j == cur) | (j == cur - 1)
        score = jnp.where(valid, imp + FORCE_BONUS * forced, NEG_INF)
        _, idx = lax.top_k(score, n_top)
        ks = gather_blocks(ks_blocks, idx)
        vs = gather_blocks(vs_blocks, idx)
        tok = idx[..., None] * SEL_BLOCK + jnp.arange(SEL_BLOCK)
        m_s = (tok <= t[None, None, :, None, None]).reshape(B, N_KV, 1, Q_BLOCK, n_top * SEL_BLOCK)
        s_s = jnp.einsum('bqgrd,bgqnld->bgrqnl', qb, ks).reshape(B, N_KV, GROUP_R, Q_BLOCK, n_top * SEL_BLOCK)
        p_s = masked_softmax(s_s, m_s)
        o_slc = jnp.einsum('bgrqk,bgqkd->bqgrd', p_s.astype(vs.dtype),
                           vs.reshape(B, N_KV, Q_BLOCK, n_top * SEL_BLOCK, HEAD_DIM))

        kw = lax.dynamic_slice_in_dim(kw_pad, q0, WINDOW + Q_BLOCK, axis=1)
        vw = lax.dynamic_slice_in_dim(vw_pad, q0, WINDOW + Q_BLOCK, axis=1)
        kp = q0 - WINDOW + jnp.arange(WINDOW + Q_BLOCK)
        m_w = (kp[None, :] <= t[:, None]) & (kp[None, :] > t[:, None] - WINDOW) & (kp[None, :] >= 0)
        p_w = masked_softmax(jnp.einsum('bqgrd,bkgd->bgrqk', qb, kw), m_w)
        o_win = jnp.einsum('bgrqk,bkgd->bqgrd', p_w.astype(vw.dtype), vw)

        g = lax.dynamic_slice_in_dim(gates, q0, Q_BLOCK, axis=1).reshape(B, Q_BLOCK, N_KV, GROUP_R, 3)
        o = g[..., 0:1] * o_cmp + g[..., 1:2] * o_slc + g[..., 2:3] * o_win
        return o.reshape(B, Q_BLOCK, ATTN_WIDTH)

    out = lax.map(block_fn, jnp.arange(S // Q_BLOCK))
    return out.transpose(1, 0, 2, 3).reshape(B, S, ATTN_WIDTH)


def setup_inputs(seed: int = 0) -> dict:
    key = jax.random.key(seed)
    ks = jax.random.split(key, 24)
    f32 = jnp.float32
    L = DEPTH

    def nrm(k, shape, s):
        return jax.random.normal(k, shape, f32) * s

    def gain(k, shape):
        return 1.0 + 0.02 * jax.random.normal(k, shape, f32)

    return {
        'x': nrm(ks[0], (BATCH, SEQ, D_MODEL), 1.0),
        'c': nrm(ks[1], (BATCH, D_MODEL), 1.0),
        'positions': jnp.broadcast_to(jnp.arange(SEQ, dtype=jnp.int32), (BATCH, SEQ)),
        'ada_w': nrm(ks[2], (L, D_MODEL, 6 * D_MODEL), 0.5 * D_MODEL ** -0.5),
        'ada_b': nrm(ks[3], (L, 6 * D_MODEL), 0.01),
        'norm_mix': gain(ks[4], (L, D_MODEL)),
        'norm_ffn': gain(ks[5], (L, D_MODEL)),
        'w_in': nrm(ks[6], (L, D_MODEL, IN_WIDTH), D_MODEL ** -0.5),
        'conv_w': nrm(ks[7], (L, CONV_KSIZE, CONV_WIDTH), CONV_KSIZE ** -0.5),
        'cmp_pe_k': nrm(ks[8], (L, CMP_BLOCK, HEAD_DIM), 0.1),
        'cmp_k_w1': nrm(ks[9], (L, CMP_BLOCK, HEAD_DIM, CMP_HIDDEN), (CMP_BLOCK * HEAD_DIM) ** -0.5),
        'cmp_k_b1': nrm(ks[10], (L, CMP_HIDDEN), 0.01),
        'cmp_k_w2': nrm(ks[11], (L, CMP_HIDDEN, HEAD_DIM), CMP_HIDDEN ** -0.5),
        'cmp_pe_v': nrm(ks[12], (L, CMP_BLOCK, HEAD_DIM), 0.1),
        'cmp_v_w1': nrm(ks[13], (L, CMP_BLOCK, HEAD_DIM, CMP_HIDDEN), (CMP_BLOCK * HEAD_DIM) ** -0.5),
        'cmp_v_b1': nrm(ks[14], (L, CMP_HIDDEN), 0.01),
        'cmp_v_w2': nrm(ks[15], (L, CMP_HIDDEN, HEAD_DIM), CMP_HIDDEN ** -0.5),
        'q_norm': gain(ks[16], (L, HEAD_DIM)),
        'k_norm': gain(ks[17], (L, 3, HEAD_DIM)),
        'out_norm_conv': gain(ks[18], (L, CONV_WIDTH)),
        'out_norm_attn': gain(ks[19], (L, ATTN_WIDTH)),
        'w_out': nrm(ks[20], (L, MIX_WIDTH, D_MODEL), MIX_WIDTH ** -0.5),
        'ffn_w1': nrm(ks[21], (L, D_MODEL, D_FF), D_MODEL ** -0.5),
        'ffn_w3': nrm(ks[22], (L, D_MODEL, D_FF), D_MODEL ** -0.5),
        'ffn_w2': nrm(ks[23], (L, D_FF, D_MODEL), D_FF ** -0.5),
    }


def reference(x, c, positions, ada_w, ada_b, norm_mix, norm_ffn, w_in, conv_w,
              cmp_pe_k, cmp_k_w1, cmp_k_b1, cmp_k_w2, cmp_pe_v, cmp_v_w1, cmp_v_b1, cmp_v_w2,
              q_norm, k_norm, out_norm_conv, out_norm_attn, w_out, ffn_w1, ffn_w3, ffn_w2):
    B, S = x.shape[0], x.shape[1]
    splits = [int(v) for v in np.cumsum([CONV_WIDTH] * 3 + [ATTN_WIDTH] + [KV_WIDTH] * 6)]
    for l in range(DEPTH):
        sh_a, sc_a, g_a, sh_f, sc_f, g_f = jnp.split(jax.nn.silu(c) @ ada_w[l] + ada_b[l], 6, axis=-1)

        h = modulate(rms_norm(x, norm_mix[l]), sh_a, sc_a)
        proj = h @ w_in[l]
        cb, cc, ch, q, kc, vc, ksl, vsl, kw, vw, gl = jnp.split(proj, splits, axis=-1)

        y_conv = short_conv_mixer(cb, cc, ch, conv_w[l])
        y_conv = rms_norm(y_conv.reshape(B, S, CONV_WIDTH // CONV_GROUP, CONV_GROUP),
                          out_norm_conv[l].reshape(CONV_WIDTH // CONV_GROUP, CONV_GROUP)).reshape(B, S, CONV_WIDTH)

        y_attn = nsa_mixer(q.reshape(B, S, N_HEADS, HEAD_DIM),
                           kc.reshape(B, S, N_KV, HEAD_DIM), vc.reshape(B, S, N_KV, HEAD_DIM),
                           ksl.reshape(B, S, N_KV, HEAD_DIM), vsl.reshape(B, S, N_KV, HEAD_DIM),
                           kw.reshape(B, S, N_KV, HEAD_DIM), vw.reshape(B, S, N_KV, HEAD_DIM),
                           gl.reshape(B, S, N_HEADS, 3), positions, q_norm[l], k_norm[l],
                           cmp_pe_k[l], cmp_k_w1[l], cmp_k_b1[l], cmp_k_w2[l],
                           cmp_pe_v[l], cmp_v_w1[l], cmp_v_b1[l], cmp_v_w2[l])
        y_attn = rms_norm(y_attn.reshape(B, S, N_HEADS, HEAD_DIM),
                          out_norm_attn[l].reshape(N_HEADS, HEAD_DIM)).reshape(B, S, ATTN_WIDTH)

        x = x + g_a[:, None, :] * (jnp.concatenate([y_conv, y_attn], axis=-1) @ w_out[l])

        h = modulate(rms_norm(x, norm_ffn[l]), sh_f, sc_f)
        x = x + g_f[:, None, :] * ((jax.nn.silu(h @ ffn_w1[l]) * (h @ ffn_w3[l])) @ ffn_w2[l])
    return x
```

```python
import math
import numpy as np
import ml_dtypes
import concourse.bass as bass
import concourse.mybir as mybir
from concourse.bass_utils import run_bass_kernel_spmd

F32 = mybir.dt.float32
BF16 = mybir.dt.bfloat16
I32 = mybir.dt.int32
AF = mybir.ActivationFunctionType
ALU = mybir.AluOpType
NPBF = ml_dtypes.bfloat16

D = 2048
DFF = 5632
INW = 5656
EPS = 1e-6
SCALE = 128 ** -0.5
NEGM = 30000.0
ENGS = ("pe", "act", "dve", "pool", "sp")


class Buf:
    __slots__ = ("name", "w", "rs")

    def __init__(self, name):
        self.name = name
        self.w = None
        self.rs = {}


def bufs(name, n):
    return [Buf(f"{name}{i}") for i in range(n)]


class Prog:
    NDMA = {"sp": 24, "pool": 16}

    def __init__(self, nc):
        self.nc = nc
        self.ops = {e: [] for e in ENGS}
        self.cnt = {e: 0 for e in ENGS}
        self.seen = {e: {} for e in ENGS}
        self.dcount = {}
        self.drr = {q: 0 for q in self.NDMA}
        for q, n in self.NDMA.items():
            for i in range(n):
                self.dcount[(q, i)] = 0
        self.sb_off = 16640
        self.sb_stack = []
        self.ntens = 0
        self.hw = 0

    def sb(self, shape, dtype, name="t"):
        esz = {F32: 4, BF16: 2, I32: 4}[dtype]
        n = 1
        for s in shape[1:]:
            n *= s
        nbytes = (n * esz + 63) // 64 * 64
        off = self.sb_off
        self.sb_off += nbytes
        self.hw = max(self.hw, self.sb_off)
        assert self.sb_off <= 229300, f"SBUF overflow {self.sb_off} at {name}"
        self.ntens += 1
        t = self.nc.alloc_sbuf_tensor_at(f"{name}_{self.ntens}", list(shape), dtype, offset=off)
        return t.ap()

    def push(self):
        self.sb_stack.append(self.sb_off)

    def pop(self):
        self.sb_off = self.sb_stack.pop()

    def _deps(self, eng, reads, writes):
        need = {}
        for b in reads:
            if b.w is not None:
                k, v = b.w
                if need.get(k, 0) < v:
                    need[k] = v
        for b in writes:
            if b.w is not None:
                k, v = b.w
                if need.get(k, 0) < v:
                    need[k] = v
            for k, v in b.rs.items():
                if need.get(k, 0) < v:
                    need[k] = v
        waits = []
        seen = self.seen[eng]
        for k, v in need.items():
            if eng == "pe" and k == "pe":
                continue
            if seen.get(k, 0) >= v:
                continue
            seen[k] = v
            waits.append((k, v))
        return waits

    def _post(self, ev, reads, writes):
        k, v = ev
        for b in reads:
            if b.rs.get(k, 0) < v:
                b.rs[k] = v
        for b in writes:
            b.w = ev
            b.rs = {}

    def op(self, eng, fn, reads=(), writes=()):
        waits = self._deps(eng, reads, writes)
        self.cnt[eng] += 1
        ev = (eng, self.cnt[eng])
        self.ops[eng].append((waits, fn, ev, 1))
        self._post(ev, reads, writes)

    def dma(self, q, fn, reads=(), writes=()):
        waits = self._deps(q, reads, writes)
        i = self.drr[q]
        self.drr[q] = (i + 1) % self.NDMA[q]
        key = (q, i)
        cur = self.dcount[key]
        if cur > 0 and self.seen[q].get(key, 0) < cur:
            self.seen[q][key] = cur
            waits.append((key, cur))
        self.dcount[key] = cur + 16
        ev = (key, cur + 16)
        self.ops[q].append((waits, fn, ev, 16))
        self._post(ev, reads, writes)

    def barrier(self):
        for e in ENGS:
            waits = []
            for e2 in ENGS:
                if e2 == e:
                    continue
                v = self.cnt[e2]
                if v > 0 and self.seen[e].get(e2, 0) < v:
                    self.seen[e][e2] = v
                    waits.append((e2, v))
            for key, v in self.dcount.items():
                if v > 0 and self.seen[e].get(key, 0) < v:
                    self.seen[e][key] = v
                    waits.append((key, v))
            if waits:
                self.ops[e].append((waits, None, None, 0))

    def emit(self):
        import contextlib

        nc = self.nc
        sems = {}
        with contextlib.ExitStack() as st:
            for e in ENGS:
                sems[e] = st.enter_context(nc.semaphore(f"s_{e}"))
            for key in self.dcount:
                sems[key] = st.enter_context(nc.semaphore(f"d_{key[0]}{key[1]}"))
            self.barrier()
            block = st.enter_context(nc.Block())
            ops = self.ops

            def replay(name, e):
                for waits, fn, ev, inc in ops[name]:
                    for k, v in waits:
                        e.wait_ge(sems[k], v)
                    if fn is not None:
                        fn(e).then_inc(sems[ev[0]], inc)

            @block.tensor
            def _(e):
                replay("pe", e)

            @block.scalar
            def _(e):
                replay("act", e)

            @block.vector
            def _(e):
                replay("dve", e)

            @block.gpsimd
            def _(e):
                replay("pool", e)

            @block.sync
            def _(e):
                replay("sp", e)


def build(nc, S, dump_names=()):
    NCH = S // 512
    NSL = NCH // 4
    NSC = S // 2048
    NSB = S // 64
    P = Prog(nc)

    def din(name, shape, dt=F32):
        return nc.dram_tensor(name, list(shape), dt, kind="ExternalInput").ap()

    def dscr(name, shape, dt):
        return nc.dram_tensor(name, list(shape), dt, kind="Internal").ap()

    xf = din("xf", [S, D])
    xq = din("xq", [NSL, 1024, D])
    posf = din("posf", [S], I32)
    posq = din("posq", [NSL * 1024], I32)
    cvec = din("cvec", [16, 128])
    hv = din("hv", [128, NSL])
    ada_w = din("ada_w", [D, 6 * D])
    ada_b = din("ada_b", [6 * D])
    norm_mix = din("norm_mix", [D])
    norm_ffn = din("norm_ffn", [D])
    w_in = din("w_in", [D, INW])
    conv_w = din("conv_w", [3, 1024])
    pe_k = din("cmp_pe_k", [32, 128])
    k_w1 = din("cmp_k_w1", [32, 128, 256])
    k_b1 = din("cmp_k_b1", [256])
    k_w2 = din("cmp_k_w2", [256, 128])
    pe_v = din("cmp_pe_v", [32, 128])
    v_w1 = din("cmp_v_w1", [32, 128, 256])
    v_b1 = din("cmp_v_b1", [256])
    v_w2 = din("cmp_v_w2", [256, 128])
    q_norm = din("q_norm", [128])
    k_norm = din("k_norm", [3, 128])
    on_conv = din("out_norm_conv", [1024])
    on_attn = din("out_norm_attn", [1024])
    w_out = din("w_out", [D, D])
    ffn_w1 = din("ffn_w1", [D, DFF])
    ffn_w3 = din("ffn_w3", [D, DFF])
    ffn_w2 = din("ffn_w2", [DFF, D])
    c_identb = din("c_identb", [128, 128], BF16)
    c_identf = din("c_identf", [128, 2, 128])
    c_ones = din("c_ones", [128, 2, 128], BF16)
    c_rot = din("c_rot", [128, 128], BF16)
    c_invf = din("c_invf", [128, 1])
    c_ovl = din("c_ovl", [128, NSC, NSB + 1], BF16)
    c_E = din("c_E", [128, 64, 128], BF16)
    c_wm0 = din("c_wm0", [128, 8, 512], BF16)
    c_wmg = din("c_wmg", [128, 8, 512], BF16)
    c_slm = din("c_slm", [16, 128, 512], BF16)
    c_cm = din("c_cm", [128, 2, 512], BF16)
    c_selb = din("c_selb", [128, NSL, 4, NSB])
    out = nc.dram_tensor("out", [NSL * 512, D], F32, kind="ExternalOutput").ap()

    wi_bf = dscr("wi_bf", [D, INW], BF16)
    wo_bf = dscr("wo_bf", [D, D], BF16)
    w1_bf = dscr("w1_bf", [D, DFF], BF16)
    w3_bf = dscr("w3_bf", [D, DFF], BF16)
    w2_bf = dscr("w2_bf", [DFF, D], BF16)
    ck1_bf = dscr("ck1_bf", [32, 128, 256], BF16)
    cv1_bf = dscr("cv1_bf", [32, 128, 256], BF16)
    ksT_d = dscr("ksT_d", [2, 128, S], BF16)
    vs_d = dscr("vs_d", [S, 256], BF16)
    ada_d = dscr("ada_d", [6 * D], F32)
    gd = dscr("gd", [24, 512], F32)

    b_wi = bufs("wi", 6)
    b_wo, b_w1, b_w3, b_w2, b_ck1, b_cv1 = [Buf(n) for n in "wo w1 w3 w2 ck1 cv1".split()]
    b_ks = bufs("ksd", NCH)
    b_vs = bufs("vsd", NCH)
    b_adad = Buf("adad")
    b_gd = Buf("gd")
    b_out = Buf("out")

    PS2 = [nc.alloc_psum_tensor(f"ps2_{i}", [128, 1024], F32).ap() for i in range(4)]
    banks = [PS2[i // 2][:, (i % 2) * 512:(i % 2) * 512 + 512] for i in range(8)]

    def bank_bf(i):
        return PS2[i // 2].bitcast(BF16)[:, (i % 2) * 1024:(i % 2) * 1024 + 1024]
    bk = bufs("bank", 8)

    def act(out_, in_, func, R, W, **kw):
        P.op("act", lambda e: e.activation(out=out_, in_=in_, func=func, **kw), R, W)

    def mm(out_, lhsT, rhs, start, stop, R, W):
        P.op("pe", lambda e: e.matmul(out_, lhsT=lhsT, rhs=rhs, start=start, stop=stop), R, W)

    def tr(out_, in_, ident, R, W):
        P.op("pe", lambda e: e.transpose(out=out_, in_=in_, identity=ident), R, W)

    def tt(eng, out_, a, b, op, R, W):
        P.op(eng, lambda e: e.tensor_tensor(out=out_, in0=a, in1=b, op=op), R, W)

    def ts(eng, out_, a, s1, op0, R, W, s2=None, op1=None):
        if op1 is None:
            P.op(eng, lambda e: e.tensor_scalar(out=out_, in0=a, scalar1=s1, scalar2=None, op0=op0), R, W)
        else:
            P.op(eng, lambda e: e.tensor_scalar(out=out_, in0=a, scalar1=s1, scalar2=s2, op0=op0, op1=op1), R, W)

    def stt(out_, a, s, b, op0, op1, R, W):
        P.op("dve", lambda e: e.scalar_tensor_tensor(out=out_, in0=a, scalar=s, in1=b, op0=op0, op1=op1), R, W)

    def cp(eng, out_, in_, R, W):
        if eng == "act":
            P.op(eng, lambda e: e.activation(out=out_, in_=in_, func=AF.Copy), R, W)
        else:
            P.op(eng, lambda e: e.tensor_copy(out=out_, in_=in_), R, W)

    def rcp(out_, in_, R, W):
        P.op("dve", lambda e: e.reciprocal(out=out_, in_=in_), R, W)

    def mset(eng, ap, val, W):
        P.op(eng, lambda e: e.memset(ap, val), (), W)

    def ld(out_, in_, R, W, q="sp", slow=False, **kw):
        if slow:
            P.dma(q, lambda e: e.dma_start(out=out_, in_=in_, allow_slow_non_contiguous=True, **kw), R, W)
        else:
            P.dma(q, lambda e: e.dma_start(out=out_, in_=in_, **kw), R, W)

    dumps = {}
    dump_set = set(dump_names)

    def dump(name, ap, b):
        if name not in dump_set:
            return
        d = nc.dram_tensor("dbg_" + name, list(ap.shape), ap.dtype, kind="ExternalOutput").ap()
        dumps[name] = d
        ld(d, ap, [b], [Buf("dbg")], q="sp")


    identb = P.sb([128, 128], BF16, "identb")
    identf2 = P.sb([128, 2, 128], F32, "identf")
    ones2 = P.sb([128, 2, 128], BF16, "ones2")
    rot = P.sb([128, 128], BF16, "rot")
    invf = P.sb([128, 1], F32, "invf")
    ovl = P.sb([128, NSC, NSB + 1], BF16, "ovl")
    Emat = P.sb([128, 64, 128], BF16, "Emat")
    hvt = P.sb([128, NSL], F32, "hvt")
    b_const = Buf("const")
    for t_, d_ in ((identb, c_identb), (identf2, c_identf), (ones2, c_ones), (rot, c_rot), (invf, c_invf),
                   (ovl, c_ovl), (Emat, c_E), (hvt, hv)):
        ld(t_, d_, [], [b_const])
    onesb = ones2[:, 0, :]
    identf = identf2[:, 0, :]
    onesf = identf2[:, 1, :]

    colp = P.sb([128, 160], F32, "colp")
    b_colp = Buf("colp")
    nmT = colp[:, 0:16]
    nfT = colp[:, 16:32]
    qn = colp[:, 32:33]
    kn = colp[:, 33:36]
    cw = colp[:, 36:60].rearrange("p (g k) -> p g k", k=3)
    onc = colp[:, 60:68]
    ona = colp[:, 68:76]
    b1c = colp[:, 76:80]
    A1T = colp[:, 80:96]
    shaT = colp[:, 96:112]
    A2T = colp[:, 112:128]
    shfT = colp[:, 128:144]
    b1eff = colp[:, 144:148]
    ld(nmT, norm_mix.rearrange("(c p) -> p c", p=128), [], [b_colp], slow=True)
    ld(nfT, norm_ffn.rearrange("(c p) -> p c", p=128), [], [b_colp], slow=True)
    ld(qn, q_norm.rearrange("(p o) -> p o", o=1), [], [b_colp], slow=True)
    ld(kn, k_norm.rearrange("a d -> d a"), [], [b_colp], slow=True)
    for k_ in range(3):
        ld(colp[:, 36 + k_:60:3], conv_w[k_].rearrange("(g p) -> p g", p=128), [], [b_colp], slow=True)
    ld(onc, on_conv.rearrange("(g p) -> p g", p=128), [], [b_colp], slow=True)
    ld(ona, on_attn.rearrange("(g p) -> p g", p=128), [], [b_colp], slow=True)
    ld(b1c[:, 0:2], k_b1.rearrange("(c p) -> p c", p=128), [], [b_colp], slow=True)
    ld(b1c[:, 2:4], v_b1.rearrange("(c p) -> p c", p=128), [], [b_colp], slow=True)

    kcmpT = P.sb([128, 2, 128 * NSC], BF16, "kcmpT")
    vcmp = P.sb([128, NSC, 2, 128], BF16, "vcmp")
    b_kcmp = Buf("kcmp")
    b_vcmp = Buf("vcmp")
    w2c = P.sb([128, 2, 2, 128], BF16, "w2c")
    b_w2c = Buf("w2c")
    ld(w2c[:, 0, :, :], k_w2.rearrange("(c p) d -> p c d", p=128), [], [b_w2c], q="pool")
    ld(w2c[:, 1, :, :], v_w2.rearrange("(c p) d -> p c d", p=128), [], [b_w2c], q="pool")

    NTF, NTB = 6, 4
    TF = [P.sb([128, 512], F32, f"TF{i}") for i in range(NTF)]
    bTF = bufs("TF", NTF)
    TB = [P.sb([128, 512], BF16, f"TB{i}") for i in range(NTB)]
    bTB = bufs("TB", NTB)
    ssq = [P.sb([128, 4], F32, f"ssq{i}") for i in range(4)]
    bssq = bufs("ssq", 4)
    ssq_i = [0]
    junk = P.sb([128, 2048], BF16, "junk")
    b_junk = Buf("junk")

    conv_list = []

    def plan_conv(src, dst, rows, c0, c1, b, rstep=256):
        for r0 in range(0, rows, rstep):
            r1 = min(rows, r0 + rstep)
            conv_list.append((src[r0:r1, c0:c1], dst[r0:r1, c0:c1], b))

    plan_conv(w_in, wi_bf, D, 4096, 5120, b_wi[4])
    kw1f = k_w1.rearrange("l d h -> (l d) h")
    vw1f = v_w1.rearrange("l d h -> (l d) h")
    plan_conv(kw1f, ck1_bf.rearrange("l d h -> (l d) h"), 4096, 0, 256, b_ck1, 512)
    plan_conv(vw1f, cv1_bf.rearrange("l d h -> (l d) h"), 4096, 0, 256, b_cv1, 512)
    plan_conv(w_in, wi_bf, D, 5120, INW, b_wi[5])
    for gi in (1, 2, 0, 3):
        plan_conv(w_in, wi_bf, D, gi * 1024, gi * 1024 + 1024, b_wi[gi])
    for c0 in (0, 1024):
        plan_conv(w_out, wo_bf, D, c0, c0 + 1024, b_wo)
    for c0 in range(0, DFF, 1024):
        plan_conv(ffn_w1, w1_bf, D, c0, min(DFF, c0 + 1024), b_w1)
        plan_conv(ffn_w3, w3_bf, D, c0, min(DFF, c0 + 1024), b_w3)
    for c0 in (0, 1024):
        plan_conv(ffn_w2, w2_bf, DFF, c0, c0 + 1024, b_w2)
    conv_pos = [0]

    def do_conv(n):
        for _ in range(n):
            if conv_pos[0] >= len(conv_list):
                return
            s_, d_, b_ = conv_list[conv_pos[0]]
            conv_pos[0] += 1
            P.dma("pool", lambda e, s_=s_, d_=d_: e.dma_start(out=d_, in_=s_, max_dma_last_dim=4096), [], [Buf("cv")])

    do_conv(8 + 16)

    def rms_xn(x_ap, bx, xn_ap, bxn):
        i = ssq_i[0] % 4
        ssq_i[0] += 1
        s_, bs = ssq[i], bssq[i]
        P.op("dve", lambda e: e.scalar_tensor_tensor(out=junk, in0=x_ap, scalar=1.0, in1=x_ap, op0=ALU.mult, op1=ALU.mult,
                                                     accum_out=s_[:, 0:1]), [bx], [b_junk, bs])
        act(s_[:, 1:2], s_[:, 0:1], AF.Ln, [bs], [bs], scale=1.0 / D, bias=EPS)
        act(s_[:, 2:3], s_[:, 1:2], AF.Exp, [bs], [bs], scale=-0.5)
        act(xn_ap, x_ap, AF.Copy, [bx, bs], [bxn], scale=s_[:, 2:3])

    def transposes(xn4, bxn4, hT, bhT, AT, shT, col0=0):
        for fc in range(16):
            b_ = fc % 2
            Bv = bank_bf(b_)
            for ti in range(4):
                tr(Bv[:, ti * 128:(ti + 1) * 128], xn4[:, ti, fc * 128:(fc + 1) * 128], identb,
                   [bxn4[ti], b_const], [bk[b_]])
            if fc % 2 == 0:
                act(hT[:, fc, col0:col0 + 512], Bv[:, 0:512], AF.Identity, [bk[b_], b_colp], [bhT[fc]],
                    scale=AT[:, fc:fc + 1], bias=shT[:, fc:fc + 1])
            else:
                ts("dve", hT[:, fc, col0:col0 + 512], Bv[:, 0:512], AT[:, fc:fc + 1], ALU.mult, [bk[b_], b_colp], [bhT[fc]],
                   s2=shT[:, fc:fc + 1], op1=ALU.add)

    TWO_PI = 2.0 * math.pi
    C1 = 6.28125
    C2 = TWO_PI - C1
    MAGIC = 12582912.0
    PI_LO = 3.1415925

    def rope_tables(posi, bposi, cosT, sinT, bcs, n):
        A, K, R, RS = TF[0][:, 0:n], TF[1][:, 0:n], TF[2][:, 0:n], TF[3][:, 0:n]
        bA, bK, bR, bRS = bTF[0], bTF[1], bTF[2], bTF[3]
        cp("dve", A, posi, [bposi], [bA])
        ts("dve", A, A, invf[:, 0:1], ALU.mult, [bA, b_const], [bA])
        ts("dve", K, A, 1.0 / TWO_PI, ALU.mult, [bA], [bK], s2=MAGIC, op1=ALU.add)
        ts("dve", K, K, -MAGIC, ALU.add, [bK], [bK])
        stt(R, K, -C1, A, ALU.mult, ALU.add, [bK, bA], [bR])
        stt(R, K, -C2, R, ALU.mult, ALU.add, [bK, bR], [bR])
        ts("dve", RS, R, -PI_LO, ALU.max, [bR], [bRS], s2=PI_LO, op1=ALU.min)
        act(sinT, RS, AF.Sin, [bRS], [bcs])
        ts("dve", K, R, math.pi / 2, ALU.add, [bR], [bK])
        ts("dve", A, K, math.pi, ALU.is_gt, [bK], [bA])
        stt(K, A, -TWO_PI, K, ALU.mult, ALU.add, [bA, bK], [bK])
        ts("dve", RS, K, -PI_LO, ALU.max, [bK], [bRS], s2=PI_LO, op1=ALU.min)
        act(cosT, RS, AF.Sin, [bRS], [bcs])

    def norm_rope_a(src, bsrc, gaincol, n, bankA):
        sq, xnb = TB[0][:, 0:n], TB[1][:, 0:n]
        rstd = TF[4][:, 0:n]
        sc = TF[5][:, 0:n]
        cp("act", sc, src, [bsrc], [bTF[5]])
        tt("dve", sq, src, sc, ALU.mult, [bsrc, bTF[5]], [bTB[0]])
        mm(banks[bankA][:, 0:n], onesb, sq, True, True, [bTB[0], b_const], [bk[bankA]])
        act(rstd, banks[bankA][:, 0:n], AF.Ln, [bk[bankA]], [bTF[4]], scale=1.0 / 128, bias=EPS)
        act(rstd, rstd, AF.Exp, [bTF[4]], [bTF[4]], scale=-0.5)
        stt(xnb, src, gaincol, rstd, ALU.mult, ALU.mult, [bsrc, bTF[4], b_colp], [bTB[1]])

    def norm_rope_b(cosT, sinT, bcs, out_, bout, n, bankB):
        xnb = TB[1][:, 0:n]
        ta, tb_ = TF[5][:, 0:n], TF[3][:, 0:n]
        mm(banks[bankB][:, 0:n], rot, xnb, True, True, [bTB[1], b_const], [bk[bankB]])
        tt("dve", ta, xnb, cosT, ALU.mult, [bTB[1], bcs], [bTF[5]])
        tt("dve", tb_, banks[bankB][:, 0:n], sinT, ALU.mult, [bk[bankB], bcs], [bTF[3]])
        tt("dve", out_, ta, tb_, ALU.add, [bTF[5], bTF[3]], [bout])

    def norm_rope(src, bsrc, gaincol, cosT, sinT, bcs, out_, bout, n, bankA, bankB):
        norm_rope_a(src, bsrc, gaincol, n, bankA)
        norm_rope_b(cosT, sinT, bcs, out_, bout, n, bankB)

    def run_jobs(jobs):
        n = len(jobs)
        if n:
            jobs[0][0]()
        for i in range(n):
            jobs[i][1]()
            if i + 1 < n:
                jobs[i + 1][0]()
            jobs[i][2]()
            if jobs[i][3] is not None:
                jobs[i][3]()

    P.push()
    cT = P.sb([128, 16], F32, "cT")
    c16 = P.sb([16, 128], F32, "c16")
    b_c = Buf("c")
    ld(c16, cvec, [], [b_c])
    tr(banks[0][:, 0:16], c16, identf[0:16, 0:16], [b_c, b_const], [bk[0]])
    act(cT, banks[0][:, 0:16], AF.Silu, [bk[0]], [b_c])
    adab = [P.sb([128, 16, 512], F32, f"adab{i}") for i in range(2)]
    b_adab = bufs("adab", 2)
    arow = [P.sb([1, 512], F32, f"arow{i}") for i in range(2)]
    b_arow = bufs("arow", 2)
    abrow = [P.sb([1, 512], F32, f"abrow{i}") for i in range(2)]
    b_abrow = bufs("abrow", 2)
    for bi in range(24):
        i = bi % 2
        ld(adab[i], ada_w[:, bi * 512:(bi + 1) * 512].rearrange("(c p) n -> p c n", p=128), [], [b_adab[i]])
        ld(abrow[i], ada_b[bi * 512:(bi + 1) * 512].rearrange("(o n) -> o n", o=1), [], [b_abrow[i]])
        bb = 2 + i
        for kc in range(16):
            mm(banks[bb][0:1, :], cT[:, kc:kc + 1], adab[i][:, kc, :], kc == 0, kc == 15, [b_c, b_adab[i]], [bk[bb]])
        tt("dve", arow[i], banks[bb][0:1, :], abrow[i], ALU.add, [bk[bb], b_abrow[i]], [b_arow[i]])
        ld(ada_d[bi * 512:(bi + 1) * 512].rearrange("(o n) -> o n", o=1), arow[i], [b_arow[i]], [b_adad], q="pool")
    adaT = P.sb([128, 96], F32, "adaT")
    b_adaT = Buf("adaT")
    for i6 in range(6):
        ld(adaT[:, i6 * 16:(i6 + 1) * 16], ada_d[i6 * D:(i6 + 1) * D].rearrange("(c p) -> p c", p=128),
           [b_adad], [b_adaT], slow=True)
    ts("dve", A1T, adaT[:, 16:32], 1.0, ALU.add, [b_adaT], [b_colp])
    tt("dve", A1T, A1T, nmT, ALU.mult, [b_colp], [b_colp])
    cp("dve", shaT, adaT[:, 0:16], [b_adaT], [b_colp])
    ts("dve", A2T, adaT[:, 64:80], 1.0, ALU.add, [b_adaT], [b_colp])
    tt("dve", A2T, A2T, nfT, ALU.mult, [b_colp], [b_colp])
    cp("dve", shfT, adaT[:, 48:64], [b_adaT], [b_colp])
    P.pop()
    P.barrier()

    P.push()
    xst = [P.sb([128, D], F32, f"xst{i}") for i in range(2)]
    b_xst = bufs("xst", 2)
    xn4s = [P.sb([128, 4, D], BF16, f"xn4_{i}") for i in range(2)]
    b_xn4s = [bufs(f"xn4_{i}_", 4) for i in range(2)]
    hT = P.sb([128, 16, 512], BF16, "hT")
    b_hT = bufs("hT", 16)
    wkv = P.sb([128, 16, 1024], BF16, "wkv")
    b_wkv = Buf("wkv")
    cbuf = P.sb([128, 4, 16 + 2048], BF16, "cbuf")
    b_cbuf = bufs("cbuf", 4)
    posis = [P.sb([128, 512], I32, f"posi{i}") for i in range(2)]
    b_posis = bufs("posi", 2)
    cosTs = [P.sb([128, 512], F32, f"cosT{i}") for i in range(2)]
    sinTs = [P.sb([128, 512], F32, f"sinT{i}") for i in range(2)]
    b_css = bufs("cs", 2)
    ccmp = P.sb([128, 128], F32, "ccmp")
    scmp = P.sb([128, 128], F32, "scmp")
    b_ccs = Buf("ccs")
    kst = [P.sb([128, 512], BF16, f"kst{i}") for i in range(2)]
    b_kst = bufs("kst", 2)
    vst = [P.sb([128, 4, 256], BF16, f"vst{i}") for i in range(2)]
    b_vst = bufs("vst", 2)
    w1t = P.sb([128, 32, 256], BF16, "w1t")
    b_w1t = Buf("w1t")
    pet = P.sb([32, 2, 128], F32, "pet")
    peT = P.sb([128, 2, 32], BF16, "peT")
    b_pe = Buf("pe")
    hid = P.sb([128, 2, 128], BF16, "hid")
    b_hid = Buf("hid")

    for idx in range(4):
        mset("pool", cbuf[:, idx, 0:16], 0.0, [b_cbuf[idx]])
    P.barrier()
    ld(wkv, wi_bf[:, 4096:5120].rearrange("(c p) n -> p c n", p=128), [], [b_wkv])

    ld(pet[:, 0, :], pe_k, [], [b_pe])
    ld(pet[:, 1, :], pe_v, [], [b_pe])
    for kv in range(2):
        tr(banks[0][:, kv * 32:(kv + 1) * 32], pet[:, kv, :], identf[0:32, 0:32], [b_pe, b_const], [bk[0]])
    cp("dve", peT, banks[0][:, 0:64].rearrange("p (a l) -> p a l", a=2), [bk[0]], [b_pe])
    for kv in range(2):
        ld(w1t, (ck1_bf if kv == 0 else cv1_bf).rearrange("l d h -> d l h"), [], [b_w1t])
        for hc in range(2):
            for l in range(32):
                mm(banks[1][:, kv * 2 + hc:kv * 2 + hc + 1], w1t[:, l, hc * 128:(hc + 1) * 128], peT[:, kv, l:l + 1],
                   l == 0, l == 31, [b_w1t, b_pe], [bk[1]])
    tt("dve", b1eff, banks[1][:, 0:4], b1c, ALU.add, [bk[1], b_colp], [b_colp])
    P.barrier()

    def compress(Q):
        for kv in range(2):
            ld(w1t, (ck1_bf if kv == 0 else cv1_bf).rearrange("l d h -> d l h"), [], [b_w1t])
            for g in range(2):
                idx = kv * 2 + g
                B_ = 2 + g
                for hc in range(2):
                    for l in range(32):
                        mm(banks[B_][:, hc * 128:(hc + 1) * 128], w1t[:, l, hc * 128:(hc + 1) * 128],
                           cbuf[:, idx, l:l + 2033:16], l == 0, l == 31, [b_w1t, b_cbuf[idx]], [bk[B_]])
                xh, x2, inner, sg = TF[0][:, 0:256], TF[1][:, 0:256], TF[2][:, 0:256], TF[3][:, 0:256]
                for hc in range(2):
                    act(xh[:, hc * 128:(hc + 1) * 128], banks[B_][:, hc * 128:(hc + 1) * 128], AF.Identity,
                        [bk[B_], b_colp], [bTF[0]], bias=b1eff[:, kv * 2 + hc:kv * 2 + hc + 1])
                tt("dve", x2, xh, xh, ALU.mult, [bTF[0]], [bTF[1]])
                ts("dve", x2, x2, 0.044715, ALU.mult, [bTF[1]], [bTF[1]], s2=1.0, op1=ALU.add)
                tt("dve", inner, x2, xh, ALU.mult, [bTF[1], bTF[0]], [bTF[2]])
                act(sg, inner, AF.Sigmoid, [bTF[2]], [bTF[3]], scale=1.5957691216057308)
                tt("dve", hid.rearrange("p a n -> p (a n)"), sg, xh, ALU.mult, [bTF[3], bTF[0]], [b_hid])
                if kv == 0:
                    for hc in range(2):
                        mm(banks[4][:, 0:128], w2c[:, 0, hc, :], hid[:, hc, :], hc == 0, hc == 1, [b_w2c, b_hid], [bk[4]])
                    norm_rope(banks[4][:, 0:128], bk[4], kn[:, 0:1], ccmp, scmp, b_ccs,
                              kcmpT[:, g, Q * 128:(Q + 1) * 128], b_kcmp, 128, 5, 6)
                else:
                    for hc in range(2):
                        mm(banks[4][:, 0:128], hid[:, hc, :], w2c[:, 1, hc, :], hc == 0, hc == 1, [b_w2c, b_hid], [bk[4]])
                    cp("act", vcmp[:, Q, g, :], banks[4][:, 0:128], [bk[4]], [b_vcmp])

    def a_load(p, ti):
        i = ti % 2
        ld(xst[i], xf[p * 512 + ti * 128:p * 512 + (ti + 1) * 128, :], [], [b_xst[i]])

    def a_rms(p, ti):
        q_ = p % 2
        i = ti % 2
        rms_xn(xst[i], b_xst[i], xn4s[q_][:, ti, :], b_xn4s[q_][ti])

    def a_tables(p):
        q_ = p % 2
        ld(posis[q_], posf[p * 512:(p + 1) * 512].partition_broadcast(128), [], [b_posis[q_]])
        rope_tables(posis[q_], b_posis[q_], cosTs[q_], sinTs[q_], b_css[q_], 512)

    for ti in range(4):
        if ti < 2:
            a_load(0, ti)
    for ti in range(4):
        a_rms(0, ti)
        if ti + 2 < 4:
            a_load(0, ti + 2)
    a_tables(0)
    for p in range(NCH):
        do_conv(6)
        nxt = p + 1 < NCH
        if nxt:
            a_load(p + 1, 0)
            a_load(p + 1, 1)
        xn4, b_xn4 = xn4s[p % 2], b_xn4s[p % 2]
        cosT, sinT, b_cs = cosTs[p % 2], sinTs[p % 2], b_css[p % 2]
        pq = p % 4
        cp("pool", ccmp[:, 32 * pq:32 * pq + 32], cosT[:, 15:512:16], [b_cs], [b_ccs])
        cp("pool", scmp[:, 32 * pq:32 * pq + 32], sinT[:, 15:512:16], [b_cs], [b_ccs])
        transposes(xn4, b_xn4, hT, b_hT, A1T, shaT)

        def kcvc(idx):
            B_ = 2 + idx % 2
            for kc in range(16):
                mm(banks[B_], wkv[:, kc, idx * 128:(idx + 1) * 128], hT[:, kc, :], kc == 0, kc == 15,
                   [b_wkv, b_hT[kc]], [bk[B_]])
            cp("act", cbuf[:, idx, 16 + 512 * pq:16 + 512 * pq + 512], banks[B_], [bk[B_]], [b_cbuf[idx]])

        def kslproj(g):
            B_ = 4 + g
            for kc in range(16):
                mm(banks[B_], wkv[:, kc, 512 + g * 128:512 + (g + 1) * 128], hT[:, kc, :], kc == 0, kc == 15,
                   [b_wkv, b_hT[kc]], [bk[B_]])

        vi = p % 2

        def vsl(ti):
            B_ = 2 + ti % 2
            for kc in range(16):
                mm(banks[B_][:, 0:256], hT[:, kc, ti * 128:(ti + 1) * 128], wkv[:, kc, 768:1024], kc == 0, kc == 15,
                   [b_wkv, b_hT[kc]], [bk[B_]])
            cp("act", vst[vi][:, ti, :], banks[B_][:, 0:256], [bk[B_]], [b_vst[vi]])

        kslproj(0)
        kslproj(1)
        kcvc(0)
        kcvc(1)
        if nxt:
            a_rms(p + 1, 0)
            a_load(p + 1, 2)
        norm_rope_a(banks[4], bk[4], kn[:, 1:2], 512, 6)
        kcvc(2)
        if nxt:
            a_rms(p + 1, 1)
            a_load(p + 1, 3)
        kcvc(3)
        norm_rope_b(cosT, sinT, b_cs, kst[0], b_kst[0], 512, 7)
        ld(ksT_d[0, :, p * 512:(p + 1) * 512], kst[0], [b_kst[0]], [b_ks[p]])
        if nxt:
            a_rms(p + 1, 2)
        norm_rope_a(banks[5], bk[5], kn[:, 1:2], 512, 6)
        vsl(0)
        vsl(1)
        norm_rope_b(cosT, sinT, b_cs, kst[1], b_kst[1], 512, 7)
        ld(ksT_d[1, :, p * 512:(p + 1) * 512], kst[1], [b_kst[1]], [b_ks[p]])
        if nxt:
            a_rms(p + 1, 3)
            a_tables(p + 1)
        vsl(2)
        vsl(3)
        ld(vs_d[p * 512:(p + 1) * 512, :].rearrange("(t p) n -> p t n", p=128), vst[vi], [b_vst[vi]], [b_vs[p]])
        if pq == 3:
            compress(p // 4)
            for idx in range(4):
                cp("pool", cbuf[:, idx, 0:16], cbuf[:, idx, 2048:2064], [b_cbuf[idx]], [b_cbuf[idx]])
    mset("dve", vcmp[0:1, 0, :, :], 0.0, [b_vcmp])
    dump("kcmpT", kcmpT, b_kcmp)
    dump("vcmp", vcmp, b_vcmp)
    do_conv(len(conv_list))
    P.pop()
    P.barrier()

    xown = P.sb([128, 4, D], F32, "xown")
    b_xown = bufs("xown", 4)
    for s in range(NSL):
        P.push()
        qT = P.sb([128, 8, 512], BF16, "qT")
        b_qT = bufs("qT", 8)
        ycT = P.sb([128, 8, 512], BF16, "ycT")
        b_ycT = bufs("ycT", 8)
        kwT = P.sb([128, 2, 1024], BF16, "kwT")
        b_kwT = Buf("kwT")
        vwt = P.sb([128, 8, 256], BF16, "vwt")
        b_vwt = Buf("vwt")

        P.push()
        xst = P.sb([128, D], F32, "xst")
        b_xst1 = Buf("xst1")
        xn4 = P.sb([128, 4, D], BF16, "xn4")
        b_xn4 = bufs("xn4", 4)
        hT = P.sb([128, 16, 512], BF16, "hT")
        b_hT = bufs("hT", 16)
        hT2 = P.sb([128, 16, 2], BF16, "hT2")
        b_hT2 = Buf("hT2")
        wtail = P.sb([128, 16, 536], BF16, "wtail")
        b_wtail = Buf("wtail")
        NWB = 3
        wblk = [P.sb([128, 16, 256], BF16, f"wblk{i}") for i in range(NWB)]
        b_wblk = bufs("wblk", NWB)
        wb_i = [0]
        posi = P.sb([128, 1024], I32, "posi")
        b_posi = Buf("posi")
        cosq = P.sb([128, 1024], F32, "cosq")
        sinq = P.sb([128, 1024], F32, "sinq")
        b_cs = Buf("cs")
        gT = P.sb([24, 512], F32, "gT")
        b_gT = Buf("gT")
        uh = P.sb([128, 4], F32, "uh")
        b_uh = Buf("uh")
        ubuf = P.sb([128, 514], F32, "ubuf")
        b_ubuf = Buf("ubuf")

        ld(wtail, wi_bf[:, 5120:INW].rearrange("(c p) n -> p c n", p=128), [], [b_wtail])
        ld(posi, posq[s * 1024:(s + 1) * 1024].partition_broadcast(128), [], [b_posi])
        for hh in range(2):
            rope_tables(posi[:, hh * 512:(hh + 1) * 512], b_posi, cosq[:, hh * 512:(hh + 1) * 512],
                        sinq[:, hh * 512:(hh + 1) * 512], b_cs, 512)

        def load_wblk(c0):
            i = wb_i[0] % NWB
            wb_i[0] += 1
            ld(wblk[i], wi_bf[:, c0:c0 + 256].rearrange("(c p) n -> p c n", p=128), [], [b_wblk[i]])
            return wblk[i], b_wblk[i]

        for part in range(2):
            for ti in range(4):
                r0 = part * 512 + ti * 128
                if part == 0:
                    ld(xst, xq[s, r0:r0 + 128, :], [], [b_xst1])
                    rms_xn(xst, b_xst1, xn4[:, ti, :], b_xn4[ti])
                else:
                    ld(xown[:, ti, :], xq[s, r0:r0 + 128, :], [], [b_xown[ti]])
                    rms_xn(xown[:, ti, :], b_xown[ti], xn4[:, ti, :], b_xn4[ti])
            transposes(xn4, b_xn4, hT, b_hT, A1T, shaT)
            if part == 0:
                cp("pool", hT2, hT[:, :, 510:512], b_hT, [b_hT2])
            jobs = []
            for g in range(2):
                def proj(g=g):
                    B_ = 2 + g
                    for kc in range(16):
                        mm(banks[B_], wtail[:, kc, g * 128:(g + 1) * 128], hT[:, kc, :], kc == 0, kc == 15,
                           [b_wtail, b_hT[kc]], [bk[B_]])
                def nra(g=g):
                    norm_rope_a(banks[2 + g], bk[2 + g], kn[:, 2:3], 512, 6)
                def nrb(g=g, part=part):
                    norm_rope_b(cosq[:, part * 512:(part + 1) * 512], sinq[:, part * 512:(part + 1) * 512], b_cs,
                                kwT[:, g, part * 512:(part + 1) * 512], b_kwT, 512, 7)
                jobs.append((proj, nra, nrb, None))
            run_jobs(jobs)
            for ti in range(4):
                B_ = 4 + ti % 2
                for kc in range(16):
                    mm(banks[B_][:, 0:256], hT[:, kc, ti * 128:(ti + 1) * 128], wtail[:, kc, 256:512], kc == 0, kc == 15,
                       [b_wtail, b_hT[kc]], [bk[B_]])
                cp("act", vwt[:, part * 4 + ti, :], banks[B_][:, 0:256], [bk[B_]], [b_vwt])

        for kc in range(16):
            mm(banks[2][0:24, :], wtail[:, kc, 512:536], hT[:, kc, :], kc == 0, kc == 15, [b_wtail, b_hT[kc]], [bk[2]])
        act(gT, banks[2][0:24, :], AF.Sigmoid, [bk[2]], [b_gT])
        ld(gd, gT, [b_gT], [b_gd], q="pool")

        for cgp in range(4):
            wcc, bwcc = load_wblk(1024 + cgp * 256)
            wch, bwch = load_wblk(2048 + cgp * 256)
            wcb, bwcb = load_wblk(cgp * 256)
            for c2 in range(2):
                cg = cgp * 2 + c2
                cs_ = slice(c2 * 128, (c2 + 1) * 128)
                for kc in range(16):
                    mm(banks[2], wcc[:, kc, cs_], hT[:, kc, :], kc == 0, kc == 15, [bwcc, b_hT[kc]], [bk[2]])
                for kc in range(16):
                    mm(banks[5][:, 0:2], wcc[:, kc, cs_], hT2[:, kc, :], kc == 0, kc == 15, [bwcc, b_hT2], [bk[5]])
                for kc in range(16):
                    mm(banks[3], wch[:, kc, cs_], hT[:, kc, :], kc == 0, kc == 15, [bwch, b_hT[kc]], [bk[3]])
                for kc in range(16):
                    mm(banks[5][:, 2:4], wch[:, kc, cs_], hT2[:, kc, :], kc == 0, kc == 15, [bwch, b_hT2], [bk[5]])
                for kc in range(16):
                    mm(banks[4], wcb[:, kc, cs_], hT[:, kc, :], kc == 0, kc == 15, [bwcb, b_hT[kc]], [bk[4]])
                cp("act", uh, banks[5][:, 0:4], [bk[5]], [b_uh])
                tt("dve", ubuf[:, 0:2], uh[:, 0:2], uh[:, 2:4], ALU.mult, [b_uh], [b_ubuf])
                ts("dve", ubuf[:, 0:2], ubuf[:, 0:2], hvt[:, s:s + 1], ALU.mult, [b_ubuf, b_const], [b_ubuf])
                ccs = TF[0]
                cp("act", ccs, banks[2], [bk[2]], [bTF[0]])
                tt("dve", ubuf[:, 2:514], ccs, banks[3], ALU.mult, [bTF[0], bk[3]], [b_ubuf])
                y = TF[1]
                ts("dve", y, ubuf[:, 2:514], cw[:, cg, 2:3], ALU.mult, [b_ubuf, b_colp], [bTF[1]])
                stt(y, ubuf[:, 1:513], cw[:, cg, 1:2], y, ALU.mult, ALU.add, [b_ubuf, bTF[1], b_colp], [bTF[1]])
                stt(y, ubuf[:, 0:512], cw[:, cg, 0:1], y, ALU.mult, ALU.add, [b_ubuf, bTF[1], b_colp], [bTF[1]])
                tt("dve", y, y, banks[4], ALU.mult, [bTF[1], bk[4]], [bTF[1]])
                sq = TB[0]
                tt("dve", sq, y, y, ALU.mult, [bTF[1]], [bTB[0]])
                mm(banks[6], onesb, sq, True, True, [bTB[0], b_const], [bk[6]])
                rstd = TF[4]
                act(rstd, banks[6], AF.Ln, [bk[6]], [bTF[4]], scale=1.0 / 128, bias=EPS)
                act(rstd, rstd, AF.Exp, [bTF[4]], [bTF[4]], scale=-0.5)
                stt(ycT[:, cg, :], y, onc[:, cg:cg + 1], rstd, ALU.mult, ALU.mult, [bTF[1], bTF[4], b_colp], [b_ycT[cg]])
        jobs = []
        qw = {}
        for h in range(8):
            def proj(h=h):
                qb, c2 = h // 2, h % 2
                if c2 == 0:
                    qw[qb] = load_wblk(3072 + qb * 256)
                wq, bwq = qw[qb]
                B_ = 2 + c2
                for kc in range(16):
                    mm(banks[B_], wq[:, kc, c2 * 128:(c2 + 1) * 128], hT[:, kc, :], kc == 0, kc == 15,
                       [bwq, b_hT[kc]], [bk[B_]])
            def nra(h=h):
                norm_rope_a(banks[2 + h % 2], bk[2 + h % 2], qn, 512, 6)
            def nrb(h=h):
                norm_rope_b(cosq[:, 512:1024], sinq[:, 512:1024], b_cs, qT[:, h, :], b_qT[h], 512, 7)
            jobs.append((proj, nra, nrb, None))
        run_jobs(jobs)
        if s == 0:
            dump("qT", qT, b_qT[7])
            dump("ycT", ycT, b_ycT[7])
            dump("kwT", kwT, b_kwT)
            dump("vwt", vwt, b_vwt)
            dump("gT", gT, b_gT)
        P.pop()
        P.barrier()

        P.push()
        yaT = None
        wm = P.sb([128, 8, 512], BF16, "wm")
        b_wm = Buf("wm")
        cm = P.sb([128, 2, 512], BF16, "cm")
        b_cm = Buf("cm")
        selb = P.sb([128, 4, NSB], F32, "selb")
        b_selb = Buf("selb")
        imp = P.sb([128, 4, NSB], F32, "imp")
        b_imp = bufs("imp", 4)
        NJP = max(1, NSB // 128)
        selT = P.sb([128, NJP, 512], BF16, "selT")
        b_selT = Buf("selT")
        oacc = P.sb([128, 8, 512], F32, "oacc")
        b_oacc = bufs("oacc", 8)
        pstore = P.sb([128, 2, NSC, 1024], BF16, "pstore")
        b_pst = [[Buf(f"pst{r}_{c}") for c in range(NSC)] for r in range(2)]
        NPT = 3
        Pt2 = [P.sb([128, 1024], BF16, f"Pt2_{i}") for i in range(NPT)]
        b_Pt2 = bufs("Pt2", NPT)
        LaccP = PS2[3]
        b_LaccP = [bk[6], bk[7]]
        slmt = [P.sb([128, 512], BF16, f"slmt{i}") for i in range(2)]
        b_slmt = bufs("slmt", 2)
        gb = [P.sb([128, 512], F32, f"gb{i}") for i in range(2)]
        b_gb = bufs("gb", 2)
        kstl = [P.sb([128, 512], BF16, f"kstl{i}") for i in range(2)]
        b_kstl = bufs("kstl", 2)
        vstl = [P.sb([128, 4, 128], BF16, f"vstl{i}") for i in range(2)]
        b_vstl = bufs("vstl", 2)
        score = P.sb([128, NSB], F32, "score")
        sc2 = P.sb([128, NSB], F32, "sc2")
        m8 = P.sb([128, 16], F32, "m8")
        selm = P.sb([128, NSB], BF16, "selm")
        b_tk = Buf("topk")
        rl = P.sb([128, 4], F32, "rl")
        b_rl = Buf("rl")

        ld(wm, c_wm0 if s == 0 else c_wmg, [], [b_wm])
        ld(cm, c_cm, [], [b_cm])
        ld(selb, c_selb[:, s, :, :], [], [b_selb])
        gb_i = [0]
        pipe = {"v": 0, "pend": None, "pt": 0}

        def pv_flush():
            pd = pipe["pend"]
            pipe["pend"] = None
            if pd is None:
                return
            pt_ap, pt_buf, first, last, vt_ap, vt_bufs = pd
            for r_ in range(2):
                ob = 4 + r_
                mm(banks[ob], vt_ap, pt_ap[:, r_ * 512:(r_ + 1) * 512], first, last, vt_bufs + [pt_buf], [bk[ob]])

        def unit(qk, pt_ap, pt_buf, first, last, vt_ap, vt_bufs):
            k = pipe["v"] % 2
            pipe["v"] += 1
            sb_ = [bk[2 * k], bk[2 * k + 1]]
            for r_ in range(2):
                n = len(qk[r_])
                for i_, (l_, rh_, Rb) in enumerate(qk[r_]):
                    mm(PS2[k][:, r_ * 512:(r_ + 1) * 512], l_, rh_, i_ == 0, i_ == n - 1, Rb, sb_)
            act(pt_ap, PS2[k], AF.Exp, sb_, [pt_buf], scale=SCALE)
            if first:
                cp("dve", LaccP, pt_ap, [pt_buf], b_LaccP)
            else:
                tt("dve", LaccP, LaccP, pt_ap, ALU.add, [pt_buf] + b_LaccP, b_LaccP)
            pv_flush()
            pipe["pend"] = (pt_ap, pt_buf, first, last, vt_ap, vt_bufs)

        def next_pt():
            i_ = pipe["pt"] % NPT
            pipe["pt"] += 1
            return Pt2[i_], b_Pt2[i_]

        def finalize(h, x):
            r_ = h % 2
            ob = 4 + r_
            lb = r_
            wt, tmp, lsb = TF[0], TF[1], TF[2]
            gi = gb_i[0] % 2
            gb_i[0] += 1
            ld(gb[gi], gd[h * 3 + x, :].partition_broadcast(128), [b_gd], [b_gb[gi]])
            cp("act", lsb, LaccP[:, r_ * 512:(r_ + 1) * 512], b_LaccP, [bTF[2]])
            mm(banks[lb], onesf, lsb, True, True, [b_const, bTF[2]], [bk[lb]])
            ts("dve", wt, banks[lb], 1e-30, ALU.max, [bk[lb]], [bTF[0]])
            rcp(wt, wt, [bTF[0]], [bTF[0]])
            tt("dve", wt, wt, gb[gi], ALU.mult, [bTF[0], b_gb[gi]], [bTF[0]])
            if x == 0:
                tt("dve", oacc[:, h, :], banks[ob], wt, ALU.mult, [bk[ob], bTF[0]], [b_oacc[h]])
            else:
                tt("dve", tmp, banks[ob], wt, ALU.mult, [bk[ob], bTF[0]], [bTF[1]])
                tt("dve", oacc[:, h, :], oacc[:, h, :], tmp, ALU.add, [bTF[1], b_oacc[h]], [b_oacc[h]])

        for g in range(2):
            for hp in range(2):
                for c in range(s + 1):
                    qk = []
                    for r in range(2):
                        h = 4 * g + 2 * hp + r
                        lst = [(kcmpT[:, g, c * 128:(c + 1) * 128], qT[:, h, :], [b_kcmp, b_qT[h]])]
                        if c == 0:
                            lst.append((identb, cm[:, 1, :], [b_const, b_cm]))
                        if c == s:
                            lst.append((identb, cm[:, 0, :], [b_const, b_cm]))
                        qk.append(lst)
                    unit(qk, pstore[:, hp, c, :], b_pst[hp][c], c == 0, c == s, vcmp[:, c, g, :], [b_vcmp])
                pv_flush()
                for r in range(2):
                    h = 4 * g + 2 * hp + r
                    finalize(h, 0)
                    for tb in range(4):
                        for c in range(s + 1):
                            mm(banks[2][:, 0:NSB + 1], pstore[:, hp, c, r * 512 + tb * 128:r * 512 + (tb + 1) * 128], ovl[:, c, :],
                               c == 0, c == s, [b_pst[hp][c], b_const], [bk[2]])
                        ts("dve", rl[:, 0:1], banks[2][:, NSB:NSB + 1], 1e-30, ALU.max, [bk[2]], [b_rl])
                        rcp(rl[:, 1:2], rl[:, 0:1], [b_rl], [b_rl])
                        if hp == 0 and r == 0:
                            ts("dve", imp[:, tb, :], banks[2][:, 0:NSB], rl[:, 1:2], ALU.mult, [bk[2], b_rl], [b_imp[tb]])
                        else:
                            stt(imp[:, tb, :], banks[2][:, 0:NSB], rl[:, 1:2], imp[:, tb, :], ALU.mult, ALU.add,
                                [bk[2], b_rl, b_imp[tb]], [b_imp[tb]])
            for tb in range(4):
                tt("dve", score, imp[:, tb, :], selb[:, tb, :], ALU.add, [b_imp[tb], b_selb], [b_tk])
                P.op("dve", lambda e, m8=m8, score=score: e.max(out=m8[:, 0:8], in_=score), [b_tk], [b_tk])
                P.op("dve", lambda e, m8=m8, score=score, sc2=sc2: e.match_replace(
                    out=sc2, in_to_replace=m8[:, 0:8], in_values=score, imm_value=-3.0e38), [b_tk], [b_tk])
                P.op("dve", lambda e, m8=m8, sc2=sc2: e.max(out=m8[:, 8:16], in_=sc2), [b_tk], [b_tk])
                ts("dve", sc2, score, m8[:, 15:16], ALU.is_ge, [b_tk], [b_tk])
                ts("dve", selm, sc2, -1.0, ALU.add, [b_tk], [b_tk], s2=30000.0, op1=ALU.mult)
                Bv = bank_bf(3)
                w_ = min(128, NSB)
                for jp in range(NJP):
                    tr(Bv[0:w_, jp * 128:jp * 128 + 128], selm[:, jp * 128:jp * 128 + w_], identb, [b_tk, b_const], [bk[3]])
                for jp in range(NJP):
                    cp("act", selT[0:w_, jp, tb * 128:(tb + 1) * 128], Bv[0:w_, jp * 128:jp * 128 + 128], [bk[3]], [b_selT])
            if s == NSL - 1 and g == 0:
                dump("imp", imp, b_imp[3])
                dump("selT", selT, b_selT)
            NKT = 16 * s + 16
            for hp in range(2):
                for kt4 in range(NKT // 4):
                    li = kt4 % 2
                    ld(kstl[li], ksT_d[g, :, kt4 * 512:(kt4 + 1) * 512], [b_ks[kt4]], [b_kstl[li]])
                    ld(vstl[li], vs_d[kt4 * 512:(kt4 + 1) * 512, g * 128:(g + 1) * 128].rearrange("(t p) d -> p t d", p=128),
                       [b_vs[kt4]], [b_vstl[li]])
                    for k4 in range(4):
                        kt = kt4 * 4 + k4
                        j0 = 2 * kt
                        jp = j0 // 128
                        KE = min(128, NSB)
                        em = (Emat[0:KE, kt % 64, :], selT[0:KE, jp, :], [b_const, b_selT])
                        diag = kt >= 16 * s
                        if diag:
                            rr = kt - 16 * s
                            di = rr % 2
                            ld(slmt[di], c_slm[rr], [], [b_slmt[di]])
                        qk = []
                        for r in range(2):
                            h = 4 * g + 2 * hp + r
                            lst = [(kstl[li][:, k4 * 128:(k4 + 1) * 128], qT[:, h, :], [b_kstl[li], b_qT[h]]), em]
                            if diag:
                                lst.append((identb, slmt[di], [b_const, b_slmt[di]]))
                            qk.append(lst)
                        pt_ap, pt_buf = next_pt()
                        unit(qk, pt_ap, pt_buf, kt == 0, kt == NKT - 1, vstl[li][:, k4, :], [b_vstl[li]])
                pv_flush()
                for r in range(2):
                    finalize(4 * g + 2 * hp + r, 1)
            for hp in range(2):
                for kt in range(8):
                    qk = []
                    for r in range(2):
                        h = 4 * g + 2 * hp + r
                        qk.append([(kwT[:, g, kt * 128:(kt + 1) * 128], qT[:, h, :], [b_kwT, b_qT[h]]),
                                   (identb, wm[:, kt, :], [b_const, b_wm])])
                    pt_ap, pt_buf = next_pt()
                    unit(qk, pt_ap, pt_buf, kt == 0, kt == 7, vwt[:, kt, g * 128:(g + 1) * 128], [b_vwt])
                pv_flush()
                for r in range(2):
                    finalize(4 * g + 2 * hp + r, 2)
        if s == NSL - 1:
            dump("oacc", oacc, b_oacc[7])
        XB_ = 2
        yaT = qT
        b_yaT = b_qT
        for h in range(8):
            sq = TB[0]
            tt("dve", sq, oacc[:, h, :], oacc[:, h, :], ALU.mult, [b_oacc[h]], [bTB[0]])
            mm(banks[XB_], onesb, sq, True, True, [bTB[0], b_const], [bk[XB_]])
            rstd = TF[4]
            act(rstd, banks[XB_], AF.Ln, [bk[XB_]], [bTF[4]], scale=1.0 / 128, bias=EPS)
            act(rstd, rstd, AF.Exp, [bTF[4]], [bTF[4]], scale=-0.5)
            stt(yaT[:, h, :], oacc[:, h, :], ona[:, h:h + 1], rstd, ALU.mult, ALU.mult, [b_oacc[h], bTF[4], b_colp], [b_yaT[h]])
        P.pop()
        P.barrier()

        P.push()
        wob = [P.sb([128, 16, 512], BF16, f"wob{i}") for i in range(2)]
        b_wob = bufs("wob", 2)
        gab = [P.sb([128, 512], F32, f"gab{i}") for i in range(2)]
        b_gab = bufs("gab", 2)
        for oc in range(4):
            i = oc % 2
            ld(wob[i], wo_bf[:, oc * 512:(oc + 1) * 512].rearrange("(c p) n -> p c n", p=128), [], [b_wob[i]])
            ld(gab[i], ada_d[2 * D + oc * 512:2 * D + (oc + 1) * 512].partition_broadcast(128), [], [b_gab[i]])
            for tb in range(4):
                B_ = tb % 4
                for mc in range(16):
                    src_, bsrc = (ycT[:, mc, :], b_ycT[mc]) if mc < 8 else (yaT[:, mc - 8, :], b_yaT[mc - 8])
                    mm(banks[B_], src_[:, tb * 128:(tb + 1) * 128], wob[i][:, mc, :], mc == 0, mc == 15,
                       [bsrc, b_wob[i]], [bk[B_]])
                tmp = TF[tb % 2]
                tt("dve", tmp, banks[B_], gab[i], ALU.mult, [bk[B_], b_gab[i]], [bTF[tb % 2]])
                xs_ = xown[:, tb, oc * 512:(oc + 1) * 512]
                tt("dve", xs_, xs_, tmp, ALU.add, [bTF[tb % 2], b_xown[tb]], [b_xown[tb]])
        P.pop()
        P.barrier()
        P.pop()
        if s == 0:
            dump("x1", xown, b_xown[3])

        P.push()
        hT = P.sb([128, 16, 512], BF16, "hT")
        b_hT = bufs("hT", 16)
        actT = P.sb([128, 44, 512], BF16, "actT")
        b_actT = bufs("actT", 44)
        w13 = [P.sb([128, 2, 16, 256], BF16, f"w13_{i}") for i in range(2)]
        b_w13 = bufs("w13", 2)
        P.push()
        xn4 = P.sb([128, 4, D], BF16, "xn4")
        b_xn4 = bufs("xn4", 4)
        for ti in range(4):
            rms_xn(xown[:, ti, :], b_xown[ti], xn4[:, ti, :], b_xn4[ti])
        transposes(xn4, b_xn4, hT, b_hT, A2T, shfT)
        P.pop()
        P.barrier()
        w2b = [P.sb([128, 11, 512], BF16, f"w2b{i}") for i in range(2)]
        b_w2b = bufs("w2b", 2)
        gfb = [P.sb([128, 512], F32, f"gfb{i}") for i in range(2)]
        b_gfb = bufs("gfb", 2)
        for f2 in range(22):
            i = f2 % 2
            ld(w13[i][:, 0, :, :], w1_bf[:, f2 * 256:(f2 + 1) * 256].rearrange("(c p) n -> p c n", p=128), [], [b_w13[i]])
            ld(w13[i][:, 1, :, :], w3_bf[:, f2 * 256:(f2 + 1) * 256].rearrange("(c p) n -> p c n", p=128), [], [b_w13[i]])
            for c2 in range(2):
                fc = f2 * 2 + c2
                GB_, UB_ = (0, 1) if fc % 2 == 0 else (2, 3)
                for kc in range(16):
                    mm(banks[GB_], w13[i][:, 0, kc, c2 * 128:(c2 + 1) * 128], hT[:, kc, :], kc == 0, kc == 15,
                       [b_w13[i], b_hT[kc]], [bk[GB_]])
                for kc in range(16):
                    mm(banks[UB_], w13[i][:, 1, kc, c2 * 128:(c2 + 1) * 128], hT[:, kc, :], kc == 0, kc == 15,
                       [b_w13[i], b_hT[kc]], [bk[UB_]])
                sg = TF[fc % 2]
                act(sg, banks[GB_], AF.Silu, [bk[GB_]], [bTF[fc % 2]])
                tt("dve", actT[:, fc, :], sg, banks[UB_], ALU.mult, [bTF[fc % 2], bk[UB_]], [b_actT[fc]])
        w2i = [0]
        for oc in range(4):
            gi = oc % 2
            ld(gfb[gi], ada_d[5 * D + oc * 512:5 * D + (oc + 1) * 512].partition_broadcast(128), [], [b_gfb[gi]])
            for fg in range(4):
                i = w2i[0] % 2
                w2i[0] += 1
                ld(w2b[i], w2_bf[fg * 1408:(fg + 1) * 1408, oc * 512:(oc + 1) * 512].rearrange("(c p) n -> p c n", p=128),
                   [], [b_w2b[i]])
                for tb in range(4):
                    B_ = 4 + tb
                    for i11 in range(11):
                        fc = fg * 11 + i11
                        mm(banks[B_], actT[:, fc, tb * 128:(tb + 1) * 128], w2b[i][:, i11, :], fc == 0, fc == 43,
                           [b_actT[fc], b_w2b[i]], [bk[B_]])
            for tb in range(4):
                B_ = 4 + tb
                tmp = TF[2 + tb % 2]
                tt("dve", tmp, banks[B_], gfb[gi], ALU.mult, [bk[B_], b_gfb[gi]], [bTF[2 + tb % 2]])
                xs_ = xown[:, tb, oc * 512:(oc + 1) * 512]
                tt("dve", xs_, xs_, tmp, ALU.add, [bTF[2 + tb % 2], b_xown[tb]], [b_xown[tb]])
        for tb in range(4):
            ld(out[s * 512 + tb * 128:s * 512 + (tb + 1) * 128, :], xown[:, tb, :], [b_xown[tb]], [b_out], q="pool")
        P.pop()
        P.barrier()

    P.emit()
    return P, dumps


def host_consts(S, j):
    NSL = S // 2048
    NSC = S // 2048
    NSB = S // 64
    c = {}
    c["c_identb"] = np.eye(128, dtype=np.float32).astype(NPBF)
    c["c_identf"] = np.stack([np.eye(128, dtype=np.float32), np.ones((128, 128), np.float32)], 1)
    ones2 = np.ones((128, 2, 128), np.float32)
    ones2[0, 1, :] = 0.0
    c["c_ones"] = ones2.astype(NPBF)
    rot = np.zeros((128, 128), np.float32)
    for m in range(64):
        rot[m + 64, m] = -1.0
    for m in range(64, 128):
        rot[m - 64, m] = 1.0
    c["c_rot"] = rot.astype(NPBF)
    inv = (1.0 / (np.float32(10000.0) ** (np.arange(0, 128, 2, dtype=np.float32) / np.float32(128)))).astype(np.float32)
    c["c_invf"] = np.concatenate([inv, inv]).reshape(128, 1).astype(np.float32)
    ovl = np.zeros((128, NSC, NSB + 1), np.float32)
    for cc in range(NSC):
        for p in range(128):
            n = 128 * cc - 1 + p
            if n < 0:
                continue
            ovl[p, cc, NSB] = 1.0
            for jb in range(NSB):
                if 16 * n < 64 * jb + 64 and 16 * n + 31 >= 64 * jb:
                    ovl[p, cc, jb] = 1.0
    c["c_ovl"] = ovl.astype(NPBF)
    E = np.zeros((128, 64, 128), np.float32)
    for v in range(64):
        for key in range(128):
            E[2 * v + key // 64, v, key] = 1.0
    c["c_E"] = E.astype(NPBF)
    pp = np.arange(128)[:, None, None]
    kt = np.arange(8)[None, :, None]
    ii = np.arange(512)[None, None, :]
    kr = 128 * kt + pp
    tr_ = 512 + ii
    wm = ((kr <= tr_) & (kr > tr_ - 512)).astype(np.float32)
    wm0 = wm.copy()
    if j == 0:
        wm0[:, 0:4, :] = 0.0
    c["c_wmg"] = ((wm - 1.0) * NEGM).astype(NPBF)
    c["c_wm0"] = ((wm0 - 1.0) * NEGM).astype(NPBF)
    rr = np.arange(16)[:, None, None]
    p2 = np.arange(128)[None, :, None]
    slm = (128 * rr + p2 <= 512 * j + ii).astype(np.float32)
    c["c_slm"] = ((slm - 1.0) * NEGM).astype(NPBF)
    pcol = np.arange(128)[:, None]
    irow = np.arange(512)[None, :]
    cmv = (16 * pcol + 15 <= 512 * j + irow).astype(np.float32)
    cm0 = np.ones((128, 512), np.float32)
    cm0[0, :] = 0.0
    c["c_cm"] = ((np.stack([cmv, cm0], 1) - 1.0) * NEGM).astype(NPBF)
    selb = np.zeros((128, NSL, 4, NSB), np.float32)
    jb = np.arange(NSB)[None, :]
    for s in range(NSL):
        for tb in range(4):
            t = 512 * (4 * s + j) + 128 * tb + np.arange(128)[:, None]
            cur = t // 64
            valid = 64 * jb <= t
            forced = (jb == 0) | (jb == cur) | (jb == cur - 1)
            selb[:, s, tb, :] = np.where(valid, np.where(forced, 1e4, 0.0), -1e30)
    c["c_selb"] = selb
    hvv = np.ones((128, NSL), np.float32)
    if j == 0:
        hvv[:, 0] = 0.0
    c["hv"] = hvv
    return c


def make_in_maps(inputs, S, ncores=8):
    x = np.asarray(inputs["x"])
    cvec = np.asarray(inputs["c"])
    pos = np.asarray(inputs["positions"]).astype(np.int32)
    NSL = S // 2048
    wnames = ["ada_w", "ada_b", "norm_mix", "norm_ffn", "w_in", "conv_w", "cmp_pe_k", "cmp_k_w1", "cmp_k_b1",
              "cmp_k_w2", "cmp_pe_v", "cmp_v_w1", "cmp_v_b1", "cmp_v_w2", "q_norm", "k_norm", "out_norm_conv",
              "out_norm_attn", "w_out", "ffn_w1", "ffn_w3", "ffn_w2"]
    shared = {}
    for n in wnames:
        a = np.ascontiguousarray(np.asarray(inputs[n], dtype=np.float32)[0])
        shared[n] = a
    in_maps = []
    for core in range(ncores):
        b, j = core // 4, core % 4
        m = dict(shared)
        m["xf"] = np.ascontiguousarray(x[b, :S])
        xqv = np.zeros((NSL, 1024, D), np.float32)
        pq = np.zeros((NSL, 1024), np.int32)
        for s in range(NSL):
            p = 4 * s + j
            lo = 512 * p - 512
            if lo >= 0:
                xqv[s] = x[b, lo:lo + 1024]
                pq[s] = pos[b, lo:lo + 1024]
            else:
                xqv[s, 512:] = x[b, 0:512]
                pq[s, 512:] = pos[b, 0:512]
        m["xq"] = xqv
        m["posf"] = np.ascontiguousarray(pos[b, :S])
        m["posq"] = pq.reshape(-1)
        m["cvec"] = np.ascontiguousarray(cvec[b].reshape(16, 128))
        m.update(host_consts(S, j))
        in_maps.append(m)
    return in_maps


_CACHE = {}


def kernel(**inputs):
    S = 16384
    if S not in _CACHE:
        nc = bass.Bass("TRN2", target_bir_lowering=False)
        build(nc, S)
        _CACHE[S] = nc
    nc = _CACHE[S]
    in_maps = make_in_maps(inputs, S)
    res = run_bass_kernel_spmd(nc, in_maps, core_ids=list(range(8)))
    NSL = S // 2048
    outp = np.zeros((2, S, D), np.float32)
    for core in range(8):
        b, j = core // 4, core % 4
        o = res.results[core]["out"]
        for s in range(NSL):
            p = 4 * s + j
            outp[b, 512 * p:512 * (p + 1)] = o[512 * s:512 * (s + 1)]
    return outp
```

```python
import math
import numpy as np
import ml_dtypes
import concourse.bass as bass
import concourse.mybir as mybir
from concourse.bass_utils import run_bass_kernel_spmd

F32 = mybir.dt.float32
BF16 = mybir.dt.bfloat16
I32 = mybir.dt.int32
AF = mybir.ActivationFunctionType
ALU = mybir.AluOpType
NPBF = ml_dtypes.bfloat16

D = 2048
DFF = 5632
INW = 5656
EPS = 1e-6
SCALE = 128 ** -0.5
NEGM = 30000.0
ENGS = ("pe", "act", "dve", "pool", "sp")


class Buf:
    __slots__ = ("name", "w", "rs")

    def __init__(self, name):
        self.name = name
        self.w = None
        self.rs = {}


def bufs(name, n):
    return [Buf(f"{name}{i}") for i in range(n)]


class Prog:
    NDMA = {"sp": 24, "pool": 16}

    def __init__(self, nc):
        self.nc = nc
        self.ops = {e: [] for e in ENGS}
        self.cnt = {e: 0 for e in ENGS}
        self.seen = {e: {} for e in ENGS}
        self.dcount = {}
        self.drr = {q: 0 for q in self.NDMA}
        for q, n in self.NDMA.items():
            for i in range(n):
                self.dcount[(q, i)] = 0
        self.sb_off = 16640
        self.sb_stack = []
        self.ntens = 0
        self.hw = 0

    def sb(self, shape, dtype, name="t"):
        esz = {F32: 4, BF16: 2, I32: 4}[dtype]
        n = 1
        for s in shape[1:]:
            n *= s
        nbytes = (n * esz + 63) // 64 * 64
        off = self.sb_off
        self.sb_off += nbytes
        self.hw = max(self.hw, self.sb_off)
        assert self.sb_off <= 229300, f"SBUF overflow {self.sb_off} at {name}"
        self.ntens += 1
        t = self.nc.alloc_sbuf_tensor_at(f"{name}_{self.ntens}", list(shape), dtype, offset=off)
        return t.ap()

    def push(self):
        self.sb_stack.append(self.sb_off)

    def pop(self):
        self.sb_off = self.sb_stack.pop()

    def _deps(self, eng, reads, writes):
        need = {}
        for b in reads:
            if b.w is not None:
                k, v = b.w
                if need.get(k, 0) < v:
                    need[k] = v
        for b in writes:
            if b.w is not None:
                k, v = b.w
                if need.get(k, 0) < v:
                    need[k] = v
            for k, v in b.rs.items():
                if need.get(k, 0) < v:
                    need[k] = v
        waits = []
        seen = self.seen[eng]
        for k, v in need.items():
            if eng == "pe" and k == "pe":
                continue
            if seen.get(k, 0) >= v:
                continue
            seen[k] = v
            waits.append((k, v))
        return waits

    def _post(self, ev, reads, writes):
        k, v = ev
        for b in reads:
            if b.rs.get(k, 0) < v:
                b.rs[k] = v
        for b in writes:
            b.w = ev
            b.rs = {}

    def op(self, eng, fn, reads=(), writes=()):
        waits = self._deps(eng, reads, writes)
        self.cnt[eng] += 1
        ev = (eng, self.cnt[eng])
        self.ops[eng].append((waits, fn, ev, 1))
        self._post(ev, reads, writes)

    def dma(self, q, fn, reads=(), writes=()):
        waits = self._deps(q, reads, writes)
        i = self.drr[q]
        self.drr[q] = (i + 1) % self.NDMA[q]
        key = (q, i)
        cur = self.dcount[key]
        if cur > 0 and self.seen[q].get(key, 0) < cur:
            self.seen[q][key] = cur
            waits.append((key, cur))
        self.dcount[key] = cur + 16
        ev = (key, cur + 16)
        self.ops[q].append((waits, fn, ev, 16))
        self._post(ev, reads, writes)

    def barrier(self):
        for e in ENGS:
            waits = []
            for e2 in ENGS:
                if e2 == e:
                    continue
                v = self.cnt[e2]
                if v > 0 and self.seen[e].get(e2, 0) < v:
                    self.seen[e][e2] = v
                    waits.append((e2, v))
            for key, v in self.dcount.items():
                if v > 0 and self.seen[e].get(key, 0) < v:
                    self.seen[e][key] = v
                    waits.append((key, v))
            if waits:
                self.ops[e].append((waits, None, None, 0))

    def emit(self):
        import contextlib

        nc = self.nc
        sems = {}
        with contextlib.ExitStack() as st:
            for e in ENGS:
                sems[e] = st.enter_context(nc.semaphore(f"s_{e}"))
            for key in self.dcount:
                sems[key] = st.enter_context(nc.semaphore(f"d_{key[0]}{key[1]}"))
            self.barrier()
            block = st.enter_context(nc.Block())
            ops = self.ops

            def replay(name, e):
                for waits, fn, ev, inc in ops[name]:
                    for k, v in waits:
                        e.wait_ge(sems[k], v)
                    if fn is not None:
                        fn(e).then_inc(sems[ev[0]], inc)

            @block.tensor
            def _(e):
                replay("pe", e)

            @block.scalar
            def _(e):
                replay("act", e)

            @block.vector
            def _(e):
                replay("dve", e)

            @block.gpsimd
            def _(e):
                replay("pool", e)

            @block.sync
            def _(e):
                replay("sp", e)


def build(nc, S, dump_names=()):
    NCH = S // 512
    NSL = NCH // 4
    NSC = S // 2048
    NSB = S // 64
    P = Prog(nc)

    def din(name, shape, dt=F32):
        return nc.dram_tensor(name, list(shape), dt, kind="ExternalInput").ap()

    def dscr(name, shape, dt):
        return nc.dram_tensor(name, list(shape), dt, kind="Internal").ap()

    xf = din("xf", [S, D])
    xq = din("xq", [NSL, 1024, D])
    posf = din("posf", [S], I32)
    posq = din("posq", [NSL * 1024], I32)
    cvec = din("cvec", [16, 128])
    hv = din("hv", [128, NSL])
    ada_w = din("ada_w", [D, 6 * D])
    ada_b = din("ada_b", [6 * D])
    norm_mix = din("norm_mix", [D])
    norm_ffn = din("norm_ffn", [D])
    w_in = din("w_in", [D, INW])
    conv_w = din("conv_w", [3, 1024])
    pe_k = din("cmp_pe_k", [32, 128])
    k_w1 = din("cmp_k_w1", [32, 128, 256])
    k_b1 = din("cmp_k_b1", [256])
    k_w2 = din("cmp_k_w2", [256, 128])
    pe_v = din("cmp_pe_v", [32, 128])
    v_w1 = din("cmp_v_w1", [32, 128, 256])
    v_b1 = din("cmp_v_b1", [256])
    v_w2 = din("cmp_v_w2", [256, 128])
    q_norm = din("q_norm", [128])
    k_norm = din("k_norm", [3, 128])
    on_conv = din("out_norm_conv", [1024])
    on_attn = din("out_norm_attn", [1024])
    w_out = din("w_out", [D, D])
    ffn_w1 = din("ffn_w1", [D, DFF])
    ffn_w3 = din("ffn_w3", [D, DFF])
    ffn_w2 = din("ffn_w2", [DFF, D])
    c_identb = din("c_identb", [128, 128], BF16)
    c_identf = din("c_identf", [128, 2, 128])
    c_ones = din("c_ones", [128, 2, 128], BF16)
    c_rot = din("c_rot", [128, 128], BF16)
    c_invf = din("c_invf", [128, 1])
    c_ovl = din("c_ovl", [128, NSC, NSB + 1], BF16)
    c_E = din("c_E", [128, 64, 128], BF16)
    c_wm0 = din("c_wm0", [128, 8, 512], BF16)
    c_wmg = din("c_wmg", [128, 8, 512], BF16)
    c_slm = din("c_slm", [16, 128, 512], BF16)
    c_cm = din("c_cm", [128, 2, 512], BF16)
    c_selb = din("c_selb", [128, NSL, 4, NSB])
    out = nc.dram_tensor("out", [NSL * 512, D], F32, kind="ExternalOutput").ap()

    wi_bf = dscr("wi_bf", [D, INW], BF16)
    wo_bf = dscr("wo_bf", [D, D], BF16)
    w1_bf = dscr("w1_bf", [D, DFF], BF16)
    w3_bf = dscr("w3_bf", [D, DFF], BF16)
    w2_bf = dscr("w2_bf", [DFF, D], BF16)
    ck1_bf = dscr("ck1_bf", [32, 128, 256], BF16)
    cv1_bf = dscr("cv1_bf", [32, 128, 256], BF16)
    ksT_d = dscr("ksT_d", [2, 128, S], BF16)
    vs_d = dscr("vs_d", [S, 256], BF16)
    ada_d = dscr("ada_d", [6 * D], F32)
    gd = dscr("gd", [24, 512], F32)

    b_wi = bufs("wi", 6)
    b_wo, b_w1, b_w3, b_w2, b_ck1, b_cv1 = [Buf(n) for n in "wo w1 w3 w2 ck1 cv1".split()]
    b_ks = bufs("ksd", NCH)
    b_vs = bufs("vsd", NCH)
    b_adad = Buf("adad")
    b_gd = Buf("gd")
    b_out = Buf("out")

    PS2 = [nc.alloc_psum_tensor(f"ps2_{i}", [128, 1024], F32).ap() for i in range(4)]
    banks = [PS2[i // 2][:, (i % 2) * 512:(i % 2) * 512 + 512] for i in range(8)]

    def bank_bf(i):
        return PS2[i // 2].bitcast(BF16)[:, (i % 2) * 1024:(i % 2) * 1024 + 1024]
    bk = bufs("bank", 8)

    def act(out_, in_, func, R, W, **kw):
        P.op("act", lambda e: e.activation(out=out_, in_=in_, func=func, **kw), R, W)

    def mm(out_, lhsT, rhs, start, stop, R, W):
        P.op("pe", lambda e: e.matmul(out_, lhsT=lhsT, rhs=rhs, start=start, stop=stop), R, W)

    def tr(out_, in_, ident, R, W):
        P.op("pe", lambda e: e.transpose(out=out_, in_=in_, identity=ident), R, W)

    def tt(eng, out_, a, b, op, R, W):
        P.op(eng, lambda e: e.tensor_tensor(out=out_, in0=a, in1=b, op=op), R, W)

    def ts(eng, out_, a, s1, op0, R, W, s2=None, op1=None):
        if op1 is None:
            P.op(eng, lambda e: e.tensor_scalar(out=out_, in0=a, scalar1=s1, scalar2=None, op0=op0), R, W)
        else:
            P.op(eng, lambda e: e.tensor_scalar(out=out_, in0=a, scalar1=s1, scalar2=s2, op0=op0, op1=op1), R, W)

    def stt(out_, a, s, b, op0, op1, R, W):
        P.op("dve", lambda e: e.scalar_tensor_tensor(out=out_, in0=a, scalar=s, in1=b, op0=op0, op1=op1), R, W)

    def cp(eng, out_, in_, R, W):
        if eng == "act":
            P.op(eng, lambda e: e.activation(out=out_, in_=in_, func=AF.Copy), R, W)
        else:
            P.op(eng, lambda e: e.tensor_copy(out=out_, in_=in_), R, W)

    def rcp(out_, in_, R, W):
        P.op("dve", lambda e: e.reciprocal(out=out_, in_=in_), R, W)

    def mset(eng, ap, val, W):
        P.op(eng, lambda e: e.memset(ap, val), (), W)

    def ld(out_, in_, R, W, q="sp", slow=False, **kw):
        if slow:
            P.dma(q, lambda e: e.dma_start(out=out_, in_=in_, allow_slow_non_contiguous=True, **kw), R, W)
        else:
            P.dma(q, lambda e: e.dma_start(out=out_, in_=in_, **kw), R, W)

    dumps = {}
    dump_set = set(dump_names)

    def dump(name, ap, b):
        if name not in dump_set:
            return
        d = nc.dram_tensor("dbg_" + name, list(ap.shape), ap.dtype, kind="ExternalOutput").ap()
        dumps[name] = d
        ld(d, ap, [b], [Buf("dbg")], q="sp")


    identb = P.sb([128, 128], BF16, "identb")
    identf2 = P.sb([128, 2, 128], F32, "identf")
    ones2 = P.sb([128, 2, 128], BF16, "ones2")
    rot = P.sb([128, 128], BF16, "rot")
    invf = P.sb([128, 1], F32, "invf")
    ovl = P.sb([128, NSC, NSB + 1], BF16, "ovl")
    Emat = P.sb([128, 64, 128], BF16, "Emat")
    hvt = P.sb([128, NSL], F32, "hvt")
    b_const = Buf("const")
    for t_, d_ in ((identb, c_identb), (identf2, c_identf), (ones2, c_ones), (rot, c_rot), (invf, c_invf),
                   (ovl, c_ovl), (Emat, c_E), (hvt, hv)):
        ld(t_, d_, [], [b_const])
    onesb = ones2[:, 0, :]
    identf = identf2[:, 0, :]
    onesf = identf2[:, 1, :]

    colp = P.sb([128, 160], F32, "colp")
    b_colp = Buf("colp")
    nmT = colp[:, 0:16]
    nfT = colp[:, 16:32]
    qn = colp[:, 32:33]
    kn = colp[:, 33:36]
    cw = colp[:, 36:60].rearrange("p (g k) -> p g k", k=3)
    onc = colp[:, 60:68]
    ona = colp[:, 68:76]
    b1c = colp[:, 76:80]
    A1T = colp[:, 80:96]
    shaT = colp[:, 96:112]
    A2T = colp[:, 112:128]
    shfT = colp[:, 128:144]
    b1eff = colp[:, 144:148]
    ld(nmT, norm_mix.rearrange("(c p) -> p c", p=128), [], [b_colp], slow=True)
    ld(nfT, norm_ffn.rearrange("(c p) -> p c", p=128), [], [b_colp], slow=True)
    ld(qn, q_norm.rearrange("(p o) -> p o", o=1), [], [b_colp], slow=True)
    ld(kn, k_norm.rearrange("a d -> d a"), [], [b_colp], slow=True)
    for k_ in range(3):
        ld(colp[:, 36 + k_:60:3], conv_w[k_].rearrange("(g p) -> p g", p=128), [], [b_colp], slow=True)
    ld(onc, on_conv.rearrange("(g p) -> p g", p=128), [], [b_colp], slow=True)
    ld(ona, on_attn.rearrange("(g p) -> p g", p=128), [], [b_colp], slow=True)
    ld(b1c[:, 0:2], k_b1.rearrange("(c p) -> p c", p=128), [], [b_colp], slow=True)
    ld(b1c[:, 2:4], v_b1.rearrange("(c p) -> p c", p=128), [], [b_colp], slow=True)

    kcmpT = P.sb([128, 2, 128 * NSC], BF16, "kcmpT")
    vcmp = P.sb([128, NSC, 2, 128], BF16, "vcmp")
    b_kcmp = Buf("kcmp")
    b_vcmp = Buf("vcmp")
    w2c = P.sb([128, 2, 2, 128], BF16, "w2c")
    b_w2c = Buf("w2c")
    ld(w2c[:, 0, :, :], k_w2.rearrange("(c p) d -> p c d", p=128), [], [b_w2c], q="pool")
    ld(w2c[:, 1, :, :], v_w2.rearrange("(c p) d -> p c d", p=128), [], [b_w2c], q="pool")

    NTF, NTB = 6, 4
    TF = [P.sb([128, 512], F32, f"TF{i}") for i in range(NTF)]
    bTF = bufs("TF", NTF)
    TB = [P.sb([128, 512], BF16, f"TB{i}") for i in range(NTB)]
    bTB = bufs("TB", NTB)
    ssq = [P.sb([128, 4], F32, f"ssq{i}") for i in range(4)]
    bssq = bufs("ssq", 4)
    ssq_i = [0]
    junk = P.sb([128, 2048], BF16, "junk")
    b_junk = Buf("junk")

    conv_list = []

    def plan_conv(src, dst, rows, c0, c1, b, rstep=256):
        for r0 in range(0, rows, rstep):
            r1 = min(rows, r0 + rstep)
            conv_list.append((src[r0:r1, c0:c1], dst[r0:r1, c0:c1], b))

    plan_conv(w_in, wi_bf, D, 4096, 5120, b_wi[4])
    kw1f = k_w1.rearrange("l d h -> (l d) h")
    vw1f = v_w1.rearrange("l d h -> (l d) h")
    plan_conv(kw1f, ck1_bf.rearrange("l d h -> (l d) h"), 4096, 0, 256, b_ck1, 512)
    plan_conv(vw1f, cv1_bf.rearrange("l d h -> (l d) h"), 4096, 0, 256, b_cv1, 512)
    plan_conv(w_in, wi_bf, D, 5120, INW, b_wi[5])
    for gi in (1, 2, 0, 3):
        plan_conv(w_in, wi_bf, D, gi * 1024, gi * 1024 + 1024, b_wi[gi])
    for c0 in (0, 1024):
        plan_conv(w_out, wo_bf, D, c0, c0 + 1024, b_wo)
    for c0 in range(0, DFF, 1024):
        plan_conv(ffn_w1, w1_bf, D, c0, min(DFF, c0 + 1024), b_w1)
        plan_conv(ffn_w3, w3_bf, D, c0, min(DFF, c0 + 1024), b_w3)
    for c0 in (0, 1024):
        plan_conv(ffn_w2, w2_bf, DFF, c0, c0 + 1024, b_w2)
    conv_pos = [0]

    def do_conv(n):
        for _ in range(n):
            if conv_pos[0] >= len(conv_list):
                return
            s_, d_, b_ = conv_list[conv_pos[0]]
            conv_pos[0] += 1
            P.dma("pool", lambda e, s_=s_, d_=d_: e.dma_start(out=d_, in_=s_, max_dma_last_dim=4096), [], [Buf("cv")])

    do_conv(8 + 16)

    def rms_xn(x_ap, bx, xn_ap, bxn):
        i = ssq_i[0] % 4
        ssq_i[0] += 1
        s_, bs = ssq[i], bssq[i]
        P.op("dve", lambda e: e.scalar_tensor_tensor(out=junk, in0=x_ap, scalar=1.0, in1=x_ap, op0=ALU.mult, op1=ALU.mult,
                                                     accum_out=s_[:, 0:1]), [bx], [b_junk, bs])
        act(s_[:, 1:2], s_[:, 0:1], AF.Sqrt, [bs], [bs], scale=1.0 / D, bias=EPS)
        rcp(s_[:, 2:3], s_[:, 1:2], [bs], [bs])
        act(xn_ap, x_ap, AF.Copy, [bx, bs], [bxn], scale=s_[:, 2:3])

    def transposes(xn4, bxn4, hT, bhT, AT, shT, col0=0):
        for fc in range(16):
            b_ = fc % 2
            Bv = bank_bf(b_)
            for ti in range(4):
                tr(Bv[:, ti * 128:(ti + 1) * 128], xn4[:, ti, fc * 128:(fc + 1) * 128], identb,
                   [bxn4[ti], b_const], [bk[b_]])
            if fc % 2 == 0:
                act(hT[:, fc, col0:col0 + 512], Bv[:, 0:512], AF.Identity, [bk[b_], b_colp], [bhT[fc]],
                    scale=AT[:, fc:fc + 1], bias=shT[:, fc:fc + 1])
            else:
                ts("dve", hT[:, fc, col0:col0 + 512], Bv[:, 0:512], AT[:, fc:fc + 1], ALU.mult, [bk[b_], b_colp], [bhT[fc]],
                   s2=shT[:, fc:fc + 1], op1=ALU.add)

    TWO_PI = 2.0 * math.pi
    C1 = 6.28125
    C2 = TWO_PI - C1
    MAGIC = 12582912.0
    PI_LO = 3.1415925

    def rope_tables(posi, bposi, cosT, sinT, bcs, n):
        A, K, R, RS = TF[0][:, 0:n], TF[1][:, 0:n], TF[2][:, 0:n], TF[3][:, 0:n]
        bA, bK, bR, bRS = bTF[0], bTF[1], bTF[2], bTF[3]
        cp("dve", A, posi, [bposi], [bA])
        ts("dve", A, A, invf[:, 0:1], ALU.mult, [bA, b_const], [bA])
        ts("dve", K, A, 1.0 / TWO_PI, ALU.mult, [bA], [bK], s2=MAGIC, op1=ALU.add)
        ts("dve", K, K, -MAGIC, ALU.add, [bK], [bK])
        stt(R, K, -C1, A, ALU.mult, ALU.add, [bK, bA], [bR])
        stt(R, K, -C2, R, ALU.mult, ALU.add, [bK, bR], [bR])
        ts("dve", RS, R, -PI_LO, ALU.max, [bR], [bRS], s2=PI_LO, op1=ALU.min)
        act(sinT, RS, AF.Sin, [bRS], [bcs])
        ts("dve", K, R, math.pi / 2, ALU.add, [bR], [bK])
        ts("dve", A, K, math.pi, ALU.is_gt, [bK], [bA])
        stt(K, A, -TWO_PI, K, ALU.mult, ALU.add, [bA, bK], [bK])
        ts("dve", RS, K, -PI_LO, ALU.max, [bK], [bRS], s2=PI_LO, op1=ALU.min)
        act(cosT, RS, AF.Sin, [bRS], [bcs])

    def norm_rope_a(src, bsrc, gaincol, n, bankA):
        sq, xnb = TB[0][:, 0:n], TB[1][:, 0:n]
        rstd = TF[4][:, 0:n]
        act(sq, src, AF.Square, [bsrc], [bTB[0]])
        mm(banks[bankA][:, 0:n], onesb, sq, True, True, [bTB[0], b_const], [bk[bankA]])
        act(rstd, banks[bankA][:, 0:n], AF.Ln, [bk[bankA]], [bTF[4]], scale=1.0 / 128, bias=EPS)
        act(rstd, rstd, AF.Exp, [bTF[4]], [bTF[4]], scale=-0.5)
        stt(xnb, src, gaincol, rstd, ALU.mult, ALU.mult, [bsrc, bTF[4], b_colp], [bTB[1]])

    def norm_rope_b(cosT, sinT, bcs, out_, bout, n, bankB):
        xnb = TB[1][:, 0:n]
        ta, tb_ = TF[5][:, 0:n], TF[3][:, 0:n]
        mm(banks[bankB][:, 0:n], rot, xnb, True, True, [bTB[1], b_const], [bk[bankB]])
        tt("dve", ta, xnb, cosT, ALU.mult, [bTB[1], bcs], [bTF[5]])
        tt("dve", tb_, banks[bankB][:, 0:n], sinT, ALU.mult, [bk[bankB], bcs], [bTF[3]])
        tt("dve", out_, ta, tb_, ALU.add, [bTF[5], bTF[3]], [bout])

    def norm_rope(src, bsrc, gaincol, cosT, sinT, bcs, out_, bout, n, bankA, bankB):
        norm_rope_a(src, bsrc, gaincol, n, bankA)
        norm_rope_b(cosT, sinT, bcs, out_, bout, n, bankB)

    def run_jobs(jobs):
        n = len(jobs)
        if n:
            jobs[0][0]()
        for i in range(n):
            jobs[i][1]()
            if i + 1 < n:
                jobs[i + 1][0]()
            jobs[i][2]()
            if jobs[i][3] is not None:
                jobs[i][3]()

    P.push()
    cT = P.sb([128, 16], F32, "cT")
    c16 = P.sb([16, 128], F32, "c16")
    b_c = Buf("c")
    ld(c16, cvec, [], [b_c])
    tr(banks[0][:, 0:16], c16, identf[0:16, 0:16], [b_c, b_const], [bk[0]])
    act(cT, banks[0][:, 0:16], AF.Silu, [bk[0]], [b_c])
    adab = [P.sb([128, 16, 512], F32, f"adab{i}") for i in range(2)]
    b_adab = bufs("adab", 2)
    arow = [P.sb([1, 512], F32, f"arow{i}") for i in range(2)]
    b_arow = bufs("arow", 2)
    abrow = [P.sb([1, 512], F32, f"abrow{i}") for i in range(2)]
    b_abrow = bufs("abrow", 2)
    for bi in range(24):
        i = bi % 2
        ld(adab[i], ada_w[:, bi * 512:(bi + 1) * 512].rearrange("(c p) n -> p c n", p=128), [], [b_adab[i]])
        ld(abrow[i], ada_b[bi * 512:(bi + 1) * 512].rearrange("(o n) -> o n", o=1), [], [b_abrow[i]])
        bb = 2 + i
        for kc in range(16):
            mm(banks[bb][0:1, :], cT[:, kc:kc + 1], adab[i][:, kc, :], kc == 0, kc == 15, [b_c, b_adab[i]], [bk[bb]])
        tt("dve", arow[i], banks[bb][0:1, :], abrow[i], ALU.add, [bk[bb], b_abrow[i]], [b_arow[i]])
        ld(ada_d[bi * 512:(bi + 1) * 512].rearrange("(o n) -> o n", o=1), arow[i], [b_arow[i]], [b_adad], q="pool")
    adaT = P.sb([128, 96], F32, "adaT")
    b_adaT = Buf("adaT")
    for i6 in range(6):
        ld(adaT[:, i6 * 16:(i6 + 1) * 16], ada_d[i6 * D:(i6 + 1) * D].rearrange("(c p) -> p c", p=128),
           [b_adad], [b_adaT], slow=True)
    ts("dve", A1T, adaT[:, 16:32], 1.0, ALU.add, [b_adaT], [b_colp])
    tt("dve", A1T, A1T, nmT, ALU.mult, [b_colp], [b_colp])
    cp("dve", shaT, adaT[:, 0:16], [b_adaT], [b_colp])
    ts("dve", A2T, adaT[:, 64:80], 1.0, ALU.add, [b_adaT], [b_colp])
    tt("dve", A2T, A2T, nfT, ALU.mult, [b_colp], [b_colp])
    cp("dve", shfT, adaT[:, 48:64], [b_adaT], [b_colp])
    P.pop()
    P.barrier()

    P.push()
    xst = [P.sb([128, D], F32, f"xst{i}") for i in range(2)]
    b_xst = bufs("xst", 2)
    xn4s = [P.sb([128, 4, D], BF16, f"xn4_{i}") for i in range(2)]
    b_xn4s = [bufs(f"xn4_{i}_", 4) for i in range(2)]
    hT = P.sb([128, 16, 512], BF16, "hT")
    b_hT = bufs("hT", 16)
    wkv = P.sb([128, 16, 1024], BF16, "wkv")
    b_wkv = Buf("wkv")
    cbuf = P.sb([128, 4, 16 + 2048], BF16, "cbuf")
    b_cbuf = bufs("cbuf", 4)
    posis = [P.sb([128, 512], I32, f"posi{i}") for i in range(2)]
    b_posis = bufs("posi", 2)
    cosTs = [P.sb([128, 512], F32, f"cosT{i}") for i in range(2)]
    sinTs = [P.sb([128, 512], F32, f"sinT{i}") for i in range(2)]
    b_css = bufs("cs", 2)
    ccmp = P.sb([128, 128], F32, "ccmp")
    scmp = P.sb([128, 128], F32, "scmp")
    b_ccs = Buf("ccs")
    kst = [P.sb([128, 512], BF16, f"kst{i}") for i in range(2)]
    b_kst = bufs("kst", 2)
    vst = [P.sb([128, 4, 256], BF16, f"vst{i}") for i in range(2)]
    b_vst = bufs("vst", 2)
    w1t = P.sb([128, 32, 256], BF16, "w1t")
    b_w1t = Buf("w1t")
    pet = P.sb([32, 2, 128], F32, "pet")
    peT = P.sb([128, 2, 32], BF16, "peT")
    b_pe = Buf("pe")
    hid = P.sb([128, 2, 128], BF16, "hid")
    b_hid = Buf("hid")

    for idx in range(4):
        mset("pool", cbuf[:, idx, 0:16], 0.0, [b_cbuf[idx]])
    P.barrier()
    ld(wkv, wi_bf[:, 4096:5120].rearrange("(c p) n -> p c n", p=128), [], [b_wkv])

    ld(pet[:, 0, :], pe_k, [], [b_pe])
    ld(pet[:, 1, :], pe_v, [], [b_pe])
    for kv in range(2):
        tr(banks[0][:, kv * 32:(kv + 1) * 32], pet[:, kv, :], identf[0:32, 0:32], [b_pe, b_const], [bk[0]])
    cp("dve", peT, banks[0][:, 0:64].rearrange("p (a l) -> p a l", a=2), [bk[0]], [b_pe])
    for kv in range(2):
        ld(w1t, (ck1_bf if kv == 0 else cv1_bf).rearrange("l d h -> d l h"), [], [b_w1t])
        for hc in range(2):
            for l in range(32):
                mm(banks[1][:, kv * 2 + hc:kv * 2 + hc + 1], w1t[:, l, hc * 128:(hc + 1) * 128], peT[:, kv, l:l + 1],
                   l == 0, l == 31, [b_w1t, b_pe], [bk[1]])
    tt("dve", b1eff, banks[1][:, 0:4], b1c, ALU.add, [bk[1], b_colp], [b_colp])

    def compress(Q):
        for kv in range(2):
            ld(w1t, (ck1_bf if kv == 0 else cv1_bf).rearrange("l d h -> d l h"), [], [b_w1t])
            for g in range(2):
                idx = kv * 2 + g
                B_ = 2 + g
                for hc in range(2):
                    for l in range(32):
                        mm(banks[B_][:, hc * 128:(hc + 1) * 128], w1t[:, l, hc * 128:(hc + 1) * 128],
                           cbuf[:, idx, l:l + 2033:16], l == 0, l == 31, [b_w1t, b_cbuf[idx]], [bk[B_]])
                xh, x2, inner, sg = TF[0][:, 0:256], TF[1][:, 0:256], TF[2][:, 0:256], TF[3][:, 0:256]
                for hc in range(2):
                    act(xh[:, hc * 128:(hc + 1) * 128], banks[B_][:, hc * 128:(hc + 1) * 128], AF.Identity,
                        [bk[B_], b_colp], [bTF[0]], bias=b1eff[:, kv * 2 + hc:kv * 2 + hc + 1])
                tt("dve", x2, xh, xh, ALU.mult, [bTF[0]], [bTF[1]])
                ts("dve", x2, x2, 0.044715, ALU.mult, [bTF[1]], [bTF[1]], s2=1.0, op1=ALU.add)
                tt("dve", inner, x2, xh, ALU.mult, [bTF[1], bTF[0]], [bTF[2]])
                act(sg, inner, AF.Sigmoid, [bTF[2]], [bTF[3]], scale=1.5957691216057308)
                tt("dve", hid.rearrange("p a n -> p (a n)"), sg, xh, ALU.mult, [bTF[3], bTF[0]], [b_hid])
                if kv == 0:
                    for hc in range(2):
                        mm(banks[4][:, 0:128], w2c[:, 0, hc, :], hid[:, hc, :], hc == 0, hc == 1, [b_w2c, b_hid], [bk[4]])
                    norm_rope(banks[4][:, 0:128], bk[4], kn[:, 0:1], ccmp, scmp, b_ccs,
                              kcmpT[:, g, Q * 128:(Q + 1) * 128], b_kcmp, 128, 5, 6)
                else:
                    for hc in range(2):
                        mm(banks[4][:, 0:128], hid[:, hc, :], w2c[:, 1, hc, :], hc == 0, hc == 1, [b_w2c, b_hid], [bk[4]])
                    cp("act", vcmp[:, Q, g, :], banks[4][:, 0:128], [bk[4]], [b_vcmp])

    def a_load(p, ti):
        i = ti % 2
        ld(xst[i], xf[p * 512 + ti * 128:p * 512 + (ti + 1) * 128, :], [], [b_xst[i]])

    def a_rms(p, ti):
        q_ = p % 2
        i = ti % 2
        rms_xn(xst[i], b_xst[i], xn4s[q_][:, ti, :], b_xn4s[q_][ti])

    def a_tables(p):
        q_ = p % 2
        ld(posis[q_], posf[p * 512:(p + 1) * 512].partition_broadcast(128), [], [b_posis[q_]])
        rope_tables(posis[q_], b_posis[q_], cosTs[q_], sinTs[q_], b_css[q_], 512)

    for ti in range(4):
        if ti < 2:
            a_load(0, ti)
    for ti in range(4):
        a_rms(0, ti)
        if ti + 2 < 4:
            a_load(0, ti + 2)
    a_tables(0)
    for p in range(NCH):
        do_conv(6)
        nxt = p + 1 < NCH
        if nxt:
            a_load(p + 1, 0)
            a_load(p + 1, 1)
        xn4, b_xn4 = xn4s[p % 2], b_xn4s[p % 2]
        cosT, sinT, b_cs = cosTs[p % 2], sinTs[p % 2], b_css[p % 2]
        pq = p % 4
        cp("pool", ccmp[:, 32 * pq:32 * pq + 32], cosT[:, 15:512:16], [b_cs], [b_ccs])
        cp("pool", scmp[:, 32 * pq:32 * pq + 32], sinT[:, 15:512:16], [b_cs], [b_ccs])
        transposes(xn4, b_xn4, hT, b_hT, A1T, shaT)

        def kcvc(idx):
            B_ = 2 + idx % 2
            for kc in range(16):
                mm(banks[B_], wkv[:, kc, idx * 128:(idx + 1) * 128], hT[:, kc, :], kc == 0, kc == 15,
                   [b_wkv, b_hT[kc]], [bk[B_]])
            cp("act", cbuf[:, idx, 16 + 512 * pq:16 + 512 * pq + 512], banks[B_], [bk[B_]], [b_cbuf[idx]])

        def kslproj(g):
            B_ = 4 + g
            for kc in range(16):
                mm(banks[B_], wkv[:, kc, 512 + g * 128:512 + (g + 1) * 128], hT[:, kc, :], kc == 0, kc == 15,
                   [b_wkv, b_hT[kc]], [bk[B_]])

        vi = p % 2

        def vsl(ti):
            B_ = 2 + ti % 2
            for kc in range(16):
                mm(banks[B_][:, 0:256], hT[:, kc, ti * 128:(ti + 1) * 128], wkv[:, kc, 768:1024], kc == 0, kc == 15,
                   [b_wkv, b_hT[kc]], [bk[B_]])
            cp("act", vst[vi][:, ti, :], banks[B_][:, 0:256], [bk[B_]], [b_vst[vi]])

        kslproj(0)
        kslproj(1)
        kcvc(0)
        kcvc(1)
        if nxt:
            a_rms(p + 1, 0)
            a_load(p + 1, 2)
        norm_rope_a(banks[4], bk[4], kn[:, 1:2], 512, 6)
        kcvc(2)
        if nxt:
            a_rms(p + 1, 1)
            a_load(p + 1, 3)
        kcvc(3)
        norm_rope_b(cosT, sinT, b_cs, kst[0], b_kst[0], 512, 7)
        ld(ksT_d[0, :, p * 512:(p + 1) * 512], kst[0], [b_kst[0]], [b_ks[p]])
        if nxt:
            a_rms(p + 1, 2)
        norm_rope_a(banks[5], bk[5], kn[:, 1:2], 512, 6)
        vsl(0)
        vsl(1)
        norm_rope_b(cosT, sinT, b_cs, kst[1], b_kst[1], 512, 7)
        ld(ksT_d[1, :, p * 512:(p + 1) * 512], kst[1], [b_kst[1]], [b_ks[p]])
        if nxt:
            a_rms(p + 1, 3)
            a_tables(p + 1)
        vsl(2)
        vsl(3)
        ld(vs_d[p * 512:(p + 1) * 512, :].rearrange("(t p) n -> p t n", p=128), vst[vi], [b_vst[vi]], [b_vs[p]])
        if pq == 3:
            compress(p // 4)
            for idx in range(4):
                cp("pool", cbuf[:, idx, 0:16], cbuf[:, idx, 2048:2064], [b_cbuf[idx]], [b_cbuf[idx]])
    mset("dve", vcmp[0:1, 0, :, :], 0.0, [b_vcmp])
    dump("kcmpT", kcmpT, b_kcmp)
    dump("vcmp", vcmp, b_vcmp)
    do_conv(len(conv_list))
    P.pop()
    P.barrier()

    xown = P.sb([128, 4, D], F32, "xown")
    b_xown = bufs("xown", 4)
    for s in range(NSL):
        P.push()
        qT = P.sb([128, 8, 512], BF16, "qT")
        b_qT = bufs("qT", 8)
        ycT = P.sb([128, 8, 512], BF16, "ycT")
        b_ycT = bufs("ycT", 8)
        kwT = P.sb([128, 2, 1024], BF16, "kwT")
        b_kwT = Buf("kwT")
        vwt = P.sb([128, 8, 256], BF16, "vwt")
        b_vwt = Buf("vwt")

        P.push()
        xst = P.sb([128, D], F32, "xst")
        b_xst1 = Buf("xst1")
        xn4 = P.sb([128, 4, D], BF16, "xn4")
        b_xn4 = bufs("xn4", 4)
        hT = P.sb([128, 16, 512], BF16, "hT")
        b_hT = bufs("hT", 16)
        hT2 = P.sb([128, 16, 2], BF16, "hT2")
        b_hT2 = Buf("hT2")
        wtail = P.sb([128, 16, 536], BF16, "wtail")
        b_wtail = Buf("wtail")
        NWB = 3
        wblk = [P.sb([128, 16, 256], BF16, f"wblk{i}") for i in range(NWB)]
        b_wblk = bufs("wblk", NWB)
        wb_i = [0]
        posi = P.sb([128, 1024], I32, "posi")
        b_posi = Buf("posi")
        cosq = P.sb([128, 1024], F32, "cosq")
        sinq = P.sb([128, 1024], F32, "sinq")
        b_cs = Buf("cs")
        gT = P.sb([24, 512], F32, "gT")
        b_gT = Buf("gT")
        uh = P.sb([128, 4], F32, "uh")
        b_uh = Buf("uh")
        ubuf = P.sb([128, 514], F32, "ubuf")
        b_ubuf = Buf("ubuf")

        for ti in range(4):
            ld(xown[:, ti, :], xq[s, ti * 128:(ti + 1) * 128, :], [], [b_xown[ti]])
        ld(wtail, wi_bf[:, 5120:INW].rearrange("(c p) n -> p c n", p=128), [], [b_wtail])
        ld(posi, posq[s * 1024:(s + 1) * 1024].partition_broadcast(128), [], [b_posi])

        def load_wblk(c0):
            i = wb_i[0] % NWB
            wb_i[0] += 1
            ld(wblk[i], wi_bf[:, c0:c0 + 256].rearrange("(c p) n -> p c n", p=128), [], [b_wblk[i]])
            return wblk[i], b_wblk[i]

        for part in range(2):
            for ti in range(4):
                rms_xn(xown[:, ti, :], b_xown[ti], xn4[:, ti, :], b_xn4[ti])
            if part == 0:
                for ti in range(4):
                    r0 = 512 + ti * 128
                    ld(xown[:, ti, :], xq[s, r0:r0 + 128, :], [], [b_xown[ti]])
                for hh in range(2):
                    rope_tables(posi[:, hh * 512:(hh + 1) * 512], b_posi, cosq[:, hh * 512:(hh + 1) * 512],
                                sinq[:, hh * 512:(hh + 1) * 512], b_cs, 512)
            transposes(xn4, b_xn4, hT, b_hT, A1T, shaT)
            if part == 0:
                cp("pool", hT2, hT[:, :, 510:512], b_hT, [b_hT2])
            jobs = []
            for g in range(2):
                def proj(g=g):
                    B_ = 2 + g
                    for kc in range(16):
                        mm(banks[B_], wtail[:, kc, g * 128:(g + 1) * 128], hT[:, kc, :], kc == 0, kc == 15,
                           [b_wtail, b_hT[kc]], [bk[B_]])
                def nra(g=g):
                    norm_rope_a(banks[2 + g], bk[2 + g], kn[:, 2:3], 512, 6)
                def nrb(g=g, part=part):
                    norm_rope_b(cosq[:, part * 512:(part + 1) * 512], sinq[:, part * 512:(part + 1) * 512], b_cs,
                                kwT[:, g, part * 512:(part + 1) * 512], b_kwT, 512, 7)
                jobs.append((proj, nra, nrb, None))
            run_jobs(jobs)
            for ti in range(4):
                B_ = 4 + ti % 2
                for kc in range(16):
                    mm(banks[B_][:, 0:256], hT[:, kc, ti * 128:(ti + 1) * 128], wtail[:, kc, 256:512], kc == 0, kc == 15,
                       [b_wtail, b_hT[kc]], [bk[B_]])
                cp("act", vwt[:, part * 4 + ti, :], banks[B_][:, 0:256], [bk[B_]], [b_vwt])

        for kc in range(16):
            mm(banks[2][0:24, :], wtail[:, kc, 512:536], hT[:, kc, :], kc == 0, kc == 15, [b_wtail, b_hT[kc]], [bk[2]])
        act(gT, banks[2][0:24, :], AF.Sigmoid, [bk[2]], [b_gT])
        ld(gd, gT, [b_gT], [b_gd], q="pool")

        for cgp in range(4):
            wcc, bwcc = load_wblk(1024 + cgp * 256)
            wch, bwch = load_wblk(2048 + cgp * 256)
            wcb, bwcb = load_wblk(cgp * 256)
            for c2 in range(2):
                cg = cgp * 2 + c2
                cs_ = slice(c2 * 128, (c2 + 1) * 128)
                Bcc, Bch, Bcb = (2, 3, 4) if cg % 2 == 0 else (0, 1, 7)
                for kc in range(16):
                    mm(banks[Bcc], wcc[:, kc, cs_], hT[:, kc, :], kc == 0, kc == 15, [bwcc, b_hT[kc]], [bk[Bcc]])
                for kc in range(16):
                    mm(banks[5][:, 0:2], wcc[:, kc, cs_], hT2[:, kc, :], kc == 0, kc == 15, [bwcc, b_hT2], [bk[5]])
                for kc in range(16):
                    mm(banks[Bch], wch[:, kc, cs_], hT[:, kc, :], kc == 0, kc == 15, [bwch, b_hT[kc]], [bk[Bch]])
                for kc in range(16):
                    mm(banks[5][:, 2:4], wch[:, kc, cs_], hT2[:, kc, :], kc == 0, kc == 15, [bwch, b_hT2], [bk[5]])
                for kc in range(16):
                    mm(banks[Bcb], wcb[:, kc, cs_], hT[:, kc, :], kc == 0, kc == 15, [bwcb, b_hT[kc]], [bk[Bcb]])
                cp("act", uh, banks[5][:, 0:4], [bk[5]], [b_uh])
                tt("dve", ubuf[:, 0:2], uh[:, 0:2], uh[:, 2:4], ALU.mult, [b_uh], [b_ubuf])
                ts("dve", ubuf[:, 0:2], ubuf[:, 0:2], hvt[:, s:s + 1], ALU.mult, [b_ubuf, b_const], [b_ubuf])
                ccs = TF[0]
                cp("act", ccs, banks[Bcc], [bk[Bcc]], [bTF[0]])
                tt("dve", ubuf[:, 2:514], ccs, banks[Bch], ALU.mult, [bTF[0], bk[Bch]], [b_ubuf])
                y = TF[1]
                ts("dve", y, ubuf[:, 2:514], cw[:, cg, 2:3], ALU.mult, [b_ubuf, b_colp], [bTF[1]])
                stt(y, ubuf[:, 1:513], cw[:, cg, 1:2], y, ALU.mult, ALU.add, [b_ubuf, bTF[1], b_colp], [bTF[1]])
                stt(y, ubuf[:, 0:512], cw[:, cg, 0:1], y, ALU.mult, ALU.add, [b_ubuf, bTF[1], b_colp], [bTF[1]])
                tt("dve", y, y, banks[Bcb], ALU.mult, [bTF[1], bk[Bcb]], [bTF[1]])
                sq = TB[0]
                act(sq, y, AF.Square, [bTF[1]], [bTB[0]])
                mm(banks[6], onesb, sq, True, True, [bTB[0], b_const], [bk[6]])
                rstd = TF[4]
                act(rstd, banks[6], AF.Ln, [bk[6]], [bTF[4]], scale=1.0 / 128, bias=EPS)
                act(rstd, rstd, AF.Exp, [bTF[4]], [bTF[4]], scale=-0.5)
                stt(ycT[:, cg, :], y, onc[:, cg:cg + 1], rstd, ALU.mult, ALU.mult, [bTF[1], bTF[4], b_colp], [b_ycT[cg]])
        jobs = []
        qw = {}
        for h in range(8):
            def proj(h=h):
                qb, c2 = h // 2, h % 2
                if c2 == 0:
                    qw[qb] = load_wblk(3072 + qb * 256)
                wq, bwq = qw[qb]
                B_ = 2 + c2
                for kc in range(16):
                    mm(banks[B_], wq[:, kc, c2 * 128:(c2 + 1) * 128], hT[:, kc, :], kc == 0, kc == 15,
                       [bwq, b_hT[kc]], [bk[B_]])
            def nra(h=h):
                norm_rope_a(banks[2 + h % 2], bk[2 + h % 2], qn, 512, 6)
            def nrb(h=h):
                norm_rope_b(cosq[:, 512:1024], sinq[:, 512:1024], b_cs, qT[:, h, :], b_qT[h], 512, 7)
            jobs.append((proj, nra, nrb, None))
        run_jobs(jobs)
        if s == 0:
            dump("qT", qT, b_qT[7])
            dump("ycT", ycT, b_ycT[7])
            dump("kwT", kwT, b_kwT)
            dump("vwt", vwt, b_vwt)
            dump("gT", gT, b_gT)
        P.pop()
        P.barrier()

        P.push()
        yaT = None
        wm = P.sb([128, 8, 512], BF16, "wm")
        b_wm = Buf("wm")
        cm = P.sb([128, 2, 512], BF16, "cm")
        b_cm = Buf("cm")
        selb = P.sb([128, 4, NSB], F32, "selb")
        b_selb = Buf("selb")
        imp = P.sb([128, 4, NSB], F32, "imp")
        b_imp = bufs("imp", 4)
        NJP = max(1, NSB // 128)
        selT = P.sb([128, NJP, 512], BF16, "selT")
        b_selT = Buf("selT")
        oacc = P.sb([128, 8, 512], F32, "oacc")
        b_oacc = bufs("oacc", 8)
        pstore = P.sb([128, 2, NSC, 1024], BF16, "pstore")
        b_pst = [[Buf(f"pst{r}_{c}") for c in range(NSC)] for r in range(2)]
        NPT = 3
        Pt2 = [P.sb([128, 1024], BF16, f"Pt2_{i}") for i in range(NPT)]
        b_Pt2 = bufs("Pt2", NPT)
        LaccP = PS2[3]
        b_LaccP = [bk[6], bk[7]]
        slmt = [P.sb([128, 512], BF16, f"slmt{i}") for i in range(2)]
        b_slmt = bufs("slmt", 2)
        gb = [P.sb([128, 512], F32, f"gb{i}") for i in range(2)]
        b_gb = bufs("gb", 2)
        kstl = [P.sb([128, 512], BF16, f"kstl{i}") for i in range(2)]
        b_kstl = bufs("kstl", 2)
        vstl = [P.sb([128, 4, 128], BF16, f"vstl{i}") for i in range(2)]
        b_vstl = bufs("vstl", 2)
        score = P.sb([128, NSB], F32, "score")
        sc2 = P.sb([128, NSB], F32, "sc2")
        m8 = P.sb([128, 16], F32, "m8")
        selm = P.sb([128, NSB], BF16, "selm")
        b_tk = Buf("topk")
        rl = P.sb([128, 4], F32, "rl")
        b_rl = Buf("rl")

        ld(wm, c_wm0 if s == 0 else c_wmg, [], [b_wm])
        ld(cm, c_cm, [], [b_cm])
        ld(selb, c_selb[:, s, :, :], [], [b_selb])
        gb_i = [0]
        pipe = {"v": 0, "pend": None, "pt": 0}

        def pv_flush():
            pd = pipe["pend"]
            pipe["pend"] = None
            if pd is None:
                return
            pt_ap, pt_buf, first, last, vt_ap, vt_bufs = pd
            for r_ in range(2):
                ob = 4 + r_
                mm(banks[ob], vt_ap, pt_ap[:, r_ * 512:(r_ + 1) * 512], first, last, vt_bufs + [pt_buf], [bk[ob]])

        def unit(qk, pt_ap, pt_buf, first, last, vt_ap, vt_bufs):
            k = pipe["v"] % 2
            pipe["v"] += 1
            sb_ = [bk[2 * k], bk[2 * k + 1]]
            for r_ in range(2):
                n = len(qk[r_])
                for i_, (l_, rh_, Rb) in enumerate(qk[r_]):
                    mm(PS2[k][:, r_ * 512:(r_ + 1) * 512], l_, rh_, i_ == 0, i_ == n - 1, Rb, sb_)
            act(pt_ap, PS2[k], AF.Exp, sb_, [pt_buf], scale=SCALE)
            if first:
                cp("dve", LaccP, pt_ap, [pt_buf], b_LaccP)
            else:
                tt("dve", LaccP, LaccP, pt_ap, ALU.add, [pt_buf] + b_LaccP, b_LaccP)
            pv_flush()
            pipe["pend"] = (pt_ap, pt_buf, first, last, vt_ap, vt_bufs)

        def next_pt():
            i_ = pipe["pt"] % NPT
            pipe["pt"] += 1
            return Pt2[i_], b_Pt2[i_]

        def finalize(h, x):
            r_ = h % 2
            ob = 4 + r_
            lb = r_
            wt, tmp, lsb = TF[0], TF[1], TF[2]
            gi = gb_i[0] % 2
            gb_i[0] += 1
            ld(gb[gi], gd[h * 3 + x, :].partition_broadcast(128), [b_gd], [b_gb[gi]])
            cp("act", lsb, LaccP[:, r_ * 512:(r_ + 1) * 512], b_LaccP, [bTF[2]])
            mm(banks[lb], onesf, lsb, True, True, [b_const, bTF[2]], [bk[lb]])
            ts("dve", wt, banks[lb], 1e-30, ALU.max, [bk[lb]], [bTF[0]])
            rcp(wt, wt, [bTF[0]], [bTF[0]])
            tt("dve", wt, wt, gb[gi], ALU.mult, [bTF[0], b_gb[gi]], [bTF[0]])
            if x == 0:
                tt("dve", oacc[:, h, :], banks[ob], wt, ALU.mult, [bk[ob], bTF[0]], [b_oacc[h]])
            else:
                tt("dve", tmp, banks[ob], wt, ALU.mult, [bk[ob], bTF[0]], [bTF[1]])
                tt("dve", oacc[:, h, :], oacc[:, h, :], tmp, ALU.add, [bTF[1], b_oacc[h]], [b_oacc[h]])

        for g in range(2):
            for hp in range(2):
                for c in range(s + 1):
                    qk = []
                    for r in range(2):
                        h = 4 * g + 2 * hp + r
                        lst = [(kcmpT[:, g, c * 128:(c + 1) * 128], qT[:, h, :], [b_kcmp, b_qT[h]])]
                        if c == 0:
                            lst.append((identb, cm[:, 1, :], [b_const, b_cm]))
                        if c == s:
                            lst.append((identb, cm[:, 0, :], [b_const, b_cm]))
                        qk.append(lst)
                    unit(qk, pstore[:, hp, c, :], b_pst[hp][c], c == 0, c == s, vcmp[:, c, g, :], [b_vcmp])
                pv_flush()
                for r in range(2):
                    h = 4 * g + 2 * hp + r
                    finalize(h, 0)
                    for tb in range(4):
                        for c in range(s + 1):
                            mm(banks[2][:, 0:NSB + 1], pstore[:, hp, c, r * 512 + tb * 128:r * 512 + (tb + 1) * 128], ovl[:, c, :],
                               c == 0, c == s, [b_pst[hp][c], b_const], [bk[2]])
                        ts("dve", rl[:, 0:1], banks[2][:, NSB:NSB + 1], 1e-30, ALU.max, [bk[2]], [b_rl])
                        rcp(rl[:, 1:2], rl[:, 0:1], [b_rl], [b_rl])
                        if hp == 0 and r == 0:
                            ts("dve", imp[:, tb, :], banks[2][:, 0:NSB], rl[:, 1:2], ALU.mult, [bk[2], b_rl], [b_imp[tb]])
                        else:
                            stt(imp[:, tb, :], banks[2][:, 0:NSB], rl[:, 1:2], imp[:, tb, :], ALU.mult, ALU.add,
                                [bk[2], b_rl, b_imp[tb]], [b_imp[tb]])
            for tb in range(4):
                tt("dve", score, imp[:, tb, :], selb[:, tb, :], ALU.add, [b_imp[tb], b_selb], [b_tk])
                P.op("dve", lambda e, m8=m8, score=score: e.max(out=m8[:, 0:8], in_=score), [b_tk], [b_tk])
                P.op("dve", lambda e, m8=m8, score=score, sc2=sc2: e.match_replace(
                    out=sc2, in_to_replace=m8[:, 0:8], in_values=score, imm_value=-3.0e38), [b_tk], [b_tk])
                P.op("dve", lambda e, m8=m8, sc2=sc2: e.max(out=m8[:, 8:16], in_=sc2), [b_tk], [b_tk])
                ts("dve", sc2, score, m8[:, 15:16], ALU.is_ge, [b_tk], [b_tk])
                ts("dve", selm, sc2, -1.0, ALU.add, [b_tk], [b_tk], s2=30000.0, op1=ALU.mult)
                Bv = bank_bf(3)
                w_ = min(128, NSB)
                for jp in range(NJP):
                    tr(Bv[0:w_, jp * 128:jp * 128 + 128], selm[:, jp * 128:jp * 128 + w_], identb, [b_tk, b_const], [bk[3]])
                for jp in range(NJP):
                    cp("act", selT[0:w_, jp, tb * 128:(tb + 1) * 128], Bv[0:w_, jp * 128:jp * 128 + 128], [bk[3]], [b_selT])
            if s == NSL - 1 and g == 0:
                dump("imp", imp, b_imp[3])
                dump("selT", selT, b_selT)
            NKT = 16 * s + 16
            for hp in range(2):
                for kt4 in range(NKT // 4):
                    li = kt4 % 2
                    ld(kstl[li], ksT_d[g, :, kt4 * 512:(kt4 + 1) * 512], [b_ks[kt4]], [b_kstl[li]])
                    ld(vstl[li], vs_d[kt4 * 512:(kt4 + 1) * 512, g * 128:(g + 1) * 128].rearrange("(t p) d -> p t d", p=128),
                       [b_vs[kt4]], [b_vstl[li]])
                    for k4 in range(4):
                        kt = kt4 * 4 + k4
                        j0 = 2 * kt
                        jp = j0 // 128
                        KE = min(128, NSB)
                        em = (Emat[0:KE, kt % 64, :], selT[0:KE, jp, :], [b_const, b_selT])
                        diag = kt >= 16 * s
                        if diag:
                            rr = kt - 16 * s
                            di = rr % 2
                            ld(slmt[di], c_slm[rr], [], [b_slmt[di]])
                        qk = []
                        for r in range(2):
                            h = 4 * g + 2 * hp + r
                            lst = [(kstl[li][:, k4 * 128:(k4 + 1) * 128], qT[:, h, :], [b_kstl[li], b_qT[h]]), em]
                            if diag:
                                lst.append((identb, slmt[di], [b_const, b_slmt[di]]))
                            qk.append(lst)
                        pt_ap, pt_buf = next_pt()
                        unit(qk, pt_ap, pt_buf, kt == 0, kt == NKT - 1, vstl[li][:, k4, :], [b_vstl[li]])
                pv_flush()
                for r in range(2):
                    finalize(4 * g + 2 * hp + r, 1)
            for hp in range(2):
                for kt in range(8):
                    qk = []
                    for r in range(2):
                        h = 4 * g + 2 * hp + r
                        qk.append([(kwT[:, g, kt * 128:(kt + 1) * 128], qT[:, h, :], [b_kwT, b_qT[h]]),
                                   (identb, wm[:, kt, :], [b_const, b_wm])])
                    pt_ap, pt_buf = next_pt()
                    unit(qk, pt_ap, pt_buf, kt == 0, kt == 7, vwt[:, kt, g * 128:(g + 1) * 128], [b_vwt])
                pv_flush()
                for r in range(2):
                    finalize(4 * g + 2 * hp + r, 2)
        if s == NSL - 1:
            dump("oacc", oacc, b_oacc[7])
        XB_ = 2
        yaT = qT
        b_yaT = b_qT
        for h in range(8):
            sq = TB[0]
            act(sq, oacc[:, h, :], AF.Square, [b_oacc[h]], [bTB[0]])
            mm(banks[XB_], onesb, sq, True, True, [bTB[0], b_const], [bk[XB_]])
            rstd = TF[4]
            act(rstd, banks[XB_], AF.Ln, [bk[XB_]], [bTF[4]], scale=1.0 / 128, bias=EPS)
            act(rstd, rstd, AF.Exp, [bTF[4]], [bTF[4]], scale=-0.5)
            stt(yaT[:, h, :], oacc[:, h, :], ona[:, h:h + 1], rstd, ALU.mult, ALU.mult, [b_oacc[h], bTF[4], b_colp], [b_yaT[h]])
        P.pop()
        P.barrier()

        P.push()
        wob = [P.sb([128, 16, 512], BF16, f"wob{i}") for i in range(2)]
        b_wob = bufs("wob", 2)
        gab = [P.sb([128, 512], F32, f"gab{i}") for i in range(2)]
        b_gab = bufs("gab", 2)
        for oc in range(4):
            i = oc % 2
            ld(wob[i], wo_bf[:, oc * 512:(oc + 1) * 512].rearrange("(c p) n -> p c n", p=128), [], [b_wob[i]])
            ld(gab[i], ada_d[2 * D + oc * 512:2 * D + (oc + 1) * 512].partition_broadcast(128), [], [b_gab[i]])
            for tb in range(4):
                B_ = tb % 4
                for mc in range(16):
                    src_, bsrc = (ycT[:, mc, :], b_ycT[mc]) if mc < 8 else (yaT[:, mc - 8, :], b_yaT[mc - 8])
                    mm(banks[B_], src_[:, tb * 128:(tb + 1) * 128], wob[i][:, mc, :], mc == 0, mc == 15,
                       [bsrc, b_wob[i]], [bk[B_]])
                tmp = TF[tb % 2]
                tt("dve", tmp, banks[B_], gab[i], ALU.mult, [bk[B_], b_gab[i]], [bTF[tb % 2]])
                xs_ = xown[:, tb, oc * 512:(oc + 1) * 512]
                tt("dve", xs_, xs_, tmp, ALU.add, [bTF[tb % 2], b_xown[tb]], [b_xown[tb]])
        P.pop()
        P.barrier()
        P.pop()
        if s == 0:
            dump("x1", xown, b_xown[3])

        P.push()
        hT = P.sb([128, 16, 512], BF16, "hT")
        b_hT = bufs("hT", 16)
        actT = P.sb([128, 44, 512], BF16, "actT")
        b_actT = bufs("actT", 44)
        w13 = [P.sb([128, 2, 16, 256], BF16, f"w13_{i}") for i in range(2)]
        b_w13 = bufs("w13", 2)
        P.push()
        xn4 = P.sb([128, 4, D], BF16, "xn4")
        b_xn4 = bufs("xn4", 4)
        for ti in range(4):
            rms_xn(xown[:, ti, :], b_xown[ti], xn4[:, ti, :], b_xn4[ti])
        transposes(xn4, b_xn4, hT, b_hT, A2T, shfT)
        P.pop()
        P.barrier()
        w2b = [P.sb([128, 11, 512], BF16, f"w2b{i}") for i in range(2)]
        b_w2b = bufs("w2b", 2)
        gfb = [P.sb([128, 512], F32, f"gfb{i}") for i in range(2)]
        b_gfb = bufs("gfb", 2)
        for f2 in range(22):
            i = f2 % 2
            ld(w13[i][:, 0, :, :], w1_bf[:, f2 * 256:(f2 + 1) * 256].rearrange("(c p) n -> p c n", p=128), [], [b_w13[i]])
            ld(w13[i][:, 1, :, :], w3_bf[:, f2 * 256:(f2 + 1) * 256].rearrange("(c p) n -> p c n", p=128), [], [b_w13[i]])
            for c2 in range(2):
                fc = f2 * 2 + c2
                GB_, UB_ = (0, 1) if fc % 2 == 0 else (2, 3)
                for kc in range(16):
                    mm(banks[GB_], w13[i][:, 0, kc, c2 * 128:(c2 + 1) * 128], hT[:, kc, :], kc == 0, kc == 15,
                       [b_w13[i], b_hT[kc]], [bk[GB_]])
                for kc in range(16):
                    mm(banks[UB_], w13[i][:, 1, kc, c2 * 128:(c2 + 1) * 128], hT[:, kc, :], kc == 0, kc == 15,
                       [b_w13[i], b_hT[kc]], [bk[UB_]])
                sg = TF[fc % 2]
                act(sg, banks[GB_], AF.Silu, [bk[GB_]], [bTF[fc % 2]])
                tt("dve", actT[:, fc, :], sg, banks[UB_], ALU.mult, [bTF[fc % 2], bk[UB_]], [b_actT[fc]])
        w2i = [0]
        for oc in range(4):
            gi = oc % 2
            ld(gfb[gi], ada_d[5 * D + oc * 512:5 * D + (oc + 1) * 512].partition_broadcast(128), [], [b_gfb[gi]])
            for fg in range(4):
                i = w2i[0] % 2
                w2i[0] += 1
                ld(w2b[i], w2_bf[fg * 1408:(fg + 1) * 1408, oc * 512:(oc + 1) * 512].rearrange("(c p) n -> p c n", p=128),
                   [], [b_w2b[i]])
                for tb in range(4):
                    B_ = 4 + tb
                    for i11 in range(11):
                        fc = fg * 11 + i11
                        mm(banks[B_], actT[:, fc, tb * 128:(tb + 1) * 128], w2b[i][:, i11, :], fc == 0, fc == 43,
                           [b_actT[fc], b_w2b[i]], [bk[B_]])
            for tb in range(4):
                B_ = 4 + tb
                tmp = TF[2 + tb % 2]
                tt("dve", tmp, banks[B_], gfb[gi], ALU.mult, [bk[B_], b_gfb[gi]], [bTF[2 + tb % 2]])
                xs_ = xown[:, tb, oc * 512:(oc + 1) * 512]
                tt("dve", xs_, xs_, tmp, ALU.add, [bTF[2 + tb % 2], b_xown[tb]], [b_xown[tb]])
        for tb in range(4):
            ld(out[s * 512 + tb * 128:s * 512 + (tb + 1) * 128, :], xown[:, tb, :], [b_xown[tb]], [b_out], q="pool")
        P.pop()
        P.barrier()

    P.emit()
    return P, dumps


def host_consts(S, j):
    NSL = S // 2048
    NSC = S // 2048
    NSB = S // 64
    c = {}
    c["c_identb"] = np.eye(128, dtype=np.float32).astype(NPBF)
    c["c_identf"] = np.stack([np.eye(128, dtype=np.float32), np.ones((128, 128), np.float32)], 1)
    ones2 = np.ones((128, 2, 128), np.float32)
    ones2[0, 1, :] = 0.0
    c["c_ones"] = ones2.astype(NPBF)
    rot = np.zeros((128, 128), np.float32)
    for m in range(64):
        rot[m + 64, m] = -1.0
    for m in range(64, 128):
        rot[m - 64, m] = 1.0
    c["c_rot"] = rot.astype(NPBF)
    inv = (1.0 / (np.float32(10000.0) ** (np.arange(0, 128, 2, dtype=np.float32) / np.float32(128)))).astype(np.float32)
    c["c_invf"] = np.concatenate([inv, inv]).reshape(128, 1).astype(np.float32)
    ovl = np.zeros((128, NSC, NSB + 1), np.float32)
    for cc in range(NSC):
        for p in range(128):
            n = 128 * cc - 1 + p
            if n < 0:
                continue
            ovl[p, cc, NSB] = 1.0
            for jb in range(NSB):
                if 16 * n < 64 * jb + 64 and 16 * n + 31 >= 64 * jb:
                    ovl[p, cc, jb] = 1.0
    c["c_ovl"] = ovl.astype(NPBF)
    E = np.zeros((128, 64, 128), np.float32)
    for v in range(64):
        for key in range(128):
            E[2 * v + key // 64, v, key] = 1.0
    c["c_E"] = E.astype(NPBF)
    pp = np.arange(128)[:, None, None]
    kt = np.arange(8)[None, :, None]
    ii = np.arange(512)[None, None, :]
    kr = 128 * kt + pp
    tr_ = 512 + ii
    wm = ((kr <= tr_) & (kr > tr_ - 512)).astype(np.float32)
    wm0 = wm.copy()
    if j == 0:
        wm0[:, 0:4, :] = 0.0
    c["c_wmg"] = ((wm - 1.0) * NEGM).astype(NPBF)
    c["c_wm0"] = ((wm0 - 1.0) * NEGM).astype(NPBF)
    rr = np.arange(16)[:, None, None]
    p2 = np.arange(128)[None, :, None]
    slm = (128 * rr + p2 <= 512 * j + ii).astype(np.float32)
    c["c_slm"] = ((slm - 1.0) * NEGM).astype(NPBF)
    pcol = np.arange(128)[:, None]
    irow = np.arange(512)[None, :]
    cmv = (16 * pcol + 15 <= 512 * j + irow).astype(np.float32)
    cm0 = np.ones((128, 512), np.float32)
    cm0[0, :] = 0.0
    c["c_cm"] = ((np.stack([cmv, cm0], 1) - 1.0) * NEGM).astype(NPBF)
    selb = np.zeros((128, NSL, 4, NSB), np.float32)
    jb = np.arange(NSB)[None, :]
    for s in range(NSL):
        for tb in range(4):
            t = 512 * (4 * s + j) + 128 * tb + np.arange(128)[:, None]
            cur = t // 64
            valid = 64 * jb <= t
            forced = (jb == 0) | (jb == cur) | (jb == cur - 1)
            selb[:, s, tb, :] = np.where(valid, np.where(forced, 1e4, 0.0), -1e30)
    c["c_selb"] = selb
    hvv = np.ones((128, NSL), np.float32)
    if j == 0:
        hvv[:, 0] = 0.0
    c["hv"] = hvv
    return c


def make_in_maps(inputs, S, ncores=8):
    x = np.asarray(inputs["x"])
    cvec = np.asarray(inputs["c"])
    pos = np.asarray(inputs["positions"]).astype(np.int32)
    NSL = S // 2048
    wnames = ["ada_w", "ada_b", "norm_mix", "norm_ffn", "w_in", "conv_w", "cmp_pe_k", "cmp_k_w1", "cmp_k_b1",
              "cmp_k_w2", "cmp_pe_v", "cmp_v_w1", "cmp_v_b1", "cmp_v_w2", "q_norm", "k_norm", "out_norm_conv",
              "out_norm_attn", "w_out", "ffn_w1", "ffn_w3", "ffn_w2"]
    shared = {}
    for n in wnames:
        a = np.ascontiguousarray(np.asarray(inputs[n], dtype=np.float32)[0])
        shared[n] = a
    in_maps = []
    for core in range(ncores):
        b, j = core // 4, core % 4
        m = dict(shared)
        m["xf"] = np.ascontiguousarray(x[b, :S])
        xqv = np.zeros((NSL, 1024, D), np.float32)
        pq = np.zeros((NSL, 1024), np.int32)
        for s in range(NSL):
            p = 4 * s + j
            lo = 512 * p - 512
            if lo >= 0:
                xqv[s] = x[b, lo:lo + 1024]
                pq[s] = pos[b, lo:lo + 1024]
            else:
                xqv[s, 512:] = x[b, 0:512]
                pq[s, 512:] = pos[b, 0:512]
        m["xq"] = xqv
        m["posf"] = np.ascontiguousarray(pos[b, :S])
        m["posq"] = pq.reshape(-1)
        m["cvec"] = np.ascontiguousarray(cvec[b].reshape(16, 128))
        m.update(host_consts(S, j))
        in_maps.append(m)
    return in_maps


_CACHE = {}


def kernel(**inputs):
    S = 16384
    if S not in _CACHE:
        nc = bass.Bass("TRN2", target_bir_lowering=False)
        build(nc, S)
        _CACHE[S] = nc
    nc = _CACHE[S]
    in_maps = make_in_maps(inputs, S)
    res = run_bass_kernel_spmd(nc, in_maps, core_ids=list(range(8)))
    NSL = S // 2048
    outp = np.zeros((2, S, D), np.float32)
    for core in range(8):
        b, j = core // 4, core % 4
        o = res.results[core]["out"]
        for s in range(NSL):
            p = 4 * s + j
            outp[b, 512 * p:512 * (p + 1)] = o[512 * s:512 * (s + 1)]
    return outp
```

```python
import math
import numpy as np
import ml_dtypes
import concourse.bass as bass
import concourse.mybir as mybir
from concourse.bass_utils import run_bass_kernel_spmd

F32 = mybir.dt.float32
BF16 = mybir.dt.bfloat16
I32 = mybir.dt.int32
AF = mybir.ActivationFunctionType
ALU = mybir.AluOpType
NPBF = ml_dtypes.bfloat16

D = 2048
DFF = 5632
INW = 5656
EPS = 1e-6
SCALE = 128 ** -0.5
NEGM = 30000.0
ENGS = ("pe", "act", "dve", "pool", "sp")


class Buf:
    __slots__ = ("name", "w", "rs")

    def __init__(self, name):
        self.name = name
        self.w = None
        self.rs = {}


def bufs(name, n):
    return [Buf(f"{name}{i}") for i in range(n)]


class Prog:
    NDMA = {"sp": 24, "pool": 16}

    def __init__(self, nc):
        self.nc = nc
        self.ops = {e: [] for e in ENGS}
        self.cnt = {e: 0 for e in ENGS}
        self.seen = {e: {} for e in ENGS}
        self.dcount = {}
        self.drr = {q: 0 for q in self.NDMA}
        for q, n in self.NDMA.items():
            for i in range(n):
                self.dcount[(q, i)] = 0
        self.sb_off = 16640
        self.sb_stack = []
        self.ntens = 0
        self.hw = 0

    def sb(self, shape, dtype, name="t"):
        esz = {F32: 4, BF16: 2, I32: 4}[dtype]
        n = 1
        for s in shape[1:]:
            n *= s
        nbytes = (n * esz + 63) // 64 * 64
        off = self.sb_off
        self.sb_off += nbytes
        self.hw = max(self.hw, self.sb_off)
        assert self.sb_off <= 229300, f"SBUF overflow {self.sb_off} at {name}"
        self.ntens += 1
        t = self.nc.alloc_sbuf_tensor_at(f"{name}_{self.ntens}", list(shape), dtype, offset=off)
        return t.ap()

    def push(self):
        self.sb_stack.append(self.sb_off)

    def pop(self):
        self.sb_off = self.sb_stack.pop()

    def _deps(self, eng, reads, writes):
        need = {}
        for b in reads:
            if b.w is not None:
                k, v = b.w
                if need.get(k, 0) < v:
                    need[k] = v
        for b in writes:
            if b.w is not None:
                k, v = b.w
                if need.get(k, 0) < v:
                    need[k] = v
            for k, v in b.rs.items():
                if need.get(k, 0) < v:
                    need[k] = v
        waits = []
        seen = self.seen[eng]
        for k, v in need.items():
            if eng == "pe" and k == "pe":
                continue
            if seen.get(k, 0) >= v:
                continue
            seen[k] = v
            waits.append((k, v))
        return waits

    def _post(self, ev, reads, writes):
        k, v = ev
        for b in reads:
            if b.rs.get(k, 0) < v:
                b.rs[k] = v
        for b in writes:
            b.w = ev
            b.rs = {}

    def op(self, eng, fn, reads=(), writes=()):
        waits = self._deps(eng, reads, writes)
        self.cnt[eng] += 1
        ev = (eng, self.cnt[eng])
        self.ops[eng].append((waits, fn, ev, 1))
        self._post(ev, reads, writes)

    def dma(self, q, fn, reads=(), writes=()):
        waits = self._deps(q, reads, writes)
        i = self.drr[q]
        self.drr[q] = (i + 1) % self.NDMA[q]
        key = (q, i)
        cur = self.dcount[key]
        if cur > 0 and self.seen[q].get(key, 0) < cur:
            self.seen[q][key] = cur
            waits.append((key, cur))
        self.dcount[key] = cur + 16
        ev = (key, cur + 16)
        self.ops[q].append((waits, fn, ev, 16))
        self._post(ev, reads, writes)

    def barrier(self):
        for e in ENGS:
            waits = []
            for e2 in ENGS:
                if e2 == e:
                    continue
                v = self.cnt[e2]
                if v > 0 and self.seen[e].get(e2, 0) < v:
                    self.seen[e][e2] = v
                    waits.append((e2, v))
            for key, v in self.dcount.items():
                if v > 0 and self.seen[e].get(key, 0) < v:
                    self.seen[e][key] = v
                    waits.append((key, v))
            if waits:
                self.ops[e].append((waits, None, None, 0))

    def emit(self):
        import contextlib

        nc = self.nc
        sems = {}
        with contextlib.ExitStack() as st:
            for e in ENGS:
                sems[e] = st.enter_context(nc.semaphore(f"s_{e}"))
            for key in self.dcount:
                sems[key] = st.enter_context(nc.semaphore(f"d_{key[0]}{key[1]}"))
            self.barrier()
            block = st.enter_context(nc.Block())
            ops = self.ops

            def replay(name, e):
                for waits, fn, ev, inc in ops[name]:
                    for k, v in waits:
                        e.wait_ge(sems[k], v)
                    if fn is not None:
                        fn(e).then_inc(sems[ev[0]], inc)

            @block.tensor
            def _(e):
                replay("pe", e)

            @block.scalar
            def _(e):
                replay("act", e)

            @block.vector
            def _(e):
                replay("dve", e)

            @block.gpsimd
            def _(e):
                replay("pool", e)

            @block.sync
            def _(e):
                replay("sp", e)


def build(nc, S, dump_names=()):
    NCH = S // 512
    NSL = NCH // 4
    NSC = S // 2048
    NSB = S // 64
    P = Prog(nc)

    def din(name, shape, dt=F32):
        return nc.dram_tensor(name, list(shape), dt, kind="ExternalInput").ap()

    def dscr(name, shape, dt):
        return nc.dram_tensor(name, list(shape), dt, kind="Internal").ap()

    xf = din("xf", [S, D])
    xq = din("xq", [NSL, 1024, D])
    posf = din("posf", [S], I32)
    posq = din("posq", [NSL * 1024], I32)
    cvec = din("cvec", [16, 128])
    hv = din("hv", [128, NSL])
    ada_w = din("ada_w", [D, 6 * D])
    ada_b = din("ada_b", [6 * D])
    norm_mix = din("norm_mix", [D])
    norm_ffn = din("norm_ffn", [D])
    w_in = din("w_in", [D, INW])
    conv_w = din("conv_w", [3, 1024])
    pe_k = din("cmp_pe_k", [32, 128])
    k_w1 = din("cmp_k_w1", [32, 128, 256])
    k_b1 = din("cmp_k_b1", [256])
    k_w2 = din("cmp_k_w2", [256, 128])
    pe_v = din("cmp_pe_v", [32, 128])
    v_w1 = din("cmp_v_w1", [32, 128, 256])
    v_b1 = din("cmp_v_b1", [256])
    v_w2 = din("cmp_v_w2", [256, 128])
    q_norm = din("q_norm", [128])
    k_norm = din("k_norm", [3, 128])
    on_conv = din("out_norm_conv", [1024])
    on_attn = din("out_norm_attn", [1024])
    w_out = din("w_out", [D, D])
    ffn_w1 = din("ffn_w1", [D, DFF])
    ffn_w3 = din("ffn_w3", [D, DFF])
    ffn_w2 = din("ffn_w2", [DFF, D])
    c_identb = din("c_identb", [128, 128], BF16)
    c_identf = din("c_identf", [128, 2, 128])
    c_ones = din("c_ones", [128, 2, 128], BF16)
    c_rot = din("c_rot", [128, 128], BF16)
    c_invf = din("c_invf", [128, 1])
    c_ovl = din("c_ovl", [128, NSC, NSB + 1], BF16)
    c_E = din("c_E", [128, 64, 128], BF16)
    c_wm0 = din("c_wm0", [128, 8, 512], BF16)
    c_wmg = din("c_wmg", [128, 8, 512], BF16)
    c_slm = din("c_slm", [16, 128, 512], BF16)
    c_cm = din("c_cm", [128, 2, 512], BF16)
    c_selb = din("c_selb", [128, NSL, 4, NSB])
    out = nc.dram_tensor("out", [NSL * 512, D], F32, kind="ExternalOutput").ap()

    wi_bf = dscr("wi_bf", [D, INW], BF16)
    wo_bf = dscr("wo_bf", [D, D], BF16)
    w1_bf = dscr("w1_bf", [D, DFF], BF16)
    w3_bf = dscr("w3_bf", [D, DFF], BF16)
    w2_bf = dscr("w2_bf", [DFF, D], BF16)
    ck1_bf = dscr("ck1_bf", [32, 128, 256], BF16)
    cv1_bf = dscr("cv1_bf", [32, 128, 256], BF16)
    ksT_d = dscr("ksT_d", [2, 128, S], BF16)
    vs_d = dscr("vs_d", [S, 256], BF16)
    ada_d = dscr("ada_d", [6 * D], F32)
    gd = dscr("gd", [24, 512], F32)

    b_wi = bufs("wi", 6)
    b_wo, b_w1, b_w3, b_w2, b_ck1, b_cv1 = [Buf(n) for n in "wo w1 w3 w2 ck1 cv1".split()]
    b_ks = bufs("ksd", NCH)
    b_vs = bufs("vsd", NCH)
    b_adad = Buf("adad")
    b_gd = Buf("gd")
    b_out = Buf("out")

    PS2 = [nc.alloc_psum_tensor(f"ps2_{i}", [128, 1024], F32).ap() for i in range(4)]
    banks = [PS2[i // 2][:, (i % 2) * 512:(i % 2) * 512 + 512] for i in range(8)]

    def bank_bf(i):
        return PS2[i // 2].bitcast(BF16)[:, (i % 2) * 1024:(i % 2) * 1024 + 1024]
    bk = bufs("bank", 8)

    def act(out_, in_, func, R, W, **kw):
        P.op("act", lambda e: e.activation(out=out_, in_=in_, func=func, **kw), R, W)

    def mm(out_, lhsT, rhs, start, stop, R, W):
        P.op("pe", lambda e: e.matmul(out_, lhsT=lhsT, rhs=rhs, start=start, stop=stop), R, W)

    def tr(out_, in_, ident, R, W):
        P.op("pe", lambda e: e.transpose(out=out_, in_=in_, identity=ident), R, W)

    def tt(eng, out_, a, b, op, R, W):
        P.op(eng, lambda e: e.tensor_tensor(out=out_, in0=a, in1=b, op=op), R, W)

    def ts(eng, out_, a, s1, op0, R, W, s2=None, op1=None):
        if op1 is None:
            P.op(eng, lambda e: e.tensor_scalar(out=out_, in0=a, scalar1=s1, scalar2=None, op0=op0), R, W)
        else:
            P.op(eng, lambda e: e.tensor_scalar(out=out_, in0=a, scalar1=s1, scalar2=s2, op0=op0, op1=op1), R, W)

    def stt(out_, a, s, b, op0, op1, R, W):
        P.op("dve", lambda e: e.scalar_tensor_tensor(out=out_, in0=a, scalar=s, in1=b, op0=op0, op1=op1), R, W)

    def cp(eng, out_, in_, R, W):
        if eng == "act":
            P.op(eng, lambda e: e.activation(out=out_, in_=in_, func=AF.Copy), R, W)
        else:
            P.op(eng, lambda e: e.tensor_copy(out=out_, in_=in_), R, W)

    def rcp(out_, in_, R, W):
        P.op("dve", lambda e: e.reciprocal(out=out_, in_=in_), R, W)

    def mset(eng, ap, val, W):
        P.op(eng, lambda e: e.memset(ap, val), (), W)

    def ld(out_, in_, R, W, q="sp", slow=False, **kw):
        if slow:
            P.dma(q, lambda e: e.dma_start(out=out_, in_=in_, allow_slow_non_contiguous=True, **kw), R, W)
        else:
            P.dma(q, lambda e: e.dma_start(out=out_, in_=in_, **kw), R, W)

    dumps = {}
    dump_set = set(dump_names)

    def dump(name, ap, b):
        if name not in dump_set:
            return
        d = nc.dram_tensor("dbg_" + name, list(ap.shape), ap.dtype, kind="ExternalOutput").ap()
        dumps[name] = d
        ld(d, ap, [b], [Buf("dbg")], q="sp")


    identb = P.sb([128, 128], BF16, "identb")
    identf2 = P.sb([128, 2, 128], F32, "identf")
    ones2 = P.sb([128, 2, 128], BF16, "ones2")
    rot = P.sb([128, 128], BF16, "rot")
    invf = P.sb([128, 1], F32, "invf")
    ovl = P.sb([128, NSC, NSB + 1], BF16, "ovl")
    Emat = P.sb([128, 64, 128], BF16, "Emat")
    hvt = P.sb([128, NSL], F32, "hvt")
    b_const = Buf("const")
    for t_, d_ in ((identb, c_identb), (identf2, c_identf), (ones2, c_ones), (rot, c_rot), (invf, c_invf),
                   (ovl, c_ovl), (Emat, c_E), (hvt, hv)):
        ld(t_, d_, [], [b_const])
    onesb = ones2[:, 0, :]
    identf = identf2[:, 0, :]
    onesf = identf2[:, 1, :]

    colp = P.sb([128, 160], F32, "colp")
    b_colp = Buf("colp")
    nmT = colp[:, 0:16]
    nfT = colp[:, 16:32]
    qn = colp[:, 32:33]
    kn = colp[:, 33:36]
    cw = colp[:, 36:60].rearrange("p (g k) -> p g k", k=3)
    onc = colp[:, 60:68]
    ona = colp[:, 68:76]
    b1c = colp[:, 76:80]
    A1T = colp[:, 80:96]
    shaT = colp[:, 96:112]
    A2T = colp[:, 112:128]
    shfT = colp[:, 128:144]
    b1eff = colp[:, 144:148]
    ld(nmT, norm_mix.rearrange("(c p) -> p c", p=128), [], [b_colp], slow=True)
    ld(nfT, norm_ffn.rearrange("(c p) -> p c", p=128), [], [b_colp], slow=True)
    ld(qn, q_norm.rearrange("(p o) -> p o", o=1), [], [b_colp], slow=True)
    ld(kn, k_norm.rearrange("a d -> d a"), [], [b_colp], slow=True)
    for k_ in range(3):
        ld(colp[:, 36 + k_:60:3], conv_w[k_].rearrange("(g p) -> p g", p=128), [], [b_colp], slow=True)
    ld(onc, on_conv.rearrange("(g p) -> p g", p=128), [], [b_colp], slow=True)
    ld(ona, on_attn.rearrange("(g p) -> p g", p=128), [], [b_colp], slow=True)
    ld(b1c[:, 0:2], k_b1.rearrange("(c p) -> p c", p=128), [], [b_colp], slow=True)
    ld(b1c[:, 2:4], v_b1.rearrange("(c p) -> p c", p=128), [], [b_colp], slow=True)

    kcmpT = P.sb([128, 2, 128 * NSC], BF16, "kcmpT")
    vcmp = P.sb([128, NSC, 2, 128], BF16, "vcmp")
    b_kcmp = Buf("kcmp")
    b_vcmp = Buf("vcmp")
    w2c = P.sb([128, 2, 2, 128], BF16, "w2c")
    b_w2c = Buf("w2c")
    ld(w2c[:, 0, :, :], k_w2.rearrange("(c p) d -> p c d", p=128), [], [b_w2c], q="pool")
    ld(w2c[:, 1, :, :], v_w2.rearrange("(c p) d -> p c d", p=128), [], [b_w2c], q="pool")

    NTF, NTB = 6, 4
    TF = [P.sb([128, 512], F32, f"TF{i}") for i in range(NTF)]
    bTF = bufs("TF", NTF)
    TB = [P.sb([128, 512], BF16, f"TB{i}") for i in range(NTB)]
    bTB = bufs("TB", NTB)
    ssq = [P.sb([128, 4], F32, f"ssq{i}") for i in range(4)]
    bssq = bufs("ssq", 4)
    ssq_i = [0]
    junk = P.sb([128, 2048], BF16, "junk")
    b_junk = Buf("junk")

    conv_list = []

    def plan_conv(src, dst, rows, c0, c1, b, rstep=256):
        for r0 in range(0, rows, rstep):
            r1 = min(rows, r0 + rstep)
            conv_list.append((src[r0:r1, c0:c1], dst[r0:r1, c0:c1], b))

    plan_conv(w_in, wi_bf, D, 4096, 5120, b_wi[4])
    kw1f = k_w1.rearrange("l d h -> (l d) h")
    vw1f = v_w1.rearrange("l d h -> (l d) h")
    plan_conv(kw1f, ck1_bf.rearrange("l d h -> (l d) h"), 4096, 0, 256, b_ck1, 512)
    plan_conv(vw1f, cv1_bf.rearrange("l d h -> (l d) h"), 4096, 0, 256, b_cv1, 512)
    plan_conv(w_in, wi_bf, D, 5120, INW, b_wi[5])
    for gi in (1, 2, 0, 3):
        plan_conv(w_in, wi_bf, D, gi * 1024, gi * 1024 + 1024, b_wi[gi])
    for c0 in (0, 1024):
        plan_conv(w_out, wo_bf, D, c0, c0 + 1024, b_wo)
    for c0 in range(0, DFF, 1024):
        plan_conv(ffn_w1, w1_bf, D, c0, min(DFF, c0 + 1024), b_w1)
        plan_conv(ffn_w3, w3_bf, D, c0, min(DFF, c0 + 1024), b_w3)
    for c0 in (0, 1024):
        plan_conv(ffn_w2, w2_bf, DFF, c0, c0 + 1024, b_w2)
    conv_pos = [0]

    def do_conv(n):
        for _ in range(n):
            if conv_pos[0] >= len(conv_list):
                return
            s_, d_, b_ = conv_list[conv_pos[0]]
            conv_pos[0] += 1
            P.dma("pool", lambda e, s_=s_, d_=d_: e.dma_start(out=d_, in_=s_, max_dma_last_dim=4096), [], [Buf("cv")])

    do_conv(8 + 16)

    def rms_xn(x_ap, bx, xn_ap, bxn):
        i = ssq_i[0] % 4
        ssq_i[0] += 1
        s_, bs = ssq[i], bssq[i]
        P.op("dve", lambda e: e.scalar_tensor_tensor(out=junk, in0=x_ap, scalar=1.0, in1=x_ap, op0=ALU.mult, op1=ALU.mult,
                                                     accum_out=s_[:, 0:1]), [bx], [b_junk, bs])
        act(s_[:, 1:2], s_[:, 0:1], AF.Sqrt, [bs], [bs], scale=1.0 / D, bias=EPS)
        rcp(s_[:, 2:3], s_[:, 1:2], [bs], [bs])
        act(xn_ap, x_ap, AF.Copy, [bx, bs], [bxn], scale=s_[:, 2:3])

    def transposes(xn4, bxn4, hT, bhT, AT, shT, col0=0):
        for fc in range(16):
            b_ = fc % 4
            Bv = bank_bf(b_)
            for ti in range(4):
                tr(Bv[:, ti * 128:(ti + 1) * 128], xn4[:, ti, fc * 128:(fc + 1) * 128], identb,
                   [bxn4[ti], b_const], [bk[b_]])
            if fc % 2 == 0:
                act(hT[:, fc, col0:col0 + 512], Bv[:, 0:512], AF.Identity, [bk[b_], b_colp], [bhT[fc]],
                    scale=AT[:, fc:fc + 1], bias=shT[:, fc:fc + 1])
            else:
                ts("dve", hT[:, fc, col0:col0 + 512], Bv[:, 0:512], AT[:, fc:fc + 1], ALU.mult, [bk[b_], b_colp], [bhT[fc]],
                   s2=shT[:, fc:fc + 1], op1=ALU.add)

    TWO_PI = 2.0 * math.pi
    C1 = 6.28125
    C2 = TWO_PI - C1
    MAGIC = 12582912.0
    PI_LO = 3.1415925

    def rope_tables(posi, bposi, cosT, sinT, bcs, n):
        A, K, R, RS = TF[0][:, 0:n], TF[1][:, 0:n], TF[2][:, 0:n], TF[3][:, 0:n]
        bA, bK, bR, bRS = bTF[0], bTF[1], bTF[2], bTF[3]
        cp("dve", A, posi, [bposi], [bA])
        ts("dve", A, A, invf[:, 0:1], ALU.mult, [bA, b_const], [bA])
        ts("dve", K, A, 1.0 / TWO_PI, ALU.mult, [bA], [bK], s2=MAGIC, op1=ALU.add)
        ts("dve", K, K, -MAGIC, ALU.add, [bK], [bK])
        stt(R, K, -C1, A, ALU.mult, ALU.add, [bK, bA], [bR])
        stt(R, K, -C2, R, ALU.mult, ALU.add, [bK, bR], [bR])
        ts("dve", RS, R, -PI_LO, ALU.max, [bR], [bRS], s2=PI_LO, op1=ALU.min)
        act(sinT, RS, AF.Sin, [bRS], [bcs])
        ts("dve", K, R, math.pi / 2, ALU.add, [bR], [bK])
        ts("dve", A, K, math.pi, ALU.is_gt, [bK], [bA])
        stt(K, A, -TWO_PI, K, ALU.mult, ALU.add, [bA, bK], [bK])
        ts("dve", RS, K, -PI_LO, ALU.max, [bK], [bRS], s2=PI_LO, op1=ALU.min)
        act(cosT, RS, AF.Sin, [bRS], [bcs])

    def norm_rope_a(src, bsrc, gaincol, n, bankA):
        sq, xnb = TB[0][:, 0:n], TB[1][:, 0:n]
        rstd = TF[4][:, 0:n]
        act(sq, src, AF.Square, [bsrc], [bTB[0]])
        mm(banks[bankA][:, 0:n], onesb, sq, True, True, [bTB[0], b_const], [bk[bankA]])
        act(rstd, banks[bankA][:, 0:n], AF.Ln, [bk[bankA]], [bTF[4]], scale=1.0 / 128, bias=EPS)
        act(rstd, rstd, AF.Exp, [bTF[4]], [bTF[4]], scale=-0.5)
        stt(xnb, src, gaincol, rstd, ALU.mult, ALU.mult, [bsrc, bTF[4], b_colp], [bTB[1]])

    def norm_rope_b(cosT, sinT, bcs, out_, bout, n, bankB):
        xnb = TB[1][:, 0:n]
        ta, tb_ = TF[5][:, 0:n], TF[3][:, 0:n]
        mm(banks[bankB][:, 0:n], rot, xnb, True, True, [bTB[1], b_const], [bk[bankB]])
        tt("dve", ta, xnb, cosT, ALU.mult, [bTB[1], bcs], [bTF[5]])
        tt("dve", tb_, banks[bankB][:, 0:n], sinT, ALU.mult, [bk[bankB], bcs], [bTF[3]])
        tt("dve", out_, ta, tb_, ALU.add, [bTF[5], bTF[3]], [bout])

    def norm_rope(src, bsrc, gaincol, cosT, sinT, bcs, out_, bout, n, bankA, bankB):
        norm_rope_a(src, bsrc, gaincol, n, bankA)
        norm_rope_b(cosT, sinT, bcs, out_, bout, n, bankB)

    def run_jobs(jobs):
        n = len(jobs)
        if n:
            jobs[0][0]()
        for i in range(n):
            jobs[i][1]()
            if i + 1 < n:
                jobs[i + 1][0]()
            jobs[i][2]()
            if jobs[i][3] is not None:
                jobs[i][3]()

    P.push()
    cT = P.sb([128, 16], F32, "cT")
    c16 = P.sb([16, 128], F32, "c16")
    b_c = Buf("c")
    ld(c16, cvec, [], [b_c])
    tr(banks[0][:, 0:16], c16, identf[0:16, 0:16], [b_c, b_const], [bk[0]])
    act(cT, banks[0][:, 0:16], AF.Silu, [bk[0]], [b_c])
    adab = [P.sb([128, 16, 512], F32, f"adab{i}") for i in range(2)]
    b_adab = bufs("adab", 2)
    arow = [P.sb([1, 512], F32, f"arow{i}") for i in range(2)]
    b_arow = bufs("arow", 2)
    abrow = [P.sb([1, 512], F32, f"abrow{i}") for i in range(2)]
    b_abrow = bufs("abrow", 2)
    for bi in range(24):
        i = bi % 2
        ld(adab[i], ada_w[:, bi * 512:(bi + 1) * 512].rearrange("(c p) n -> p c n", p=128), [], [b_adab[i]])
        ld(abrow[i], ada_b[bi * 512:(bi + 1) * 512].rearrange("(o n) -> o n", o=1), [], [b_abrow[i]])
        bb = 2 + i
        for kc in range(16):
            mm(banks[bb][0:1, :], cT[:, kc:kc + 1], adab[i][:, kc, :], kc == 0, kc == 15, [b_c, b_adab[i]], [bk[bb]])
        tt("dve", arow[i], banks[bb][0:1, :], abrow[i], ALU.add, [bk[bb], b_abrow[i]], [b_arow[i]])
        ld(ada_d[bi * 512:(bi + 1) * 512].rearrange("(o n) -> o n", o=1), arow[i], [b_arow[i]], [b_adad], q="pool")
    adaT = P.sb([128, 96], F32, "adaT")
    b_adaT = Buf("adaT")
    for i6 in range(6):
        ld(adaT[:, i6 * 16:(i6 + 1) * 16], ada_d[i6 * D:(i6 + 1) * D].rearrange("(c p) -> p c", p=128),
           [b_adad], [b_adaT], slow=True)
    ts("dve", A1T, adaT[:, 16:32], 1.0, ALU.add, [b_adaT], [b_colp])
    tt("dve", A1T, A1T, nmT, ALU.mult, [b_colp], [b_colp])
    cp("dve", shaT, adaT[:, 0:16], [b_adaT], [b_colp])
    ts("dve", A2T, adaT[:, 64:80], 1.0, ALU.add, [b_adaT], [b_colp])
    tt("dve", A2T, A2T, nfT, ALU.mult, [b_colp], [b_colp])
    cp("dve", shfT, adaT[:, 48:64], [b_adaT], [b_colp])
    P.pop()
    P.barrier()

    P.push()
    xst = [P.sb([128, D], F32, f"xst{i}") for i in range(2)]
    b_xst = bufs("xst", 2)
    xn4s = [P.sb([128, 4, D], BF16, f"xn4_{i}") for i in range(2)]
    b_xn4s = [bufs(f"xn4_{i}_", 4) for i in range(2)]
    hT = P.sb([128, 16, 512], BF16, "hT")
    b_hT = bufs("hT", 16)
    wkv = P.sb([128, 16, 1024], BF16, "wkv")
    b_wkv = Buf("wkv")
    cbuf = P.sb([128, 4, 16 + 2048], BF16, "cbuf")
    b_cbuf = bufs("cbuf", 4)
    posis = [P.sb([128, 512], I32, f"posi{i}") for i in range(2)]
    b_posis = bufs("posi", 2)
    cosTs = [P.sb([128, 512], F32, f"cosT{i}") for i in range(2)]
    sinTs = [P.sb([128, 512], F32, f"sinT{i}") for i in range(2)]
    b_css = bufs("cs", 2)
    ccmp = P.sb([128, 128], F32, "ccmp")
    scmp = P.sb([128, 128], F32, "scmp")
    b_ccs = Buf("ccs")
    kst = [P.sb([128, 512], BF16, f"kst{i}") for i in range(2)]
    b_kst = bufs("kst", 2)
    vst = [P.sb([128, 4, 256], BF16, f"vst{i}") for i in range(2)]
    b_vst = bufs("vst", 2)
    w1t = P.sb([128, 32, 256], BF16, "w1t")
    b_w1t = Buf("w1t")
    pet = P.sb([32, 2, 128], F32, "pet")
    peT = P.sb([128, 2, 32], BF16, "peT")
    b_pe = Buf("pe")
    hid = P.sb([128, 2, 128], BF16, "hid")
    b_hid = Buf("hid")

    for idx in range(4):
        mset("pool", cbuf[:, idx, 0:16], 0.0, [b_cbuf[idx]])
    P.barrier()
    ld(wkv, wi_bf[:, 4096:5120].rearrange("(c p) n -> p c n", p=128), [], [b_wkv])

    ld(pet[:, 0, :], pe_k, [], [b_pe])
    ld(pet[:, 1, :], pe_v, [], [b_pe])
    for kv in range(2):
        tr(banks[0][:, kv * 32:(kv + 1) * 32], pet[:, kv, :], identf[0:32, 0:32], [b_pe, b_const], [bk[0]])
    cp("dve", peT, banks[0][:, 0:64].rearrange("p (a l) -> p a l", a=2), [bk[0]], [b_pe])
    for kv in range(2):
        ld(w1t, (ck1_bf if kv == 0 else cv1_bf).rearrange("l d h -> d l h"), [], [b_w1t])
        for hc in range(2):
            for l in range(32):
                mm(banks[1][:, kv * 2 + hc:kv * 2 + hc + 1], w1t[:, l, hc * 128:(hc + 1) * 128], peT[:, kv, l:l + 1],
                   l == 0, l == 31, [b_w1t, b_pe], [bk[1]])
    tt("dve", b1eff, banks[1][:, 0:4], b1c, ALU.add, [bk[1], b_colp], [b_colp])

    def compress(Q):
        for kv in range(2):
            ld(w1t, (ck1_bf if kv == 0 else cv1_bf).rearrange("l d h -> d l h"), [], [b_w1t])
            for g in range(2):
                idx = kv * 2 + g
                B_ = 2 + g
                for hc in range(2):
                    for l in range(32):
                        mm(banks[B_][:, hc * 128:(hc + 1) * 128], w1t[:, l, hc * 128:(hc + 1) * 128],
                           cbuf[:, idx, l:l + 2033:16], l == 0, l == 31, [b_w1t, b_cbuf[idx]], [bk[B_]])
                xh, x2, inner, sg = TF[0][:, 0:256], TF[1][:, 0:256], TF[2][:, 0:256], TF[3][:, 0:256]
                for hc in range(2):
                    act(xh[:, hc * 128:(hc + 1) * 128], banks[B_][:, hc * 128:(hc + 1) * 128], AF.Identity,
                        [bk[B_], b_colp], [bTF[0]], bias=b1eff[:, kv * 2 + hc:kv * 2 + hc + 1])
                tt("dve", x2, xh, xh, ALU.mult, [bTF[0]], [bTF[1]])
                ts("dve", x2, x2, 0.044715, ALU.mult, [bTF[1]], [bTF[1]], s2=1.0, op1=ALU.add)
                tt("dve", inner, x2, xh, ALU.mult, [bTF[1], bTF[0]], [bTF[2]])
                act(sg, inner, AF.Sigmoid, [bTF[2]], [bTF[3]], scale=1.5957691216057308)
                tt("dve", hid.rearrange("p a n -> p (a n)"), sg, xh, ALU.mult, [bTF[3], bTF[0]], [b_hid])
                if kv == 0:
                    for hc in range(2):
                        mm(banks[4][:, 0:128], w2c[:, 0, hc, :], hid[:, hc, :], hc == 0, hc == 1, [b_w2c, b_hid], [bk[4]])
                    norm_rope(banks[4][:, 0:128], bk[4], kn[:, 0:1], ccmp, scmp, b_ccs,
                              kcmpT[:, g, Q * 128:(Q + 1) * 128], b_kcmp, 128, 5, 6)
                else:
                    for hc in range(2):
                        mm(banks[4][:, 0:128], hid[:, hc, :], w2c[:, 1, hc, :], hc == 0, hc == 1, [b_w2c, b_hid], [bk[4]])
                    cp("act", vcmp[:, Q, g, :], banks[4][:, 0:128], [bk[4]], [b_vcmp])

    def a_load(p, ti):
        i = ti % 2
        ld(xst[i], xf[p * 512 + ti * 128:p * 512 + (ti + 1) * 128, :], [], [b_xst[i]])

    def a_rms(p, ti):
        q_ = p % 2
        i = ti % 2
        rms_xn(xst[i], b_xst[i], xn4s[q_][:, ti, :], b_xn4s[q_][ti])

    def a_tables(p):
        q_ = p % 2
        ld(posis[q_], posf[p * 512:(p + 1) * 512].partition_broadcast(128), [], [b_posis[q_]])
        rope_tables(posis[q_], b_posis[q_], cosTs[q_], sinTs[q_], b_css[q_], 512)

    for ti in range(4):
        if ti < 2:
            a_load(0, ti)
    for ti in range(4):
        a_rms(0, ti)
        if ti + 2 < 4:
            a_load(0, ti + 2)
    a_tables(0)
    for p in range(NCH):
        do_conv(6)
        nxt = p + 1 < NCH
        if nxt:
            a_load(p + 1, 0)
            a_load(p + 1, 1)
        xn4, b_xn4 = xn4s[p % 2], b_xn4s[p % 2]
        cosT, sinT, b_cs = cosTs[p % 2], sinTs[p % 2], b_css[p % 2]
        pq = p % 4
        cp("pool", ccmp[:, 32 * pq:32 * pq + 32], cosT[:, 15:512:16], [b_cs], [b_ccs])
        cp("pool", scmp[:, 32 * pq:32 * pq + 32], sinT[:, 15:512:16], [b_cs], [b_ccs])
        transposes(xn4, b_xn4, hT, b_hT, A1T, shaT)

        def kcvc(idx):
            B_ = 2 + idx % 2
            for kc in range(16):
                mm(banks[B_], wkv[:, kc, idx * 128:(idx + 1) * 128], hT[:, kc, :], kc == 0, kc == 15,
                   [b_wkv, b_hT[kc]], [bk[B_]])
            cp("act", cbuf[:, idx, 16 + 512 * pq:16 + 512 * pq + 512], banks[B_], [bk[B_]], [b_cbuf[idx]])

        def kslproj(g):
            B_ = 4 + g
            for kc in range(16):
                mm(banks[B_], wkv[:, kc, 512 + g * 128:512 + (g + 1) * 128], hT[:, kc, :], kc == 0, kc == 15,
                   [b_wkv, b_hT[kc]], [bk[B_]])

        vi = p % 2

        def vsl(ti):
            B_ = 2 + ti % 2
            for kc in range(16):
                mm(banks[B_][:, 0:256], hT[:, kc, ti * 128:(ti + 1) * 128], wkv[:, kc, 768:1024], kc == 0, kc == 15,
                   [b_wkv, b_hT[kc]], [bk[B_]])
            cp("act", vst[vi][:, ti, :], banks[B_][:, 0:256], [bk[B_]], [b_vst[vi]])

        kslproj(0)
        kslproj(1)
        kcvc(0)
        kcvc(1)
        if nxt:
            a_rms(p + 1, 0)
            a_load(p + 1, 2)
        norm_rope_a(banks[4], bk[4], kn[:, 1:2], 512, 6)
        kcvc(2)
        if nxt:
            a_rms(p + 1, 1)
            a_load(p + 1, 3)
        kcvc(3)
        norm_rope_b(cosT, sinT, b_cs, kst[0], b_kst[0], 512, 7)
        ld(ksT_d[0, :, p * 512:(p + 1) * 512], kst[0], [b_kst[0]], [b_ks[p]])
        if nxt:
            a_rms(p + 1, 2)
        norm_rope_a(banks[5], bk[5], kn[:, 1:2], 512, 6)
        vsl(0)
        vsl(1)
        norm_rope_b(cosT, sinT, b_cs, kst[1], b_kst[1], 512, 7)
        ld(ksT_d[1, :, p * 512:(p + 1) * 512], kst[1], [b_kst[1]], [b_ks[p]])
        if nxt:
            a_rms(p + 1, 3)
            a_tables(p + 1)
        vsl(2)
        vsl(3)
        ld(vs_d[p * 512:(p + 1) * 512, :].rearrange("(t p) n -> p t n", p=128), vst[vi], [b_vst[vi]], [b_vs[p]])
        if pq == 3:
            compress(p // 4)
            for idx in range(4):
                cp("pool", cbuf[:, idx, 0:16], cbuf[:, idx, 2048:2064], [b_cbuf[idx]], [b_cbuf[idx]])
    mset("dve", vcmp[0:1, 0, :, :], 0.0, [b_vcmp])
    dump("kcmpT", kcmpT, b_kcmp)
    dump("vcmp", vcmp, b_vcmp)
    do_conv(len(conv_list))
    P.pop()
    P.barrier()

    xown = P.sb([128, 4, D], F32, "xown")
    b_xown = bufs("xown", 4)
    for s in range(NSL):
        P.push()
        qT = P.sb([128, 8, 512], BF16, "qT")
        b_qT = bufs("qT", 8)
        ycT = P.sb([128, 8, 512], BF16, "ycT")
        b_ycT = bufs("ycT", 8)
        kwT = P.sb([128, 2, 1024], BF16, "kwT")
        b_kwT = Buf("kwT")
        vwt = P.sb([128, 8, 256], BF16, "vwt")
        b_vwt = Buf("vwt")

        P.push()
        xst = P.sb([128, D], F32, "xst")
        b_xst1 = Buf("xst1")
        xn4 = P.sb([128, 4, D], BF16, "xn4")
        b_xn4 = bufs("xn4", 4)
        hT = P.sb([128, 16, 512], BF16, "hT")
        b_hT = bufs("hT", 16)
        hT2 = P.sb([128, 16, 2], BF16, "hT2")
        b_hT2 = Buf("hT2")
        wtail = P.sb([128, 16, 536], BF16, "wtail")
        b_wtail = Buf("wtail")
        NWB = 3
        wblk = [P.sb([128, 16, 256], BF16, f"wblk{i}") for i in range(NWB)]
        b_wblk = bufs("wblk", NWB)
        wb_i = [0]
        posi = P.sb([128, 1024], I32, "posi")
        b_posi = Buf("posi")
        cosq = P.sb([128, 1024], F32, "cosq")
        sinq = P.sb([128, 1024], F32, "sinq")
        b_cs = Buf("cs")
        gT = P.sb([24, 512], F32, "gT")
        b_gT = Buf("gT")
        uh = P.sb([128, 4], F32, "uh")
        b_uh = Buf("uh")
        ubuf = P.sb([128, 514], F32, "ubuf")
        b_ubuf = Buf("ubuf")

        for ti in range(4):
            ld(xown[:, ti, :], xq[s, ti * 128:(ti + 1) * 128, :], [], [b_xown[ti]])
        ld(wtail, wi_bf[:, 5120:INW].rearrange("(c p) n -> p c n", p=128), [], [b_wtail])
        ld(posi, posq[s * 1024:(s + 1) * 1024].partition_broadcast(128), [], [b_posi])

        def load_wblk(c0):
            i = wb_i[0] % NWB
            wb_i[0] += 1
            ld(wblk[i], wi_bf[:, c0:c0 + 256].rearrange("(c p) n -> p c n", p=128), [], [b_wblk[i]])
            return wblk[i], b_wblk[i]

        for part in range(2):
            for ti in range(4):
                rms_xn(xown[:, ti, :], b_xown[ti], xn4[:, ti, :], b_xn4[ti])
            if part == 0:
                for ti in range(4):
                    r0 = 512 + ti * 128
                    ld(xown[:, ti, :], xq[s, r0:r0 + 128, :], [], [b_xown[ti]])
                for hh in range(2):
                    rope_tables(posi[:, hh * 512:(hh + 1) * 512], b_posi, cosq[:, hh * 512:(hh + 1) * 512],
                                sinq[:, hh * 512:(hh + 1) * 512], b_cs, 512)
            transposes(xn4, b_xn4, hT, b_hT, A1T, shaT)
            if part == 0:
                cp("pool", hT2, hT[:, :, 510:512], b_hT, [b_hT2])
            jobs = []
            for g in range(2):
                def proj(g=g):
                    B_ = 2 + g
                    for kc in range(16):
                        mm(banks[B_], wtail[:, kc, g * 128:(g + 1) * 128], hT[:, kc, :], kc == 0, kc == 15,
                           [b_wtail, b_hT[kc]], [bk[B_]])
                def nra(g=g):
                    norm_rope_a(banks[2 + g], bk[2 + g], kn[:, 2:3], 512, 6)
                def nrb(g=g, part=part):
                    norm_rope_b(cosq[:, part * 512:(part + 1) * 512], sinq[:, part * 512:(part + 1) * 512], b_cs,
                                kwT[:, g, part * 512:(part + 1) * 512], b_kwT, 512, 7)
                jobs.append((proj, nra, nrb, None))
            run_jobs(jobs)
            for ti in range(4):
                B_ = 4 + ti % 2
                for kc in range(16):
                    mm(banks[B_][:, 0:256], hT[:, kc, ti * 128:(ti + 1) * 128], wtail[:, kc, 256:512], kc == 0, kc == 15,
                       [b_wtail, b_hT[kc]], [bk[B_]])
                cp("act", vwt[:, part * 4 + ti, :], banks[B_][:, 0:256], [bk[B_]], [b_vwt])

        for kc in range(16):
            mm(banks[2][0:24, :], wtail[:, kc, 512:536], hT[:, kc, :], kc == 0, kc == 15, [b_wtail, b_hT[kc]], [bk[2]])
        act(gT, banks[2][0:24, :], AF.Sigmoid, [bk[2]], [b_gT])
        ld(gd, gT, [b_gT], [b_gd], q="pool")

        for cgp in range(4):
            wcc, bwcc = load_wblk(1024 + cgp * 256)
            wch, bwch = load_wblk(2048 + cgp * 256)
            wcb, bwcb = load_wblk(cgp * 256)
            for c2 in range(2):
                cg = cgp * 2 + c2
                cs_ = slice(c2 * 128, (c2 + 1) * 128)
                Bcc, Bch, Bcb = (2, 3, 4) if cg % 2 == 0 else (0, 1, 7)
                for kc in range(16):
                    mm(banks[Bcc], wcc[:, kc, cs_], hT[:, kc, :], kc == 0, kc == 15, [bwcc, b_hT[kc]], [bk[Bcc]])
                for kc in range(16):
                    mm(banks[5][:, 0:2], wcc[:, kc, cs_], hT2[:, kc, :], kc == 0, kc == 15, [bwcc, b_hT2], [bk[5]])
                for kc in range(16):
                    mm(banks[Bch], wch[:, kc, cs_], hT[:, kc, :], kc == 0, kc == 15, [bwch, b_hT[kc]], [bk[Bch]])
                for kc in range(16):
                    mm(banks[5][:, 2:4], wch[:, kc, cs_], hT2[:, kc, :], kc == 0, kc == 15, [bwch, b_hT2], [bk[5]])
                for kc in range(16):
                    mm(banks[Bcb], wcb[:, kc, cs_], hT[:, kc, :], kc == 0, kc == 15, [bwcb, b_hT[kc]], [bk[Bcb]])
                cp("act", uh, banks[5][:, 0:4], [bk[5]], [b_uh])
                tt("dve", ubuf[:, 0:2], uh[:, 0:2], uh[:, 2:4], ALU.mult, [b_uh], [b_ubuf])
                ts("dve", ubuf[:, 0:2], ubuf[:, 0:2], hvt[:, s:s + 1], ALU.mult, [b_ubuf, b_const], [b_ubuf])
                ccs = TF[0]
                cp("act", ccs, banks[Bcc], [bk[Bcc]], [bTF[0]])
                tt("dve", ubuf[:, 2:514], ccs, banks[Bch], ALU.mult, [bTF[0], bk[Bch]], [b_ubuf])
                y = TF[1]
                ts("dve", y, ubuf[:, 2:514], cw[:, cg, 2:3], ALU.mult, [b_ubuf, b_colp], [bTF[1]])
                stt(y, ubuf[:, 1:513], cw[:, cg, 1:2], y, ALU.mult, ALU.add, [b_ubuf, bTF[1], b_colp], [bTF[1]])
                stt(y, ubuf[:, 0:512], cw[:, cg, 0:1], y, ALU.mult, ALU.add, [b_ubuf, bTF[1], b_colp], [bTF[1]])
                tt("dve", y, y, banks[Bcb], ALU.mult, [bTF[1], bk[Bcb]], [bTF[1]])
                sq = TB[0]
                act(sq, y, AF.Square, [bTF[1]], [bTB[0]])
                mm(banks[6], onesb, sq, True, True, [bTB[0], b_const], [bk[6]])
                rstd = TF[4]
                act(rstd, banks[6], AF.Ln, [bk[6]], [bTF[4]], scale=1.0 / 128, bias=EPS)
                act(rstd, rstd, AF.Exp, [bTF[4]], [bTF[4]], scale=-0.5)
                stt(ycT[:, cg, :], y, onc[:, cg:cg + 1], rstd, ALU.mult, ALU.mult, [bTF[1], bTF[4], b_colp], [b_ycT[cg]])
        jobs = []
        qw = {}
        for h in range(8):
            def proj(h=h):
                qb, c2 = h // 2, h % 2
                if c2 == 0:
                    qw[qb] = load_wblk(3072 + qb * 256)
                wq, bwq = qw[qb]
                B_ = 2 + c2
                for kc in range(16):
                    mm(banks[B_], wq[:, kc, c2 * 128:(c2 + 1) * 128], hT[:, kc, :], kc == 0, kc == 15,
                       [bwq, b_hT[kc]], [bk[B_]])
            def nra(h=h):
                norm_rope_a(banks[2 + h % 2], bk[2 + h % 2], qn, 512, 6)
            def nrb(h=h):
                norm_rope_b(cosq[:, 512:1024], sinq[:, 512:1024], b_cs, qT[:, h, :], b_qT[h], 512, 7)
            jobs.append((proj, nra, nrb, None))
        run_jobs(jobs)
        if s == 0:
            dump("qT", qT, b_qT[7])
            dump("ycT", ycT, b_ycT[7])
            dump("kwT", kwT, b_kwT)
            dump("vwt", vwt, b_vwt)
            dump("gT", gT, b_gT)
        P.pop()
        P.barrier()

        P.push()
        yaT = None
        wm = P.sb([128, 8, 512], BF16, "wm")
        b_wm = Buf("wm")
        cm = P.sb([128, 2, 512], BF16, "cm")
        b_cm = Buf("cm")
        selb = P.sb([128, 4, NSB], F32, "selb")
        b_selb = Buf("selb")
        imp = P.sb([128, 4, NSB], F32, "imp")
        b_imp = bufs("imp", 4)
        NJP = max(1, NSB // 128)
        selT = P.sb([128, NJP, 512], BF16, "selT")
        b_selT = Buf("selT")
        oacc = P.sb([128, 8, 512], F32, "oacc")
        b_oacc = bufs("oacc", 8)
        pstore = P.sb([128, 2, NSC, 1024], BF16, "pstore")
        b_pst = [[Buf(f"pst{r}_{c}") for c in range(NSC)] for r in range(2)]
        NPT = 3
        Pt2 = [P.sb([128, 1024], BF16, f"Pt2_{i}") for i in range(NPT)]
        b_Pt2 = bufs("Pt2", NPT)
        LaccP = PS2[3]
        b_LaccP = [bk[6], bk[7]]
        slmt = [P.sb([128, 512], BF16, f"slmt{i}") for i in range(2)]
        b_slmt = bufs("slmt", 2)
        gb = [P.sb([128, 512], F32, f"gb{i}") for i in range(2)]
        b_gb = bufs("gb", 2)
        kstl = [P.sb([128, 512], BF16, f"kstl{i}") for i in range(2)]
        b_kstl = bufs("kstl", 2)
        vstl = [P.sb([128, 4, 128], BF16, f"vstl{i}") for i in range(2)]
        b_vstl = bufs("vstl", 2)
        score = P.sb([128, NSB], F32, "score")
        sc2 = P.sb([128, NSB], F32, "sc2")
        m8 = P.sb([128, 16], F32, "m8")
        selm = P.sb([128, NSB], BF16, "selm")
        b_tk = Buf("topk")
        rl = P.sb([128, 4], F32, "rl")
        b_rl = Buf("rl")

        ld(wm, c_wm0 if s == 0 else c_wmg, [], [b_wm])
        ld(cm, c_cm, [], [b_cm])
        ld(selb, c_selb[:, s, :, :], [], [b_selb])
        gb_i = [0]
        pipe = {"v": 0, "pend": None, "pt": 0}

        def pv_flush():
            pd = pipe["pend"]
            pipe["pend"] = None
            if pd is None:
                return
            pt_ap, pt_buf, first, last, vt_ap, vt_bufs = pd
            for r_ in range(2):
                ob = 4 + r_
                mm(banks[ob], vt_ap, pt_ap[:, r_ * 512:(r_ + 1) * 512], first, last, vt_bufs + [pt_buf], [bk[ob]])

        def unit(qk, pt_ap, pt_buf, first, last, vt_ap, vt_bufs):
            k = pipe["v"] % 2
            pipe["v"] += 1
            sb_ = [bk[2 * k], bk[2 * k + 1]]
            for r_ in range(2):
                n = len(qk[r_])
                for i_, (l_, rh_, Rb) in enumerate(qk[r_]):
                    mm(PS2[k][:, r_ * 512:(r_ + 1) * 512], l_, rh_, i_ == 0, i_ == n - 1, Rb, sb_)
            act(pt_ap, PS2[k], AF.Exp, sb_, [pt_buf], scale=SCALE)
            if first:
                cp("dve", LaccP, pt_ap, [pt_buf], b_LaccP)
            else:
                tt("dve", LaccP, LaccP, pt_ap, ALU.add, [pt_buf] + b_LaccP, b_LaccP)
            pv_flush()
            pipe["pend"] = (pt_ap, pt_buf, first, last, vt_ap, vt_bufs)

        def next_pt():
            i_ = pipe["pt"] % NPT
            pipe["pt"] += 1
            return Pt2[i_], b_Pt2[i_]

        def finalize(h, x):
            r_ = h % 2
            ob = 4 + r_
            lb = r_
            wt, tmp, lsb = TF[0], TF[1], TF[2]
            gi = gb_i[0] % 2
            gb_i[0] += 1
            ld(gb[gi], gd[h * 3 + x, :].partition_broadcast(128), [b_gd], [b_gb[gi]])
            cp("act", lsb, LaccP[:, r_ * 512:(r_ + 1) * 512], b_LaccP, [bTF[2]])
            mm(banks[lb], onesf, lsb, True, True, [b_const, bTF[2]], [bk[lb]])
            ts("dve", wt, banks[lb], 1e-30, ALU.max, [bk[lb]], [bTF[0]])
            rcp(wt, wt, [bTF[0]], [bTF[0]])
            tt("dve", wt, wt, gb[gi], ALU.mult, [bTF[0], b_gb[gi]], [bTF[0]])
            if x == 0:
                tt("dve", oacc[:, h, :], banks[ob], wt, ALU.mult, [bk[ob], bTF[0]], [b_oacc[h]])
            else:
                tt("dve", tmp, banks[ob], wt, ALU.mult, [bk[ob], bTF[0]], [bTF[1]])
                tt("dve", oacc[:, h, :], oacc[:, h, :], tmp, ALU.add, [bTF[1], b_oacc[h]], [b_oacc[h]])

        for g in range(2):
            for hp in range(2):
                for c in range(s + 1):
                    qk = []
                    for r in range(2):
                        h = 4 * g + 2 * hp + r
                        lst = [(kcmpT[:, g, c * 128:(c + 1) * 128], qT[:, h, :], [b_kcmp, b_qT[h]])]
                        if c == 0:
                            lst.append((identb, cm[:, 1, :], [b_const, b_cm]))
                        if c == s:
                            lst.append((identb, cm[:, 0, :], [b_const, b_cm]))
                        qk.append(lst)
                    unit(qk, pstore[:, hp, c, :], b_pst[hp][c], c == 0, c == s, vcmp[:, c, g, :], [b_vcmp])
                pv_flush()
                for r in range(2):
                    h = 4 * g + 2 * hp + r
                    finalize(h, 0)
                    for tb in range(4):
                        for c in range(s + 1):
                            mm(banks[2][:, 0:NSB + 1], pstore[:, hp, c, r * 512 + tb * 128:r * 512 + (tb + 1) * 128], ovl[:, c, :],
                               c == 0, c == s, [b_pst[hp][c], b_const], [bk[2]])
                        ts("dve", rl[:, 0:1], banks[2][:, NSB:NSB + 1], 1e-30, ALU.max, [bk[2]], [b_rl])
                        rcp(rl[:, 1:2], rl[:, 0:1], [b_rl], [b_rl])
                        if hp == 0 and r == 0:
                            ts("dve", imp[:, tb, :], banks[2][:, 0:NSB], rl[:, 1:2], ALU.mult, [bk[2], b_rl], [b_imp[tb]])
                        else:
                            stt(imp[:, tb, :], banks[2][:, 0:NSB], rl[:, 1:2], imp[:, tb, :], ALU.mult, ALU.add,
                                [bk[2], b_rl, b_imp[tb]], [b_imp[tb]])
            for tb in range(4):
                tt("dve", score, imp[:, tb, :], selb[:, tb, :], ALU.add, [b_imp[tb], b_selb], [b_tk])
                P.op("dve", lambda e, m8=m8, score=score: e.max(out=m8[:, 0:8], in_=score), [b_tk], [b_tk])
                P.op("dve", lambda e, m8=m8, score=score, sc2=sc2: e.match_replace(
                    out=sc2, in_to_replace=m8[:, 0:8], in_values=score, imm_value=-3.0e38), [b_tk], [b_tk])
                P.op("dve", lambda e, m8=m8, sc2=sc2: e.max(out=m8[:, 8:16], in_=sc2), [b_tk], [b_tk])
                ts("dve", sc2, score, m8[:, 15:16], ALU.is_ge, [b_tk], [b_tk])
                ts("dve", selm, sc2, -1.0, ALU.add, [b_tk], [b_tk], s2=30000.0, op1=ALU.mult)
                Bv = bank_bf(3)
                w_ = min(128, NSB)
                for jp in range(NJP):
                    tr(Bv[0:w_, jp * 128:jp * 128 + 128], selm[:, jp * 128:jp * 128 + w_], identb, [b_tk, b_const], [bk[3]])
                for jp in range(NJP):
                    cp("act", selT[0:w_, jp, tb * 128:(tb + 1) * 128], Bv[0:w_, jp * 128:jp * 128 + 128], [bk[3]], [b_selT])
            if s == NSL - 1 and g == 0:
                dump("imp", imp, b_imp[3])
                dump("selT", selT, b_selT)
            NKT = 16 * s + 16
            for hp in range(2):
                for kt4 in range(NKT // 4):
                    li = kt4 % 2
                    ld(kstl[li], ksT_d[g, :, kt4 * 512:(kt4 + 1) * 512], [b_ks[kt4]], [b_kstl[li]])
                    ld(vstl[li], vs_d[kt4 * 512:(kt4 + 1) * 512, g * 128:(g + 1) * 128].rearrange("(t p) d -> p t d", p=128),
                       [b_vs[kt4]], [b_vstl[li]])
                    for k4 in range(4):
                        kt = kt4 * 4 + k4
                        j0 = 2 * kt
                        jp = j0 // 128
                        KE = min(128, NSB)
                        em = (Emat[0:KE, kt % 64, :], selT[0:KE, jp, :], [b_const, b_selT])
                        diag = kt >= 16 * s
                        if diag:
                            rr = kt - 16 * s
                            di = rr % 2
                            ld(slmt[di], c_slm[rr], [], [b_slmt[di]])
                        qk = []
                        for r in range(2):
                            h = 4 * g + 2 * hp + r
                            lst = [(kstl[li][:, k4 * 128:(k4 + 1) * 128], qT[:, h, :], [b_kstl[li], b_qT[h]]), em]
                            if diag:
                                lst.append((identb, slmt[di], [b_const, b_slmt[di]]))
                            qk.append(lst)
                        pt_ap, pt_buf = next_pt()
                        unit(qk, pt_ap, pt_buf, kt == 0, kt == NKT - 1, vstl[li][:, k4, :], [b_vstl[li]])
                pv_flush()
                for r in range(2):
                    finalize(4 * g + 2 * hp + r, 1)
            for hp in range(2):
                for kt in range(8):
                    qk = []
                    for r in range(2):
                        h = 4 * g + 2 * hp + r
                        qk.append([(kwT[:, g, kt * 128:(kt + 1) * 128], qT[:, h, :], [b_kwT, b_qT[h]]),
                                   (identb, wm[:, kt, :], [b_const, b_wm])])
                    pt_ap, pt_buf = next_pt()
                    unit(qk, pt_ap, pt_buf, kt == 0, kt == 7, vwt[:, kt, g * 128:(g + 1) * 128], [b_vwt])
                pv_flush()
                for r in range(2):
                    finalize(4 * g + 2 * hp + r, 2)
        if s == NSL - 1:
            dump("oacc", oacc, b_oacc[7])
        XB_ = 2
        yaT = qT
        b_yaT = b_qT
        for h in range(8):
            sq = TB[0]
            act(sq, oacc[:, h, :], AF.Square, [b_oacc[h]], [bTB[0]])
            mm(banks[XB_], onesb, sq, True, True, [bTB[0], b_const], [bk[XB_]])
            rstd = TF[4]
            act(rstd, banks[XB_], AF.Ln, [bk[XB_]], [bTF[4]], scale=1.0 / 128, bias=EPS)
            act(rstd, rstd, AF.Exp, [bTF[4]], [bTF[4]], scale=-0.5)
            stt(yaT[:, h, :], oacc[:, h, :], ona[:, h:h + 1], rstd, ALU.mult, ALU.mult, [b_oacc[h], bTF[4], b_colp], [b_yaT[h]])
        P.pop()
        P.barrier()

        P.push()
        wob = [P.sb([128, 16, 512], BF16, f"wob{i}") for i in range(2)]
        b_wob = bufs("wob", 2)
        gab = [P.sb([128, 512], F32, f"gab{i}") for i in range(2)]
        b_gab = bufs("gab", 2)
        for oc in range(4):
            i = oc % 2
            ld(wob[i], wo_bf[:, oc * 512:(oc + 1) * 512].rearrange("(c p) n -> p c n", p=128), [], [b_wob[i]])
            ld(gab[i], ada_d[2 * D + oc * 512:2 * D + (oc + 1) * 512].partition_broadcast(128), [], [b_gab[i]])
            for tb in range(4):
                B_ = tb % 4
                for mc in range(16):
                    src_, bsrc = (ycT[:, mc, :], b_ycT[mc]) if mc < 8 else (yaT[:, mc - 8, :], b_yaT[mc - 8])
                    mm(banks[B_], src_[:, tb * 128:(tb + 1) * 128], wob[i][:, mc, :], mc == 0, mc == 15,
                       [bsrc, b_wob[i]], [bk[B_]])
                tmp = TF[tb % 2]
                tt("dve", tmp, banks[B_], gab[i], ALU.mult, [bk[B_], b_gab[i]], [bTF[tb % 2]])
                xs_ = xown[:, tb, oc * 512:(oc + 1) * 512]
                tt("dve", xs_, xs_, tmp, ALU.add, [bTF[tb % 2], b_xown[tb]], [b_xown[tb]])
        P.pop()
        P.barrier()
        P.pop()
        if s == 0:
            dump("x1", xown, b_xown[3])

        P.push()
        hT = P.sb([128, 16, 512], BF16, "hT")
        b_hT = bufs("hT", 16)
        actT = P.sb([128, 44, 512], BF16, "actT")
        b_actT = bufs("actT", 44)
        w13 = [P.sb([128, 2, 16, 256], BF16, f"w13_{i}") for i in range(2)]
        b_w13 = bufs("w13", 2)
        P.push()
        xn4 = P.sb([128, 4, D], BF16, "xn4")
        b_xn4 = bufs("xn4", 4)
        for ti in range(4):
            rms_xn(xown[:, ti, :], b_xown[ti], xn4[:, ti, :], b_xn4[ti])
        transposes(xn4, b_xn4, hT, b_hT, A2T, shfT)
        P.pop()
        P.barrier()
        w2b = [P.sb([128, 11, 512], BF16, f"w2b{i}") for i in range(2)]
        b_w2b = bufs("w2b", 2)
        gfb = [P.sb([128, 512], F32, f"gfb{i}") for i in range(2)]
        b_gfb = bufs("gfb", 2)
        for f2 in range(22):
            i = f2 % 2
            ld(w13[i][:, 0, :, :], w1_bf[:, f2 * 256:(f2 + 1) * 256].rearrange("(c p) n -> p c n", p=128), [], [b_w13[i]])
            ld(w13[i][:, 1, :, :], w3_bf[:, f2 * 256:(f2 + 1) * 256].rearrange("(c p) n -> p c n", p=128), [], [b_w13[i]])
            for c2 in range(2):
                fc = f2 * 2 + c2
                GB_, UB_ = (0, 1) if fc % 2 == 0 else (2, 3)
                for kc in range(16):
                    mm(banks[GB_], w13[i][:, 0, kc, c2 * 128:(c2 + 1) * 128], hT[:, kc, :], kc == 0, kc == 15,
                       [b_w13[i], b_hT[kc]], [bk[GB_]])
                for kc in range(16):
                    mm(banks[UB_], w13[i][:, 1, kc, c2 * 128:(c2 + 1) * 128], hT[:, kc, :], kc == 0, kc == 15,
                       [b_w13[i], b_hT[kc]], [bk[UB_]])
                sg = TF[fc % 2]
                act(sg, banks[GB_], AF.Silu, [bk[GB_]], [bTF[fc % 2]])
                tt("dve", actT[:, fc, :], sg, banks[UB_], ALU.mult, [bTF[fc % 2], bk[UB_]], [b_actT[fc]])
        w2i = [0]
        for oc in range(4):
            gi = oc % 2
            ld(gfb[gi], ada_d[5 * D + oc * 512:5 * D + (oc + 1) * 512].partition_broadcast(128), [], [b_gfb[gi]])
            for fg in range(4):
                i = w2i[0] % 2
                w2i[0] += 1
                ld(w2b[i], w2_bf[fg * 1408:(fg + 1) * 1408, oc * 512:(oc + 1) * 512].rearrange("(c p) n -> p c n", p=128),
                   [], [b_w2b[i]])
                for tb in range(4):
                    B_ = 4 + tb
                    for i11 in range(11):
                        fc = fg * 11 + i11
                        mm(banks[B_], actT[:, fc, tb * 128:(tb + 1) * 128], w2b[i][:, i11, :], fc == 0, fc == 43,
                           [b_actT[fc], b_w2b[i]], [bk[B_]])
            for tb in range(4):
                B_ = 4 + tb
                tmp = TF[2 + tb % 2]
                tt("dve", tmp, banks[B_], gfb[gi], ALU.mult, [bk[B_], b_gfb[gi]], [bTF[2 + tb % 2]])
                xs_ = xown[:, tb, oc * 512:(oc + 1) * 512]
                tt("dve", xs_, xs_, tmp, ALU.add, [bTF[2 + tb % 2], b_xown[tb]], [b_xown[tb]])
        for tb in range(4):
            ld(out[s * 512 + tb * 128:s * 512 + (tb + 1) * 128, :], xown[:, tb, :], [b_xown[tb]], [b_out], q="pool")
        P.pop()
        P.barrier()

    P.emit()
    return P, dumps


def host_consts(S, j):
    NSL = S // 2048
    NSC = S // 2048
    NSB = S // 64
    c = {}
    c["c_identb"] = np.eye(128, dtype=np.float32).astype(NPBF)
    c["c_identf"] = np.stack([np.eye(128, dtype=np.float32), np.ones((128, 128), np.float32)], 1)
    ones2 = np.ones((128, 2, 128), np.float32)
    ones2[0, 1, :] = 0.0
    c["c_ones"] = ones2.astype(NPBF)
    rot = np.zeros((128, 128), np.float32)
    for m in range(64):
        rot[m + 64, m] = -1.0
    for m in range(64, 128):
        rot[m - 64, m] = 1.0
    c["c_rot"] = rot.astype(NPBF)
    inv = (1.0 / (np.float32(10000.0) ** (np.arange(0, 128, 2, dtype=np.float32) / np.float32(128)))).astype(np.float32)
    c["c_invf"] = np.concatenate([inv, inv]).reshape(128, 1).astype(np.float32)
    ovl = np.zeros((128, NSC, NSB + 1), np.float32)
    for cc in range(NSC):
        for p in range(128):
            n = 128 * cc - 1 + p
            if n < 0:
                continue
            ovl[p, cc, NSB] = 1.0
            for jb in range(NSB):
                if 16 * n < 64 * jb + 64 and 16 * n + 31 >= 64 * jb:
                    ovl[p, cc, jb] = 1.0
    c["c_ovl"] = ovl.astype(NPBF)
    E = np.zeros((128, 64, 128), np.float32)
    for v in range(64):
        for key in range(128):
            E[2 * v + key // 64, v, key] = 1.0
    c["c_E"] = E.astype(NPBF)
    pp = np.arange(128)[:, None, None]
    kt = np.arange(8)[None, :, None]
    ii = np.arange(512)[None, None, :]
    kr = 128 * kt + pp
    tr_ = 512 + ii
    wm = ((kr <= tr_) & (kr > tr_ - 512)).astype(np.float32)
    wm0 = wm.copy()
    if j == 0:
        wm0[:, 0:4, :] = 0.0
    c["c_wmg"] = ((wm - 1.0) * NEGM).astype(NPBF)
    c["c_wm0"] = ((wm0 - 1.0) * NEGM).astype(NPBF)
    rr = np.arange(16)[:, None, None]
    p2 = np.arange(128)[None, :, None]
    slm = (128 * rr + p2 <= 512 * j + ii).astype(np.float32)
    c["c_slm"] = ((slm - 1.0) * NEGM).astype(NPBF)
    pcol = np.arange(128)[:, None]
    irow = np.arange(512)[None, :]
    cmv = (16 * pcol + 15 <= 512 * j + irow).astype(np.float32)
    cm0 = np.ones((128, 512), np.float32)
    cm0[0, :] = 0.0
    c["c_cm"] = ((np.stack([cmv, cm0], 1) - 1.0) * NEGM).astype(NPBF)
    selb = np.zeros((128, NSL, 4, NSB), np.float32)
    jb = np.arange(NSB)[None, :]
    for s in range(NSL):
        for tb in range(4):
            t = 512 * (4 * s + j) + 128 * tb + np.arange(128)[:, None]
            cur = t // 64
            valid = 64 * jb <= t
            forced = (jb == 0) | (jb == cur) | (jb == cur - 1)
            selb[:, s, tb, :] = np.where(valid, np.where(forced, 1e4, 0.0), -1e30)
    c["c_selb"] = selb
    hvv = np.ones((128, NSL), np.float32)
    if j == 0:
        hvv[:, 0] = 0.0
    c["hv"] = hvv
    return c


def make_in_maps(inputs, S, ncores=8):
    x = np.asarray(inputs["x"])
    cvec = np.asarray(inputs["c"])
    pos = np.asarray(inputs["positions"]).astype(np.int32)
    NSL = S // 2048
    wnames = ["ada_w", "ada_b", "norm_mix", "norm_ffn", "w_in", "conv_w", "cmp_pe_k", "cmp_k_w1", "cmp_k_b1",
              "cmp_k_w2", "cmp_pe_v", "cmp_v_w1", "cmp_v_b1", "cmp_v_w2", "q_norm", "k_norm", "out_norm_conv",
              "out_norm_attn", "w_out", "ffn_w1", "ffn_w3", "ffn_w2"]
    shared = {}
    for n in wnames:
        a = np.ascontiguousarray(np.asarray(inputs[n], dtype=np.float32)[0])
        shared[n] = a
    in_maps = []
    for core in range(ncores):
        b, j = core // 4, core % 4
        m = dict(shared)
        m["xf"] = np.ascontiguousarray(x[b, :S])
        xqv = np.zeros((NSL, 1024, D), np.float32)
        pq = np.zeros((NSL, 1024), np.int32)
        for s in range(NSL):
            p = 4 * s + j
            lo = 512 * p - 512
            if lo >= 0:
                xqv[s] = x[b, lo:lo + 1024]
                pq[s] = pos[b, lo:lo + 1024]
            else:
                xqv[s, 512:] = x[b, 0:512]
                pq[s, 512:] = pos[b, 0:512]
        m["xq"] = xqv
        m["posf"] = np.ascontiguousarray(pos[b, :S])
        m["posq"] = pq.reshape(-1)
        m["cvec"] = np.ascontiguousarray(cvec[b].reshape(16, 128))
        m.update(host_consts(S, j))
        in_maps.append(m)
    return in_maps


_CACHE = {}


def kernel(**inputs):
    S = 16384
    if S not in _CACHE:
        nc = bass.Bass("TRN2", target_bir_lowering=False)
        build(nc, S)
        _CACHE[S] = nc
    nc = _CACHE[S]
    in_maps = make_in_maps(inputs, S)
    res = run_bass_kernel_spmd(nc, in_maps, core_ids=list(range(8)))
    NSL = S // 2048
    outp = np.zeros((2, S, D), np.float32)
    for core in range(8):
        b, j = core // 4, core % 4
        o = res.results[core]["out"]
        for s in range(NSL):
            p = 4 * s + j
            outp[b, 512 * p:512 * (p + 1)] = o[512 * s:512 * (s + 1)]
    return outp
```

```python
import math
import numpy as np
import ml_dtypes
import concourse.bass as bass
import concourse.mybir as mybir
from concourse.bass_utils import run_bass_kernel_spmd

F32 = mybir.dt.float32
BF16 = mybir.dt.bfloat16
I32 = mybir.dt.int32
AF = mybir.ActivationFunctionType
ALU = mybir.AluOpType
NPBF = ml_dtypes.bfloat16

D = 2048
DFF = 5632
INW = 5656
EPS = 1e-6
SCALE = 128 ** -0.5
NEGM = 30000.0
ENGS = ("pe", "act", "dve", "pool", "sp")


class Buf:
    __slots__ = ("name", "w", "rs")

    def __init__(self, name):
        self.name = name
        self.w = None
        self.rs = {}


def bufs(name, n):
    return [Buf(f"{name}{i}") for i in range(n)]


class Prog:
    NDMA = {"sp": 24, "pool": 16}

    def __init__(self, nc):
        self.nc = nc
        self.ops = {e: [] for e in ENGS}
        self.cnt = {e: 0 for e in ENGS}
        self.seen = {e: {} for e in ENGS}
        self.dcount = {}
        self.drr = {q: 0 for q in self.NDMA}
        for q, n in self.NDMA.items():
            for i in range(n):
                self.dcount[(q, i)] = 0
        self.sb_off = 16640
        self.sb_stack = []
        self.ntens = 0
        self.hw = 0

    def sb(self, shape, dtype, name="t"):
        esz = {F32: 4, BF16: 2, I32: 4}[dtype]
        n = 1
        for s in shape[1:]:
            n *= s
        nbytes = (n * esz + 63) // 64 * 64
        off = self.sb_off
        self.sb_off += nbytes
        self.hw = max(self.hw, self.sb_off)
        assert self.sb_off <= 229300, f"SBUF overflow {self.sb_off} at {name}"
        self.ntens += 1
        t = self.nc.alloc_sbuf_tensor_at(f"{name}_{self.ntens}", list(shape), dtype, offset=off)
        return t.ap()

    def push(self):
        self.sb_stack.append(self.sb_off)

    def pop(self):
        self.sb_off = self.sb_stack.pop()

    def _deps(self, eng, reads, writes):
        need = {}
        for b in reads:
            if b.w is not None:
                k, v = b.w
                if need.get(k, 0) < v:
                    need[k] = v
        for b in writes:
            if b.w is not None:
                k, v = b.w
                if need.get(k, 0) < v:
                    need[k] = v
            for k, v in b.rs.items():
                if need.get(k, 0) < v:
                    need[k] = v
        waits = []
        seen = self.seen[eng]
        for k, v in need.items():
            if eng == "pe" and k == "pe":
                continue
            if seen.get(k, 0) >= v:
                continue
            seen[k] = v
            waits.append((k, v))
        return waits

    def _post(self, ev, reads, writes):
        k, v = ev
        for b in reads:
            if b.rs.get(k, 0) < v:
                b.rs[k] = v
        for b in writes:
            b.w = ev
            b.rs = {}

    def op(self, eng, fn, reads=(), writes=()):
        waits = self._deps(eng, reads, writes)
        self.cnt[eng] += 1
        ev = (eng, self.cnt[eng])
        self.ops[eng].append((waits, fn, ev, 1))
        self._post(ev, reads, writes)

    def dma(self, q, fn, reads=(), writes=()):
        waits = self._deps(q, reads, writes)
        i = self.drr[q]
        self.drr[q] = (i + 1) % self.NDMA[q]
        key = (q, i)
        cur = self.dcount[key]
        if cur > 0 and self.seen[q].get(key, 0) < cur:
            self.seen[q][key] = cur
            waits.append((key, cur))
        self.dcount[key] = cur + 16
        ev = (key, cur + 16)
        self.ops[q].append((waits, fn, ev, 16))
        self._post(ev, reads, writes)

    def barrier(self):
        for e in ENGS:
            waits = []
            for e2 in ENGS:
                if e2 == e:
                    continue
                v = self.cnt[e2]
                if v > 0 and self.seen[e].get(e2, 0) < v:
                    self.seen[e][e2] = v
                    waits.append((e2, v))
            for key, v in self.dcount.items():
                if v > 0 and self.seen[e].get(key, 0) < v:
                    self.seen[e][key] = v
                    waits.append((key, v))
            if waits:
                self.ops[e].append((waits, None, None, 0))

    def emit(self):
        import contextlib

        nc = self.nc
        sems = {}
        with contextlib.ExitStack() as st:
            for e in ENGS:
                sems[e] = st.enter_context(nc.semaphore(f"s_{e}"))
            for key in self.dcount:
                sems[key] = st.enter_context(nc.semaphore(f"d_{key[0]}{key[1]}"))
            self.barrier()
            block = st.enter_context(nc.Block())
            ops = self.ops

            def replay(name, e):
                for waits, fn, ev, inc in ops[name]:
                    for k, v in waits:
                        e.wait_ge(sems[k], v)
                    if fn is not None:
                        fn(e).then_inc(sems[ev[0]], inc)

            @block.tensor
            def _(e):
                replay("pe", e)

            @block.scalar
            def _(e):
                replay("act", e)

            @block.vector
            def _(e):
                replay("dve", e)

            @block.gpsimd
            def _(e):
                replay("pool", e)

            @block.sync
            def _(e):
                replay("sp", e)


def build(nc, S, dump_names=()):
    NCH = S // 512
    NSL = NCH // 4
    NSC = S // 2048
    NSB = S // 64
    P = Prog(nc)

    def din(name, shape, dt=F32):
        return nc.dram_tensor(name, list(shape), dt, kind="ExternalInput").ap()

    def dscr(name, shape, dt):
        return nc.dram_tensor(name, list(shape), dt, kind="Internal").ap()

    xf = din("xf", [S, D])
    xq = din("xq", [NSL, 1024, D])
    posf = din("posf", [S], I32)
    posq = din("posq", [NSL * 1024], I32)
    cvec = din("cvec", [16, 128])
    hv = din("hv", [128, NSL])
    ada_w = din("ada_w", [D, 6 * D])
    ada_b = din("ada_b", [6 * D])
    norm_mix = din("norm_mix", [D])
    norm_ffn = din("norm_ffn", [D])
    w_in = din("w_in", [D, INW])
    conv_w = din("conv_w", [3, 1024])
    pe_k = din("cmp_pe_k", [32, 128])
    k_w1 = din("cmp_k_w1", [32, 128, 256])
    k_b1 = din("cmp_k_b1", [256])
    k_w2 = din("cmp_k_w2", [256, 128])
    pe_v = din("cmp_pe_v", [32, 128])
    v_w1 = din("cmp_v_w1", [32, 128, 256])
    v_b1 = din("cmp_v_b1", [256])
    v_w2 = din("cmp_v_w2", [256, 128])
    q_norm = din("q_norm", [128])
    k_norm = din("k_norm", [3, 128])
    on_conv = din("out_norm_conv", [1024])
    on_attn = din("out_norm_attn", [1024])
    w_out = din("w_out", [D, D])
    ffn_w1 = din("ffn_w1", [D, DFF])
    ffn_w3 = din("ffn_w3", [D, DFF])
    ffn_w2 = din("ffn_w2", [DFF, D])
    c_identb = din("c_identb", [128, 128], BF16)
    c_identf = din("c_identf", [128, 2, 128])
    c_ones = din("c_ones", [128, 2, 128], BF16)
    c_rot = din("c_rot", [128, 128], BF16)
    c_invf = din("c_invf", [128, 1])
    c_ovl = din("c_ovl", [128, NSC, NSB + 1], BF16)
    c_E = din("c_E", [128, 64, 128], BF16)
    c_wm0 = din("c_wm0", [128, 8, 512], BF16)
    c_wmg = din("c_wmg", [128, 8, 512], BF16)
    c_slm = din("c_slm", [16, 128, 512], BF16)
    c_cm = din("c_cm", [128, 2, 512], BF16)
    c_selb = din("c_selb", [128, NSL, 4, NSB])
    out = nc.dram_tensor("out", [NSL * 512, D], F32, kind="ExternalOutput").ap()

    wi_bf = dscr("wi_bf", [D, INW], BF16)
    wo_bf = dscr("wo_bf", [D, D], BF16)
    w1_bf = dscr("w1_bf", [D, DFF], BF16)
    w3_bf = dscr("w3_bf", [D, DFF], BF16)
    w2_bf = dscr("w2_bf", [DFF, D], BF16)
    ck1_bf = dscr("ck1_bf", [32, 128, 256], BF16)
    cv1_bf = dscr("cv1_bf", [32, 128, 256], BF16)
    ksT_d = dscr("ksT_d", [2, 128, S], BF16)
    vs_d = dscr("vs_d", [S, 256], BF16)
    ada_d = dscr("ada_d", [6 * D], F32)
    gd = dscr("gd", [24, 512], F32)

    b_wi = bufs("wi", 6)
    b_wo, b_w1, b_w3, b_w2, b_ck1, b_cv1 = [Buf(n) for n in "wo w1 w3 w2 ck1 cv1".split()]
    b_ks = bufs("ksd", NCH)
    b_vs = bufs("vsd", NCH)
    b_adad = Buf("adad")
    b_gd = Buf("gd")
    b_out = Buf("out")

    PS2 = [nc.alloc_psum_tensor(f"ps2_{i}", [128, 1024], F32).ap() for i in range(4)]
    banks = [PS2[i // 2][:, (i % 2) * 512:(i % 2) * 512 + 512] for i in range(8)]

    def bank_bf(i):
        return PS2[i // 2].bitcast(BF16)[:, (i % 2) * 1024:(i % 2) * 1024 + 1024]
    bk = bufs("bank", 8)

    def act(out_, in_, func, R, W, **kw):
        P.op("act", lambda e: e.activation(out=out_, in_=in_, func=func, **kw), R, W)

    def mm(out_, lhsT, rhs, start, stop, R, W):
        P.op("pe", lambda e: e.matmul(out_, lhsT=lhsT, rhs=rhs, start=start, stop=stop), R, W)

    def tr(out_, in_, ident, R, W):
        P.op("pe", lambda e: e.transpose(out=out_, in_=in_, identity=ident), R, W)

    def tt(eng, out_, a, b, op, R, W):
        P.op(eng, lambda e: e.tensor_tensor(out=out_, in0=a, in1=b, op=op), R, W)

    def ts(eng, out_, a, s1, op0, R, W, s2=None, op1=None):
        if op1 is None:
            P.op(eng, lambda e: e.tensor_scalar(out=out_, in0=a, scalar1=s1, scalar2=None, op0=op0), R, W)
        else:
            P.op(eng, lambda e: e.tensor_scalar(out=out_, in0=a, scalar1=s1, scalar2=s2, op0=op0, op1=op1), R, W)

    def stt(out_, a, s, b, op0, op1, R, W):
        P.op("dve", lambda e: e.scalar_tensor_tensor(out=out_, in0=a, scalar=s, in1=b, op0=op0, op1=op1), R, W)

    def cp(eng, out_, in_, R, W):
        if eng == "act":
            P.op(eng, lambda e: e.activation(out=out_, in_=in_, func=AF.Copy), R, W)
        else:
            P.op(eng, lambda e: e.tensor_copy(out=out_, in_=in_), R, W)

    def rcp(out_, in_, R, W):
        P.op("dve", lambda e: e.reciprocal(out=out_, in_=in_), R, W)

    def mset(eng, ap, val, W):
        P.op(eng, lambda e: e.memset(ap, val), (), W)

    def ld(out_, in_, R, W, q="sp", slow=False, **kw):
        if slow:
            P.dma(q, lambda e: e.dma_start(out=out_, in_=in_, allow_slow_non_contiguous=True, **kw), R, W)
        else:
            P.dma(q, lambda e: e.dma_start(out=out_, in_=in_, **kw), R, W)

    dumps = {}
    dump_set = set(dump_names)

    def dump(name, ap, b):
        if name not in dump_set:
            return
        d = nc.dram_tensor("dbg_" + name, list(ap.shape), ap.dtype, kind="ExternalOutput").ap()
        dumps[name] = d
        ld(d, ap, [b], [Buf("dbg")], q="sp")


    identb = P.sb([128, 128], BF16, "identb")
    identf2 = P.sb([128, 2, 128], F32, "identf")
    ones2 = P.sb([128, 2, 128], BF16, "ones2")
    rot = P.sb([128, 128], BF16, "rot")
    invf = P.sb([128, 1], F32, "invf")
    ovl = P.sb([128, NSC, NSB + 1], BF16, "ovl")
    Emat = P.sb([128, 64, 128], BF16, "Emat")
    hvt = P.sb([128, NSL], F32, "hvt")
    b_const = Buf("const")
    for t_, d_ in ((identb, c_identb), (identf2, c_identf), (ones2, c_ones), (rot, c_rot), (invf, c_invf),
                   (ovl, c_ovl), (Emat, c_E), (hvt, hv)):
        ld(t_, d_, [], [b_const])
    onesb = ones2[:, 0, :]
    identf = identf2[:, 0, :]
    onesf = identf2[:, 1, :]

    colp = P.sb([128, 160], F32, "colp")
    b_colp = Buf("colp")
    nmT = colp[:, 0:16]
    nfT = colp[:, 16:32]
    qn = colp[:, 32:33]
    kn = colp[:, 33:36]
    cw = colp[:, 36:60].rearrange("p (g k) -> p g k", k=3)
    onc = colp[:, 60:68]
    ona = colp[:, 68:76]
    b1c = colp[:, 76:80]
    A1T = colp[:, 80:96]
    shaT = colp[:, 96:112]
    A2T = colp[:, 112:128]
    shfT = colp[:, 128:144]
    b1eff = colp[:, 144:148]
    ld(nmT, norm_mix.rearrange("(c p) -> p c", p=128), [], [b_colp], slow=True)
    ld(nfT, norm_ffn.rearrange("(c p) -> p c", p=128), [], [b_colp], slow=True)
    ld(qn, q_norm.rearrange("(p o) -> p o", o=1), [], [b_colp], slow=True)
    ld(kn, k_norm.rearrange("a d -> d a"), [], [b_colp], slow=True)
    for k_ in range(3):
        ld(colp[:, 36 + k_:60:3], conv_w[k_].rearrange("(g p) -> p g", p=128), [], [b_colp], slow=True)
    ld(onc, on_conv.rearrange("(g p) -> p g", p=128), [], [b_colp], slow=True)
    ld(ona, on_attn.rearrange("(g p) -> p g", p=128), [], [b_colp], slow=True)
    ld(b1c[:, 0:2], k_b1.rearrange("(c p) -> p c", p=128), [], [b_colp], slow=True)
    ld(b1c[:, 2:4], v_b1.rearrange("(c p) -> p c", p=128), [], [b_colp], slow=True)

    kcmpT = P.sb([128, 2, 128 * NSC], BF16, "kcmpT")
    vcmp = P.sb([128, NSC, 2, 128], BF16, "vcmp")
    b_kcmp = Buf("kcmp")
    b_vcmp = Buf("vcmp")
    w2c = P.sb([128, 2, 2, 128], BF16, "w2c")
    b_w2c = Buf("w2c")
    ld(w2c[:, 0, :, :], k_w2.rearrange("(c p) d -> p c d", p=128), [], [b_w2c], q="pool")
    ld(w2c[:, 1, :, :], v_w2.rearrange("(c p) d -> p c d", p=128), [], [b_w2c], q="pool")

    NTF, NTB = 6, 4
    TF = [P.sb([128, 512], F32, f"TF{i}") for i in range(NTF)]
    bTF = bufs("TF", NTF)
    TB = [P.sb([128, 512], BF16, f"TB{i}") for i in range(NTB)]
    bTB = bufs("TB", NTB)
    ssq = [P.sb([128, 4], F32, f"ssq{i}") for i in range(4)]
    bssq = bufs("ssq", 4)
    ssq_i = [0]
    junk = P.sb([128, 2048], BF16, "junk")
    b_junk = Buf("junk")

    conv_list = []

    def plan_conv(src, dst, rows, c0, c1, b, rstep=256):
        for r0 in range(0, rows, rstep):
            r1 = min(rows, r0 + rstep)
            conv_list.append((src[r0:r1, c0:c1], dst[r0:r1, c0:c1], b))

    plan_conv(w_in, wi_bf, D, 4096, 5120, b_wi[4])
    kw1f = k_w1.rearrange("l d h -> (l d) h")
    vw1f = v_w1.rearrange("l d h -> (l d) h")
    plan_conv(kw1f, ck1_bf.rearrange("l d h -> (l d) h"), 4096, 0, 256, b_ck1, 512)
    plan_conv(vw1f, cv1_bf.rearrange("l d h -> (l d) h"), 4096, 0, 256, b_cv1, 512)
    plan_conv(w_in, wi_bf, D, 5120, INW, b_wi[5])
    for gi in (1, 2, 0, 3):
        plan_conv(w_in, wi_bf, D, gi * 1024, gi * 1024 + 1024, b_wi[gi])
    for c0 in (0, 1024):
        plan_conv(w_out, wo_bf, D, c0, c0 + 1024, b_wo)
    for c0 in range(0, DFF, 1024):
        plan_conv(ffn_w1, w1_bf, D, c0, min(DFF, c0 + 1024), b_w1)
        plan_conv(ffn_w3, w3_bf, D, c0, min(DFF, c0 + 1024), b_w3)
    for c0 in (0, 1024):
        plan_conv(ffn_w2, w2_bf, DFF, c0, c0 + 1024, b_w2)
    conv_pos = [0]

    def do_conv(n):
        for _ in range(n):
            if conv_pos[0] >= len(conv_list):
                return
            s_, d_, b_ = conv_list[conv_pos[0]]
            conv_pos[0] += 1
            P.dma("pool", lambda e, s_=s_, d_=d_: e.dma_start(out=d_, in_=s_, max_dma_last_dim=4096), [], [Buf("cv")])

    do_conv(8 + 16)

    def rms_xn(x_ap, bx, xn_ap, bxn):
        i = ssq_i[0] % 4
        ssq_i[0] += 1
        s_, bs = ssq[i], bssq[i]
        P.op("dve", lambda e: e.scalar_tensor_tensor(out=junk, in0=x_ap, scalar=1.0, in1=x_ap, op0=ALU.mult, op1=ALU.mult,
                                                     accum_out=s_[:, 0:1]), [bx], [b_junk, bs])
        act(s_[:, 1:2], s_[:, 0:1], AF.Sqrt, [bs], [bs], scale=1.0 / D, bias=EPS)
        rcp(s_[:, 2:3], s_[:, 1:2], [bs], [bs])
        act(xn_ap, x_ap, AF.Copy, [bx, bs], [bxn], scale=s_[:, 2:3])

    def transposes(xn4, bxn4, hT, bhT, AT, shT, col0=0):
        for fc in range(16):
            b_ = fc % 4
            Bv = bank_bf(b_)
            for ti in range(4):
                tr(Bv[:, ti * 128:(ti + 1) * 128], xn4[:, ti, fc * 128:(fc + 1) * 128], identb,
                   [bxn4[ti], b_const], [bk[b_]])
            if fc % 2 == 0:
                act(hT[:, fc, col0:col0 + 512], Bv[:, 0:512], AF.Identity, [bk[b_], b_colp], [bhT[fc]],
                    scale=AT[:, fc:fc + 1], bias=shT[:, fc:fc + 1])
            else:
                ts("dve", hT[:, fc, col0:col0 + 512], Bv[:, 0:512], AT[:, fc:fc + 1], ALU.mult, [bk[b_], b_colp], [bhT[fc]],
                   s2=shT[:, fc:fc + 1], op1=ALU.add)

    TWO_PI = 2.0 * math.pi
    C1 = 6.28125
    C2 = TWO_PI - C1
    MAGIC = 12582912.0
    PI_LO = 3.1415925

    def rope_tables(posi, bposi, cosT, sinT, bcs, n):
        A, K, R, RS = TF[0][:, 0:n], TF[1][:, 0:n], TF[2][:, 0:n], TF[3][:, 0:n]
        bA, bK, bR, bRS = bTF[0], bTF[1], bTF[2], bTF[3]
        cp("dve", A, posi, [bposi], [bA])
        ts("dve", A, A, invf[:, 0:1], ALU.mult, [bA, b_const], [bA])
        ts("dve", K, A, 1.0 / TWO_PI, ALU.mult, [bA], [bK], s2=MAGIC, op1=ALU.add)
        ts("dve", K, K, -MAGIC, ALU.add, [bK], [bK])
        stt(R, K, -C1, A, ALU.mult, ALU.add, [bK, bA], [bR])
        stt(R, K, -C2, R, ALU.mult, ALU.add, [bK, bR], [bR])
        ts("dve", RS, R, -PI_LO, ALU.max, [bR], [bRS], s2=PI_LO, op1=ALU.min)
        act(sinT, RS, AF.Sin, [bRS], [bcs])
        ts("dve", K, R, math.pi / 2, ALU.add, [bR], [bK])
        ts("dve", A, K, math.pi, ALU.is_gt, [bK], [bA])
        stt(K, A, -TWO_PI, K, ALU.mult, ALU.add, [bA, bK], [bK])
        ts("dve", RS, K, -PI_LO, ALU.max, [bK], [bRS], s2=PI_LO, op1=ALU.min)
        act(cosT, RS, AF.Sin, [bRS], [bcs])

    def norm_rope_a(src, bsrc, gaincol, n, bankA):
        sq, xnb = TB[0][:, 0:n], TB[1][:, 0:n]
        rstd = TF[4][:, 0:n]
        act(sq, src, AF.Square, [bsrc], [bTB[0]])
        mm(banks[bankA][:, 0:n], onesb, sq, True, True, [bTB[0], b_const], [bk[bankA]])
        act(rstd, banks[bankA][:, 0:n], AF.Ln, [bk[bankA]], [bTF[4]], scale=1.0 / 128, bias=EPS)
        act(rstd, rstd, AF.Exp, [bTF[4]], [bTF[4]], scale=-0.5)
        stt(xnb, src, gaincol, rstd, ALU.mult, ALU.mult, [bsrc, bTF[4], b_colp], [bTB[1]])

    def norm_rope_b(cosT, sinT, bcs, out_, bout, n, bankB):
        xnb = TB[1][:, 0:n]
        ta, tb_ = TF[5][:, 0:n], TF[3][:, 0:n]
        mm(banks[bankB][:, 0:n], rot, xnb, True, True, [bTB[1], b_const], [bk[bankB]])
        tt("dve", ta, xnb, cosT, ALU.mult, [bTB[1], bcs], [bTF[5]])
        tt("dve", tb_, banks[bankB][:, 0:n], sinT, ALU.mult, [bk[bankB], bcs], [bTF[3]])
        tt("dve", out_, ta, tb_, ALU.add, [bTF[5], bTF[3]], [bout])

    def norm_rope(src, bsrc, gaincol, cosT, sinT, bcs, out_, bout, n, bankA, bankB):
        norm_rope_a(src, bsrc, gaincol, n, bankA)
        norm_rope_b(cosT, sinT, bcs, out_, bout, n, bankB)

    def run_jobs(jobs):
        n = len(jobs)
        if n:
            jobs[0][0]()
        for i in range(n):
            jobs[i][1]()
            if i + 1 < n:
                jobs[i + 1][0]()
            jobs[i][2]()
            if jobs[i][3] is not None:
                jobs[i][3]()

    P.push()
    cT = P.sb([128, 16], F32, "cT")
    c16 = P.sb([16, 128], F32, "c16")
    b_c = Buf("c")
    ld(c16, cvec, [], [b_c])
    tr(banks[0][:, 0:16], c16, identf[0:16, 0:16], [b_c, b_const], [bk[0]])
    act(cT, banks[0][:, 0:16], AF.Silu, [bk[0]], [b_c])
    adab = [P.sb([128, 16, 512], F32, f"adab{i}") for i in range(2)]
    b_adab = bufs("adab", 2)
    arow = [P.sb([1, 512], F32, f"arow{i}") for i in range(2)]
    b_arow = bufs("arow", 2)
    abrow = [P.sb([1, 512], F32, f"abrow{i}") for i in range(2)]
    b_abrow = bufs("abrow", 2)
    for bi in range(24):
        i = bi % 2
        ld(adab[i], ada_w[:, bi * 512:(bi + 1) * 512].rearrange("(c p) n -> p c n", p=128), [], [b_adab[i]])
        ld(abrow[i], ada_b[bi * 512:(bi + 1) * 512].rearrange("(o n) -> o n", o=1), [], [b_abrow[i]])
        bb = 2 + i
        for kc in range(16):
            mm(banks[bb][0:1, :], cT[:, kc:kc + 1], adab[i][:, kc, :], kc == 0, kc == 15, [b_c, b_adab[i]], [bk[bb]])
        tt("dve", arow[i], banks[bb][0:1, :], abrow[i], ALU.add, [bk[bb], b_abrow[i]], [b_arow[i]])
        ld(ada_d[bi * 512:(bi + 1) * 512].rearrange("(o n) -> o n", o=1), arow[i], [b_arow[i]], [b_adad], q="pool")
    adaT = P.sb([128, 96], F32, "adaT")
    b_adaT = Buf("adaT")
    for i6 in range(6):
        ld(adaT[:, i6 * 16:(i6 + 1) * 16], ada_d[i6 * D:(i6 + 1) * D].rearrange("(c p) -> p c", p=128),
           [b_adad], [b_adaT], slow=True)
    ts("dve", A1T, adaT[:, 16:32], 1.0, ALU.add, [b_adaT], [b_colp])
    tt("dve", A1T, A1T, nmT, ALU.mult, [b_colp], [b_colp])
    cp("dve", shaT, adaT[:, 0:16], [b_adaT], [b_colp])
    ts("dve", A2T, adaT[:, 64:80], 1.0, ALU.add, [b_adaT], [b_colp])
    tt("dve", A2T, A2T, nfT, ALU.mult, [b_colp], [b_colp])
    cp("dve", shfT, adaT[:, 48:64], [b_adaT], [b_colp])
    P.pop()
    P.barrier()

    P.push()
    xst = [P.sb([128, D], F32, f"xst{i}") for i in range(2)]
    b_xst = bufs("xst", 2)
    xn4s = [P.sb([128, 4, D], BF16, f"xn4_{i}") for i in range(2)]
    b_xn4s = [bufs(f"xn4_{i}_", 4) for i in range(2)]
    hT = P.sb([128, 16, 512], BF16, "hT")
    b_hT = bufs("hT", 16)
    wkv = P.sb([128, 16, 1024], BF16, "wkv")
    b_wkv = Buf("wkv")
    cbuf = P.sb([128, 4, 16 + 2048], BF16, "cbuf")
    b_cbuf = bufs("cbuf", 4)
    posis = [P.sb([128, 512], I32, f"posi{i}") for i in range(2)]
    b_posis = bufs("posi", 2)
    cosTs = [P.sb([128, 512], F32, f"cosT{i}") for i in range(2)]
    sinTs = [P.sb([128, 512], F32, f"sinT{i}") for i in range(2)]
    b_css = bufs("cs", 2)
    ccmp = P.sb([128, 128], F32, "ccmp")
    scmp = P.sb([128, 128], F32, "scmp")
    b_ccs = Buf("ccs")
    kst = [P.sb([128, 512], BF16, f"kst{i}") for i in range(2)]
    b_kst = bufs("kst", 2)
    vst = [P.sb([128, 4, 256], BF16, f"vst{i}") for i in range(2)]
    b_vst = bufs("vst", 2)
    w1t = P.sb([128, 32, 256], BF16, "w1t")
    b_w1t = Buf("w1t")
    pet = P.sb([32, 2, 128], F32, "pet")
    peT = P.sb([128, 2, 32], BF16, "peT")
    b_pe = Buf("pe")
    hid = P.sb([128, 2, 2, 128], BF16, "hid")
    b_hid = Buf("hid")

    for idx in range(4):
        mset("pool", cbuf[:, idx, 0:16], 0.0, [b_cbuf[idx]])
    P.barrier()
    ld(wkv, wi_bf[:, 4096:5120].rearrange("(c p) n -> p c n", p=128), [], [b_wkv])

    ld(pet[:, 0, :], pe_k, [], [b_pe])
    ld(pet[:, 1, :], pe_v, [], [b_pe])
    for kv in range(2):
        tr(banks[0][:, kv * 32:(kv + 1) * 32], pet[:, kv, :], identf[0:32, 0:32], [b_pe, b_const], [bk[0]])
    cp("dve", peT, banks[0][:, 0:64].rearrange("p (a l) -> p a l", a=2), [bk[0]], [b_pe])
    for kv in range(2):
        ld(w1t, (ck1_bf if kv == 0 else cv1_bf).rearrange("l d h -> d l h"), [], [b_w1t])
        for hc in range(2):
            for l in range(32):
                mm(banks[1][:, kv * 2 + hc:kv * 2 + hc + 1], w1t[:, l, hc * 128:(hc + 1) * 128], peT[:, kv, l:l + 1],
                   l == 0, l == 31, [b_w1t, b_pe], [bk[1]])
    tt("dve", b1eff, banks[1][:, 0:4], b1c, ALU.add, [bk[1], b_colp], [b_colp])

    def compress(Q):
        for kv in range(2):
            ld(w1t, (ck1_bf if kv == 0 else cv1_bf).rearrange("l d h -> d l h"), [], [b_w1t])
            B_ = 2 + kv
            cb_bufs = [b_cbuf[2 * kv], b_cbuf[2 * kv + 1]]
            for hc in range(2):
                o_ = banks[B_][:, hc * 256:(hc + 1) * 256].rearrange("p (g n) -> p g n", g=2)
                for l in range(32):
                    mm(o_, w1t[:, l, hc * 128:(hc + 1) * 128], cbuf[:, 2 * kv:2 * kv + 2, l:l + 2033:16],
                       l == 0, l == 31, [b_w1t] + cb_bufs, [bk[B_]])
            xh, x2, inner, sg = TF[0], TF[1], TF[2], TF[3]
            for hc in range(2):
                act(xh[:, hc * 256:(hc + 1) * 256], banks[B_][:, hc * 256:(hc + 1) * 256], AF.Identity,
                    [bk[B_], b_colp], [bTF[0]], bias=b1eff[:, kv * 2 + hc:kv * 2 + hc + 1])
            tt("dve", x2, xh, xh, ALU.mult, [bTF[0]], [bTF[1]])
            ts("dve", x2, x2, 0.044715, ALU.mult, [bTF[1]], [bTF[1]], s2=1.0, op1=ALU.add)
            tt("dve", inner, x2, xh, ALU.mult, [bTF[1], bTF[0]], [bTF[2]])
            act(sg, inner, AF.Sigmoid, [bTF[2]], [bTF[3]], scale=1.5957691216057308)
            tt("dve", hid.rearrange("p a g n -> p (a g n)"), sg, xh, ALU.mult, [bTF[3], bTF[0]], [b_hid])
            for g in range(2):
                if kv == 0:
                    for hc in range(2):
                        mm(banks[4][:, 0:128], w2c[:, 0, hc, :], hid[:, hc, g, :], hc == 0, hc == 1, [b_w2c, b_hid], [bk[4]])
                    norm_rope(banks[4][:, 0:128], bk[4], kn[:, 0:1], ccmp, scmp, b_ccs,
                              kcmpT[:, g, Q * 128:(Q + 1) * 128], b_kcmp, 128, 5, 6)
                else:
                    for hc in range(2):
                        mm(banks[4][:, 0:128], hid[:, hc, g, :], w2c[:, 1, hc, :], hc == 0, hc == 1, [b_w2c, b_hid], [bk[4]])
                    cp("act", vcmp[:, Q, g, :], banks[4][:, 0:128], [bk[4]], [b_vcmp])

    def a_load(p, ti):
        i = ti % 2
        ld(xst[i], xf[p * 512 + ti * 128:p * 512 + (ti + 1) * 128, :], [], [b_xst[i]])

    def a_rms(p, ti):
        q_ = p % 2
        i = ti % 2
        rms_xn(xst[i], b_xst[i], xn4s[q_][:, ti, :], b_xn4s[q_][ti])

    def a_tables(p):
        q_ = p % 2
        ld(posis[q_], posf[p * 512:(p + 1) * 512].partition_broadcast(128), [], [b_posis[q_]])
        rope_tables(posis[q_], b_posis[q_], cosTs[q_], sinTs[q_], b_css[q_], 512)

    for ti in range(4):
        if ti < 2:
            a_load(0, ti)
    for ti in range(4):
        a_rms(0, ti)
        if ti + 2 < 4:
            a_load(0, ti + 2)
    a_tables(0)
    for p in range(NCH):
        do_conv(6)
        nxt = p + 1 < NCH
        if nxt:
            a_load(p + 1, 0)
            a_load(p + 1, 1)
        xn4, b_xn4 = xn4s[p % 2], b_xn4s[p % 2]
        cosT, sinT, b_cs = cosTs[p % 2], sinTs[p % 2], b_css[p % 2]
        pq = p % 4
        cp("pool", ccmp[:, 32 * pq:32 * pq + 32], cosT[:, 15:512:16], [b_cs], [b_ccs])
        cp("pool", scmp[:, 32 * pq:32 * pq + 32], sinT[:, 15:512:16], [b_cs], [b_ccs])
        transposes(xn4, b_xn4, hT, b_hT, A1T, shaT)

        def kcvc(idx):
            B_ = 2 + idx % 2
            for kc in range(16):
                mm(banks[B_], wkv[:, kc, idx * 128:(idx + 1) * 128], hT[:, kc, :], kc == 0, kc == 15,
                   [b_wkv, b_hT[kc]], [bk[B_]])
            cp("act", cbuf[:, idx, 16 + 512 * pq:16 + 512 * pq + 512], banks[B_], [bk[B_]], [b_cbuf[idx]])

        def kslproj(g):
            B_ = 4 + g
            for kc in range(16):
                mm(banks[B_], wkv[:, kc, 512 + g * 128:512 + (g + 1) * 128], hT[:, kc, :], kc == 0, kc == 15,
                   [b_wkv, b_hT[kc]], [bk[B_]])

        vi = p % 2

        def vsl(ti):
            B_ = 2 + ti % 2
            for kc in range(16):
                mm(banks[B_][:, 0:256], hT[:, kc, ti * 128:(ti + 1) * 128], wkv[:, kc, 768:1024], kc == 0, kc == 15,
                   [b_wkv, b_hT[kc]], [bk[B_]])
            cp("act", vst[vi][:, ti, :], banks[B_][:, 0:256], [bk[B_]], [b_vst[vi]])

        kslproj(0)
        kslproj(1)
        kcvc(0)
        kcvc(1)
        if nxt:
            a_rms(p + 1, 0)
            a_load(p + 1, 2)
        norm_rope_a(banks[4], bk[4], kn[:, 1:2], 512, 6)
        kcvc(2)
        if nxt:
            a_rms(p + 1, 1)
            a_load(p + 1, 3)
        kcvc(3)
        norm_rope_b(cosT, sinT, b_cs, kst[0], b_kst[0], 512, 7)
        ld(ksT_d[0, :, p * 512:(p + 1) * 512], kst[0], [b_kst[0]], [b_ks[p]])
        if nxt:
            a_rms(p + 1, 2)
        norm_rope_a(banks[5], bk[5], kn[:, 1:2], 512, 6)
        vsl(0)
        vsl(1)
        norm_rope_b(cosT, sinT, b_cs, kst[1], b_kst[1], 512, 7)
        ld(ksT_d[1, :, p * 512:(p + 1) * 512], kst[1], [b_kst[1]], [b_ks[p]])
        if nxt:
            a_rms(p + 1, 3)
            a_tables(p + 1)
        vsl(2)
        vsl(3)
        ld(vs_d[p * 512:(p + 1) * 512, :].rearrange("(t p) n -> p t n", p=128), vst[vi], [b_vst[vi]], [b_vs[p]])
        if pq == 3:
            compress(p // 4)
            for idx in range(4):
                cp("pool", cbuf[:, idx, 0:16], cbuf[:, idx, 2048:2064], [b_cbuf[idx]], [b_cbuf[idx]])
    mset("dve", vcmp[0:1, 0, :, :], 0.0, [b_vcmp])
    dump("kcmpT", kcmpT, b_kcmp)
    dump("vcmp", vcmp, b_vcmp)
    do_conv(len(conv_list))
    P.pop()
    P.barrier()

    xown = P.sb([128, 4, D], F32, "xown")
    b_xown = bufs("xown", 4)
    for s in range(NSL):
        P.push()
        qT = P.sb([128, 8, 512], BF16, "qT")
        b_qT = bufs("qT", 8)
        ycT = P.sb([128, 8, 512], BF16, "ycT")
        b_ycT = bufs("ycT", 8)
        kwT = P.sb([128, 2, 1024], BF16, "kwT")
        b_kwT = Buf("kwT")
        vwt = P.sb([128, 8, 256], BF16, "vwt")
        b_vwt = Buf("vwt")

        P.push()
        xst = P.sb([128, D], F32, "xst")
        b_xst1 = Buf("xst1")
        xn4 = P.sb([128, 4, D], BF16, "xn4")
        b_xn4 = bufs("xn4", 4)
        hT = P.sb([128, 16, 512], BF16, "hT")
        b_hT = bufs("hT", 16)
        hT2 = P.sb([128, 16, 2], BF16, "hT2")
        b_hT2 = Buf("hT2")
        wtail = P.sb([128, 16, 536], BF16, "wtail")
        b_wtail = Buf("wtail")
        NWB = 3
        wblk = [P.sb([128, 16, 256], BF16, f"wblk{i}") for i in range(NWB)]
        b_wblk = bufs("wblk", NWB)
        wb_i = [0]
        posi = P.sb([128, 1024], I32, "posi")
        b_posi = Buf("posi")
        cosq = P.sb([128, 1024], F32, "cosq")
        sinq = P.sb([128, 1024], F32, "sinq")
        b_cs = Buf("cs")
        gT = P.sb([24, 512], F32, "gT")
        b_gT = Buf("gT")
        uh = P.sb([128, 4], F32, "uh")
        b_uh = Buf("uh")
        ubuf = P.sb([128, 514], F32, "ubuf")
        b_ubuf = Buf("ubuf")

        for ti in range(4):
            ld(xown[:, ti, :], xq[s, ti * 128:(ti + 1) * 128, :], [], [b_xown[ti]])
        ld(wtail, wi_bf[:, 5120:INW].rearrange("(c p) n -> p c n", p=128), [], [b_wtail])
        ld(posi, posq[s * 1024:(s + 1) * 1024].partition_broadcast(128), [], [b_posi])

        def load_wblk(c0):
            i = wb_i[0] % NWB
            wb_i[0] += 1
            ld(wblk[i], wi_bf[:, c0:c0 + 256].rearrange("(c p) n -> p c n", p=128), [], [b_wblk[i]])
            return wblk[i], b_wblk[i]

        for part in range(2):
            for ti in range(4):
                rms_xn(xown[:, ti, :], b_xown[ti], xn4[:, ti, :], b_xn4[ti])
            if part == 0:
                for ti in range(4):
                    r0 = 512 + ti * 128
                    ld(xown[:, ti, :], xq[s, r0:r0 + 128, :], [], [b_xown[ti]])
            transposes(xn4, b_xn4, hT, b_hT, A1T, shaT)
            if part == 0:
                cp("pool", hT2, hT[:, :, 510:512], b_hT, [b_hT2])
                for hh in range(2):
                    rope_tables(posi[:, hh * 512:(hh + 1) * 512], b_posi, cosq[:, hh * 512:(hh + 1) * 512],
                                sinq[:, hh * 512:(hh + 1) * 512], b_cs, 512)
            jobs = []
            for g in range(2):
                def proj(g=g):
                    B_ = 2 + g
                    for kc in range(16):
                        mm(banks[B_], wtail[:, kc, g * 128:(g + 1) * 128], hT[:, kc, :], kc == 0, kc == 15,
                           [b_wtail, b_hT[kc]], [bk[B_]])
                def nra(g=g):
                    norm_rope_a(banks[2 + g], bk[2 + g], kn[:, 2:3], 512, 6)
                def nrb(g=g, part=part):
                    norm_rope_b(cosq[:, part * 512:(part + 1) * 512], sinq[:, part * 512:(part + 1) * 512], b_cs,
                                kwT[:, g, part * 512:(part + 1) * 512], b_kwT, 512, 7)
                jobs.append((proj, nra, nrb, None))
            run_jobs(jobs)
            for ti in range(4):
                B_ = 4 + ti % 2
                for kc in range(16):
                    mm(banks[B_][:, 0:256], hT[:, kc, ti * 128:(ti + 1) * 128], wtail[:, kc, 256:512], kc == 0, kc == 15,
                       [b_wtail, b_hT[kc]], [bk[B_]])
                cp("act", vwt[:, part * 4 + ti, :], banks[B_][:, 0:256], [bk[B_]], [b_vwt])

        for kc in range(16):
            mm(banks[2][0:24, :], wtail[:, kc, 512:536], hT[:, kc, :], kc == 0, kc == 15, [b_wtail, b_hT[kc]], [bk[2]])
        act(gT, banks[2][0:24, :], AF.Sigmoid, [bk[2]], [b_gT])
        ld(gd, gT, [b_gT], [b_gd], q="pool")

        for cgp in range(4):
            wcc, bwcc = load_wblk(1024 + cgp * 256)
            wch, bwch = load_wblk(2048 + cgp * 256)
            wcb, bwcb = load_wblk(cgp * 256)
            for c2 in range(2):
                cg = cgp * 2 + c2
                cs_ = slice(c2 * 128, (c2 + 1) * 128)
                Bcc, Bch, Bcb = (2, 3, 4) if cg % 2 == 0 else (0, 1, 7)
                for kc in range(16):
                    mm(banks[Bcc], wcc[:, kc, cs_], hT[:, kc, :], kc == 0, kc == 15, [bwcc, b_hT[kc]], [bk[Bcc]])
                for kc in range(16):
                    mm(banks[5][:, 0:2], wcc[:, kc, cs_], hT2[:, kc, :], kc == 0, kc == 15, [bwcc, b_hT2], [bk[5]])
                for kc in range(16):
                    mm(banks[Bch], wch[:, kc, cs_], hT[:, kc, :], kc == 0, kc == 15, [bwch, b_hT[kc]], [bk[Bch]])
                for kc in range(16):
                    mm(banks[5][:, 2:4], wch[:, kc, cs_], hT2[:, kc, :], kc == 0, kc == 15, [bwch, b_hT2], [bk[5]])
                for kc in range(16):
                    mm(banks[Bcb], wcb[:, kc, cs_], hT[:, kc, :], kc == 0, kc == 15, [bwcb, b_hT[kc]], [bk[Bcb]])
                cp("act", uh, banks[5][:, 0:4], [bk[5]], [b_uh])
                tt("dve", ubuf[:, 0:2], uh[:, 0:2], uh[:, 2:4], ALU.mult, [b_uh], [b_ubuf])
                ts("dve", ubuf[:, 0:2], ubuf[:, 0:2], hvt[:, s:s + 1], ALU.mult, [b_ubuf, b_const], [b_ubuf])
                ccs = TF[0]
                cp("act", ccs, banks[Bcc], [bk[Bcc]], [bTF[0]])
                tt("dve", ubuf[:, 2:514], ccs, banks[Bch], ALU.mult, [bTF[0], bk[Bch]], [b_ubuf])
                y = TF[1]
                ts("dve", y, ubuf[:, 2:514], cw[:, cg, 2:3], ALU.mult, [b_ubuf, b_colp], [bTF[1]])
                stt(y, ubuf[:, 1:513], cw[:, cg, 1:2], y, ALU.mult, ALU.add, [b_ubuf, bTF[1], b_colp], [bTF[1]])
                stt(y, ubuf[:, 0:512], cw[:, cg, 0:1], y, ALU.mult, ALU.add, [b_ubuf, bTF[1], b_colp], [bTF[1]])
                tt("dve", y, y, banks[Bcb], ALU.mult, [bTF[1], bk[Bcb]], [bTF[1]])
                sq = TB[0]
                act(sq, y, AF.Square, [bTF[1]], [bTB[0]])
                mm(banks[6], onesb, sq, True, True, [bTB[0], b_const], [bk[6]])
                rstd = TF[4]
                act(rstd, banks[6], AF.Ln, [bk[6]], [bTF[4]], scale=1.0 / 128, bias=EPS)
                act(rstd, rstd, AF.Exp, [bTF[4]], [bTF[4]], scale=-0.5)
                stt(ycT[:, cg, :], y, onc[:, cg:cg + 1], rstd, ALU.mult, ALU.mult, [bTF[1], bTF[4], b_colp], [b_ycT[cg]])
        jobs = []
        qw = {}
        for h in range(8):
            def proj(h=h):
                qb, c2 = h // 2, h % 2
                if c2 == 0:
                    qw[qb] = load_wblk(3072 + qb * 256)
                wq, bwq = qw[qb]
                B_ = 2 + c2
                for kc in range(16):
                    mm(banks[B_], wq[:, kc, c2 * 128:(c2 + 1) * 128], hT[:, kc, :], kc == 0, kc == 15,
                       [bwq, b_hT[kc]], [bk[B_]])
            def nra(h=h):
                norm_rope_a(banks[2 + h % 2], bk[2 + h % 2], qn, 512, 6)
            def nrb(h=h):
                norm_rope_b(cosq[:, 512:1024], sinq[:, 512:1024], b_cs, qT[:, h, :], b_qT[h], 512, 7)
            jobs.append((proj, nra, nrb, None))
        run_jobs(jobs)
        if s == 0:
            dump("qT", qT, b_qT[7])
            dump("ycT", ycT, b_ycT[7])
            dump("kwT", kwT, b_kwT)
            dump("vwt", vwt, b_vwt)
            dump("gT", gT, b_gT)
        P.pop()
        P.barrier()

        P.push()
        yaT = None
        wm = P.sb([128, 8, 512], BF16, "wm")
        b_wm = Buf("wm")
        cm = P.sb([128, 2, 512], BF16, "cm")
        b_cm = Buf("cm")
        selb = P.sb([128, 4, NSB], F32, "selb")
        b_selb = Buf("selb")
        imp = P.sb([128, 4, NSB], F32, "imp")
        b_imp = bufs("imp", 4)
        NJP = max(1, NSB // 128)
        selT = P.sb([128, NJP, 512], BF16, "selT")
        b_selT = Buf("selT")
        oacc = P.sb([128, 8, 512], F32, "oacc")
        b_oacc = bufs("oacc", 8)
        pstore = P.sb([128, 2, NSC, 1024], BF16, "pstore")
        b_pst = [[Buf(f"pst{r}_{c}") for c in range(NSC)] for r in range(2)]
        NPT = 3
        Pt2 = [P.sb([128, 1024], BF16, f"Pt2_{i}") for i in range(NPT)]
        b_Pt2 = bufs("Pt2", NPT)
        LaccP = PS2[3]
        b_LaccP = [bk[6], bk[7]]
        slmt = [P.sb([128, 512], BF16, f"slmt{i}") for i in range(2)]
        b_slmt = bufs("slmt", 2)
        gb = [P.sb([128, 512], F32, f"gb{i}") for i in range(2)]
        b_gb = bufs("gb", 2)
        kstl = [P.sb([128, 512], BF16, f"kstl{i}") for i in range(2)]
        b_kstl = bufs("kstl", 2)
        vstl = [P.sb([128, 4, 128], BF16, f"vstl{i}") for i in range(2)]
        b_vstl = bufs("vstl", 2)
        score = P.sb([128, NSB], F32, "score")
        sc2 = P.sb([128, NSB], F32, "sc2")
        m8 = P.sb([128, 16], F32, "m8")
        selm = P.sb([128, NSB], BF16, "selm")
        b_tk = Buf("topk")
        rl = P.sb([128, 4], F32, "rl")
        b_rl = Buf("rl")

        ld(wm, c_wm0 if s == 0 else c_wmg, [], [b_wm])
        ld(cm, c_cm, [], [b_cm])
        ld(selb, c_selb[:, s, :, :], [], [b_selb])
        gb_i = [0]
        pipe = {"v": 0, "pend": None, "pt": 0}

        def pv_flush():
            pd = pipe["pend"]
            pipe["pend"] = None
            if pd is None:
                return
            pt_ap, pt_buf, first, last, vt_ap, vt_bufs = pd
            for r_ in range(2):
                ob = 4 + r_
                mm(banks[ob], vt_ap, pt_ap[:, r_ * 512:(r_ + 1) * 512], first, last, vt_bufs + [pt_buf], [bk[ob]])

        def unit(qk, pt_ap, pt_buf, first, last, vt_ap, vt_bufs):
            k = pipe["v"] % 2
            pipe["v"] += 1
            sb_ = [bk[2 * k], bk[2 * k + 1]]
            for r_ in range(2):
                n = len(qk[r_])
                for i_, (l_, rh_, Rb) in enumerate(qk[r_]):
                    mm(PS2[k][:, r_ * 512:(r_ + 1) * 512], l_, rh_, i_ == 0, i_ == n - 1, Rb, sb_)
            act(pt_ap, PS2[k], AF.Exp, sb_, [pt_buf], scale=SCALE)
            if first:
                cp("dve", LaccP, pt_ap, [pt_buf], b_LaccP)
            else:
                tt("dve", LaccP, LaccP, pt_ap, ALU.add, [pt_buf] + b_LaccP, b_LaccP)
            pv_flush()
            pipe["pend"] = (pt_ap, pt_buf, first, last, vt_ap, vt_bufs)

        def next_pt():
            i_ = pipe["pt"] % NPT
            pipe["pt"] += 1
            return Pt2[i_], b_Pt2[i_]

        def finalize(h, x):
            r_ = h % 2
            ob = 4 + r_
            lb = r_
            wt, tmp, lsb = TF[0], TF[1], TF[2]
            gi = gb_i[0] % 2
            gb_i[0] += 1
            ld(gb[gi], gd[h * 3 + x, :].partition_broadcast(128), [b_gd], [b_gb[gi]])
            cp("act", lsb, LaccP[:, r_ * 512:(r_ + 1) * 512], b_LaccP, [bTF[2]])
            mm(banks[lb], onesf, lsb, True, True, [b_const, bTF[2]], [bk[lb]])
            ts("dve", wt, banks[lb], 1e-30, ALU.max, [bk[lb]], [bTF[0]])
            rcp(wt, wt, [bTF[0]], [bTF[0]])
            tt("dve", wt, wt, gb[gi], ALU.mult, [bTF[0], b_gb[gi]], [bTF[0]])
            if x == 0:
                tt("dve", oacc[:, h, :], banks[ob], wt, ALU.mult, [bk[ob], bTF[0]], [b_oacc[h]])
            else:
                tt("dve", tmp, banks[ob], wt, ALU.mult, [bk[ob], bTF[0]], [bTF[1]])
                tt("dve", oacc[:, h, :], oacc[:, h, :], tmp, ALU.add, [bTF[1], b_oacc[h]], [b_oacc[h]])

        for g in range(2):
            for hp in range(2):
                for c in range(s + 1):
                    qk = []
                    for r in range(2):
                        h = 4 * g + 2 * hp + r
                        lst = [(kcmpT[:, g, c * 128:(c + 1) * 128], qT[:, h, :], [b_kcmp, b_qT[h]])]
                        if c == 0:
                            lst.append((identb, cm[:, 1, :], [b_const, b_cm]))
                        if c == s:
                            lst.append((identb, cm[:, 0, :], [b_const, b_cm]))
                        qk.append(lst)
                    unit(qk, pstore[:, hp, c, :], b_pst[hp][c], c == 0, c == s, vcmp[:, c, g, :], [b_vcmp])
                pv_flush()
                for r in range(2):
                    h = 4 * g + 2 * hp + r
                    finalize(h, 0)
                    for tb in range(4):
                        for c in range(s + 1):
                            mm(banks[2][:, 0:NSB + 1], pstore[:, hp, c, r * 512 + tb * 128:r * 512 + (tb + 1) * 128], ovl[:, c, :],
                               c == 0, c == s, [b_pst[hp][c], b_const], [bk[2]])
                        ts("dve", rl[:, 0:1], banks[2][:, NSB:NSB + 1], 1e-30, ALU.max, [bk[2]], [b_rl])
                        rcp(rl[:, 1:2], rl[:, 0:1], [b_rl], [b_rl])
                        if hp == 0 and r == 0:
                            ts("dve", imp[:, tb, :], banks[2][:, 0:NSB], rl[:, 1:2], ALU.mult, [bk[2], b_rl], [b_imp[tb]])
                        else:
                            stt(imp[:, tb, :], banks[2][:, 0:NSB], rl[:, 1:2], imp[:, tb, :], ALU.mult, ALU.add,
                                [bk[2], b_rl, b_imp[tb]], [b_imp[tb]])
            for tb in range(4):
                tt("dve", score, imp[:, tb, :], selb[:, tb, :], ALU.add, [b_imp[tb], b_selb], [b_tk])
                P.op("dve", lambda e, m8=m8, score=score: e.max(out=m8[:, 0:8], in_=score), [b_tk], [b_tk])
                P.op("dve", lambda e, m8=m8, score=score, sc2=sc2: e.match_replace(
                    out=sc2, in_to_replace=m8[:, 0:8], in_values=score, imm_value=-3.0e38), [b_tk], [b_tk])
                P.op("dve", lambda e, m8=m8, sc2=sc2: e.max(out=m8[:, 8:16], in_=sc2), [b_tk], [b_tk])
                ts("dve", sc2, score, m8[:, 15:16], ALU.is_ge, [b_tk], [b_tk])
                ts("dve", selm, sc2, -1.0, ALU.add, [b_tk], [b_tk], s2=30000.0, op1=ALU.mult)
                Bv = bank_bf(3)
                w_ = min(128, NSB)
                for jp in range(NJP):
                    tr(Bv[0:w_, jp * 128:jp * 128 + 128], selm[:, jp * 128:jp * 128 + w_], identb, [b_tk, b_const], [bk[3]])
                for jp in range(NJP):
                    cp("act", selT[0:w_, jp, tb * 128:(tb + 1) * 128], Bv[0:w_, jp * 128:jp * 128 + 128], [bk[3]], [b_selT])
            if s == NSL - 1 and g == 0:
                dump("imp", imp, b_imp[3])
                dump("selT", selT, b_selT)
            NKT = 16 * s + 16
            for hp in range(2):
                for kt4 in range(NKT // 4):
                    li = kt4 % 2
                    ld(kstl[li], ksT_d[g, :, kt4 * 512:(kt4 + 1) * 512], [b_ks[kt4]], [b_kstl[li]])
                    ld(vstl[li], vs_d[kt4 * 512:(kt4 + 1) * 512, g * 128:(g + 1) * 128].rearrange("(t p) d -> p t d", p=128),
                       [b_vs[kt4]], [b_vstl[li]])
                    for k4 in range(4):
                        kt = kt4 * 4 + k4
                        j0 = 2 * kt
                        jp = j0 // 128
                        KE = min(128, NSB)
                        em = (Emat[0:KE, kt % 64, :], selT[0:KE, jp, :], [b_const, b_selT])
                        diag = kt >= 16 * s
                        if diag:
                            rr = kt - 16 * s
                            di = rr % 2
                            ld(slmt[di], c_slm[rr], [], [b_slmt[di]])
                        qk = []
                        for r in range(2):
                            h = 4 * g + 2 * hp + r
                            lst = [(kstl[li][:, k4 * 128:(k4 + 1) * 128], qT[:, h, :], [b_kstl[li], b_qT[h]]), em]
                            if diag:
                                lst.append((identb, slmt[di], [b_const, b_slmt[di]]))
                            qk.append(lst)
                        pt_ap, pt_buf = next_pt()
                        unit(qk, pt_ap, pt_buf, kt == 0, kt == NKT - 1, vstl[li][:, k4, :], [b_vstl[li]])
                pv_flush()
                for r in range(2):
                    finalize(4 * g + 2 * hp + r, 1)
            for hp in range(2):
                for kt in range(8):
                    qk = []
                    for r in range(2):
                        h = 4 * g + 2 * hp + r
                        qk.append([(kwT[:, g, kt * 128:(kt + 1) * 128], qT[:, h, :], [b_kwT, b_qT[h]]),
                                   (identb, wm[:, kt, :], [b_const, b_wm])])
                    pt_ap, pt_buf = next_pt()
                    unit(qk, pt_ap, pt_buf, kt == 0, kt == 7, vwt[:, kt, g * 128:(g + 1) * 128], [b_vwt])
                pv_flush()
                for r in range(2):
                    finalize(4 * g + 2 * hp + r, 2)
        if s == NSL - 1:
            dump("oacc", oacc, b_oacc[7])
        XB_ = 2
        yaT = qT
        b_yaT = b_qT
        for h in range(8):
            sq = TB[0]
            act(sq, oacc[:, h, :], AF.Square, [b_oacc[h]], [bTB[0]])
            mm(banks[XB_], onesb, sq, True, True, [bTB[0], b_const], [bk[XB_]])
            rstd = TF[4]
            act(rstd, banks[XB_], AF.Ln, [bk[XB_]], [bTF[4]], scale=1.0 / 128, bias=EPS)
            act(rstd, rstd, AF.Exp, [bTF[4]], [bTF[4]], scale=-0.5)
            stt(yaT[:, h, :], oacc[:, h, :], ona[:, h:h + 1], rstd, ALU.mult, ALU.mult, [b_oacc[h], bTF[4], b_colp], [b_yaT[h]])
        P.pop()
        P.barrier()

        P.push()
        wob = [P.sb([128, 16, 512], BF16, f"wob{i}") for i in range(2)]
        b_wob = bufs("wob", 2)
        gab = [P.sb([128, 512], F32, f"gab{i}") for i in range(2)]
        b_gab = bufs("gab", 2)
        for oc in range(4):
            i = oc % 2
            ld(wob[i], wo_bf[:, oc * 512:(oc + 1) * 512].rearrange("(c p) n -> p c n", p=128), [], [b_wob[i]])
            ld(gab[i], ada_d[2 * D + oc * 512:2 * D + (oc + 1) * 512].partition_broadcast(128), [], [b_gab[i]])
            for tb in range(4):
                B_ = tb % 4
                for mc in range(16):
                    src_, bsrc = (ycT[:, mc, :], b_ycT[mc]) if mc < 8 else (yaT[:, mc - 8, :], b_yaT[mc - 8])
                    mm(banks[B_], src_[:, tb * 128:(tb + 1) * 128], wob[i][:, mc, :], mc == 0, mc == 15,
                       [bsrc, b_wob[i]], [bk[B_]])
                tmp = TF[tb % 2]
                tt("dve", tmp, banks[B_], gab[i], ALU.mult, [bk[B_], b_gab[i]], [bTF[tb % 2]])
                xs_ = xown[:, tb, oc * 512:(oc + 1) * 512]
                tt("dve", xs_, xs_, tmp, ALU.add, [bTF[tb % 2], b_xown[tb]], [b_xown[tb]])
        P.pop()
        P.barrier()
        P.pop()
        if s == 0:
            dump("x1", xown, b_xown[3])

        P.push()
        hT = P.sb([128, 16, 512], BF16, "hT")
        b_hT = bufs("hT", 16)
        actT = P.sb([128, 44, 512], BF16, "actT")
        b_actT = bufs("actT", 44)
        w13 = [P.sb([128, 2, 16, 256], BF16, f"w13_{i}") for i in range(2)]
        b_w13 = bufs("w13", 2)
        P.push()
        xn4 = P.sb([128, 4, D], BF16, "xn4")
        b_xn4 = bufs("xn4", 4)
        for ti in range(4):
            rms_xn(xown[:, ti, :], b_xown[ti], xn4[:, ti, :], b_xn4[ti])
        transposes(xn4, b_xn4, hT, b_hT, A2T, shfT)
        P.pop()
        P.barrier()
        w2b = [P.sb([128, 11, 512], BF16, f"w2b{i}") for i in range(2)]
        b_w2b = bufs("w2b", 2)
        gfb = [P.sb([128, 512], F32, f"gfb{i}") for i in range(2)]
        b_gfb = bufs("gfb", 2)
        for f2 in range(22):
            i = f2 % 2
            ld(w13[i][:, 0, :, :], w1_bf[:, f2 * 256:(f2 + 1) * 256].rearrange("(c p) n -> p c n", p=128), [], [b_w13[i]])
            ld(w13[i][:, 1, :, :], w3_bf[:, f2 * 256:(f2 + 1) * 256].rearrange("(c p) n -> p c n", p=128), [], [b_w13[i]])
            for c2 in range(2):
                fc = f2 * 2 + c2
                GB_, UB_ = (0, 1) if fc % 2 == 0 else (2, 3)
                for kc in range(16):
                    mm(banks[GB_], w13[i][:, 0, kc, c2 * 128:(c2 + 1) * 128], hT[:, kc, :], kc == 0, kc == 15,
                       [b_w13[i], b_hT[kc]], [bk[GB_]])
                for kc in range(16):
                    mm(banks[UB_], w13[i][:, 1, kc, c2 * 128:(c2 + 1) * 128], hT[:, kc, :], kc == 0, kc == 15,
                       [b_w13[i], b_hT[kc]], [bk[UB_]])
                sg = TF[fc % 2]
                act(sg, banks[GB_], AF.Silu, [bk[GB_]], [bTF[fc % 2]])
                tt("dve", actT[:, fc, :], sg, banks[UB_], ALU.mult, [bTF[fc % 2], bk[UB_]], [b_actT[fc]])
        w2i = [0]
        for oc in range(4):
            gi = oc % 2
            ld(gfb[gi], ada_d[5 * D + oc * 512:5 * D + (oc + 1) * 512].partition_broadcast(128), [], [b_gfb[gi]])
            for fg in range(4):
                i = w2i[0] % 2
                w2i[0] += 1
                ld(w2b[i], w2_bf[fg * 1408:(fg + 1) * 1408, oc * 512:(oc + 1) * 512].rearrange("(c p) n -> p c n", p=128),
                   [], [b_w2b[i]])
                for tb in range(4):
                    B_ = 4 + tb
                    for i11 in range(11):
                        fc = fg * 11 + i11
                        mm(banks[B_], actT[:, fc, tb * 128:(tb + 1) * 128], w2b[i][:, i11, :], fc == 0, fc == 43,
                           [b_actT[fc], b_w2b[i]], [bk[B_]])
            for tb in range(4):
                B_ = 4 + tb
                tmp = TF[2 + tb % 2]
                tt("dve", tmp, banks[B_], gfb[gi], ALU.mult, [bk[B_], b_gfb[gi]], [bTF[2 + tb % 2]])
                xs_ = xown[:, tb, oc * 512:(oc + 1) * 512]
                tt("dve", xs_, xs_, tmp, ALU.add, [bTF[2 + tb % 2], b_xown[tb]], [b_xown[tb]])
        for tb in range(4):
            ld(out[s * 512 + tb * 128:s * 512 + (tb + 1) * 128, :], xown[:, tb, :], [b_xown[tb]], [b_out], q="pool")
        P.pop()
        P.barrier()

    P.emit()
    return P, dumps


def host_consts(S, j):
    NSL = S // 2048
    NSC = S // 2048
    NSB = S // 64
    c = {}
    c["c_identb"] = np.eye(128, dtype=np.float32).astype(NPBF)
    c["c_identf"] = np.stack([np.eye(128, dtype=np.float32), np.ones((128, 128), np.float32)], 1)
    ones2 = np.ones((128, 2, 128), np.float32)
    ones2[0, 1, :] = 0.0
    c["c_ones"] = ones2.astype(NPBF)
    rot = np.zeros((128, 128), np.float32)
    for m in range(64):
        rot[m + 64, m] = -1.0
    for m in range(64, 128):
        rot[m - 64, m] = 1.0
    c["c_rot"] = rot.astype(NPBF)
    inv = (1.0 / (np.float32(10000.0) ** (np.arange(0, 128, 2, dtype=np.float32) / np.float32(128)))).astype(np.float32)
    c["c_invf"] = np.concatenate([inv, inv]).reshape(128, 1).astype(np.float32)
    ovl = np.zeros((128, NSC, NSB + 1), np.float32)
    for cc in range(NSC):
        for p in range(128):
            n = 128 * cc - 1 + p
            if n < 0:
                continue
            ovl[p, cc, NSB] = 1.0
            for jb in range(NSB):
                if 16 * n < 64 * jb + 64 and 16 * n + 31 >= 64 * jb:
                    ovl[p, cc, jb] = 1.0
    c["c_ovl"] = ovl.astype(NPBF)
    E = np.zeros((128, 64, 128), np.float32)
    for v in range(64):
        for key in range(128):
            E[2 * v + key // 64, v, key] = 1.0
    c["c_E"] = E.astype(NPBF)
    pp = np.arange(128)[:, None, None]
    kt = np.arange(8)[None, :, None]
    ii = np.arange(512)[None, None, :]
    kr = 128 * kt + pp
    tr_ = 512 + ii
    wm = ((kr <= tr_) & (kr > tr_ - 512)).astype(np.float32)
    wm0 = wm.copy()
    if j == 0:
        wm0[:, 0:4, :] = 0.0
    c["c_wmg"] = ((wm - 1.0) * NEGM).astype(NPBF)
    c["c_wm0"] = ((wm0 - 1.0) * NEGM).astype(NPBF)
    rr = np.arange(16)[:, None, None]
    p2 = np.arange(128)[None, :, None]
    slm = (128 * rr + p2 <= 512 * j + ii).astype(np.float32)
    c["c_slm"] = ((slm - 1.0) * NEGM).astype(NPBF)
    pcol = np.arange(128)[:, None]
    irow = np.arange(512)[None, :]
    cmv = (16 * pcol + 15 <= 512 * j + irow).astype(np.float32)
    cm0 = np.ones((128, 512), np.float32)
    cm0[0, :] = 0.0
    c["c_cm"] = ((np.stack([cmv, cm0], 1) - 1.0) * NEGM).astype(NPBF)
    selb = np.zeros((128, NSL, 4, NSB), np.float32)
    jb = np.arange(NSB)[None, :]
    for s in range(NSL):
        for tb in range(4):
            t = 512 * (4 * s + j) + 128 * tb + np.arange(128)[:, None]
            cur = t // 64
            valid = 64 * jb <= t
            forced = (jb == 0) | (jb == cur) | (jb == cur - 1)
            selb[:, s, tb, :] = np.where(valid, np.where(forced, 1e4, 0.0), -1e30)
    c["c_selb"] = selb
    hvv = np.ones((128, NSL), np.float32)
    if j == 0:
        hvv[:, 0] = 0.0
    c["hv"] = hvv
    return c


def make_in_maps(inputs, S, ncores=8):
    x = np.asarray(inputs["x"])
    cvec = np.asarray(inputs["c"])
    pos = np.asarray(inputs["positions"]).astype(np.int32)
    NSL = S // 2048
    wnames = ["ada_w", "ada_b", "norm_mix", "norm_ffn", "w_in", "conv_w", "cmp_pe_k", "cmp_k_w1", "cmp_k_b1",
              "cmp_k_w2", "cmp_pe_v", "cmp_v_w1", "cmp_v_b1", "cmp_v_w2", "q_norm", "k_norm", "out_norm_conv",
              "out_norm_attn", "w_out", "ffn_w1", "ffn_w3", "ffn_w2"]
    shared = {}
    for n in wnames:
        a = np.ascontiguousarray(np.asarray(inputs[n], dtype=np.float32)[0])
        shared[n] = a
    in_maps = []
    for core in range(ncores):
        b, j = core // 4, core % 4
        m = dict(shared)
        m["xf"] = np.ascontiguousarray(x[b, :S])
        xqv = np.zeros((NSL, 1024, D), np.float32)
        pq = np.zeros((NSL, 1024), np.int32)
        for s in range(NSL):
            p = 4 * s + j
            lo = 512 * p - 512
            if lo >= 0:
                xqv[s] = x[b, lo:lo + 1024]
                pq[s] = pos[b, lo:lo + 1024]
            else:
                xqv[s, 512:] = x[b, 0:512]
                pq[s, 512:] = pos[b, 0:512]
        m["xq"] = xqv
        m["posf"] = np.ascontiguousarray(pos[b, :S])
        m["posq"] = pq.reshape(-1)
        m["cvec"] = np.ascontiguousarray(cvec[b].reshape(16, 128))
        m.update(host_consts(S, j))
        in_maps.append(m)
    return in_maps


_CACHE = {}


def kernel(**inputs):
    S = 16384
    if S not in _CACHE:
        nc = bass.Bass("TRN2", target_bir_lowering=False)
        build(nc, S)
        _CACHE[S] = nc
    nc = _CACHE[S]
    in_maps = make_in_maps(inputs, S)
    res = run_bass_kernel_spmd(nc, in_maps, core_ids=list(range(8)))
    NSL = S // 2048
    outp = np.zeros((2, S, D), np.float32)
    for core in range(8):
        b, j = core // 4, core % 4
        o = res.results[core]["out"]
        for s in range(NSL):
            p = 4 * s + j
            outp[b, 512 * p:512 * (p + 1)] = o[512 * s:512 * (s + 1)]
    return outp
```
